# Optimizing a Trainium2 kernel written in Bass

```python
import math
import jax, jax.numpy as jnp
from jax import lax
import numpy as np

D_MODEL = 1024
BATCH = 1
SEQ = 16384
DEPTH = 1
DEC_BATCH = 16
DEC_SEQ = 16
PAST_LEN = 2048

CHUNK = 64
Q_BLOCK = 128
HEAD_DIM = 64
N_DIFF_HEADS = 4
N_FOX_HEADS = 8
A_WIDTH = N_DIFF_HEADS * 2 * HEAD_DIM
B_WIDTH = N_FOX_HEADS * HEAD_DIM
D_FF = ((8 * D_MODEL // 3 + 255) // 256) * 256
NUM_BUCKETS = 32
MAX_DISTANCE = 128
ALPHA = (2 * DEPTH) ** 0.25
BETA = (8 * DEPTH) ** -0.25
FORGET_BIAS = 3.0
LN_EPS = 1e-5
NEG = -1e30
IN_COLS = 3 * A_WIDTH + 3 * B_WIDTH + N_FOX_HEADS + 2 * D_MODEL
SPLITS = [int(v) for v in np.cumsum([A_WIDTH] * 3 + [B_WIDTH] * 3 + [N_FOX_HEADS])]

kernel_name = 'diff_fox_hybrid_stream_step'


def layer_norm(x, g, b):
    x32 = x.astype(jnp.float32)
    mu = jnp.mean(x32, axis=-1, keepdims=True)
    var = jnp.mean(jnp.square(x32 - mu), axis=-1, keepdims=True)
    return ((x32 - mu) * lax.rsqrt(var + LN_EPS) * g + b).astype(x.dtype)


def modulation(c, w, b):
    m = jax.nn.silu(c) @ w + b
    return [t[:, None, :] for t in jnp.split(m, 6, axis=-1)]


def t5_bucket(rel):
    nb = NUM_BUCKETS // 2
    ret = jnp.where(rel > 0, nb, 0)
    n = jnp.abs(rel)
    max_exact = nb // 2
    nf = jnp.maximum(n, 1).astype(jnp.float32)
    large = max_exact + (jnp.log(nf / max_exact) / math.log(MAX_DISTANCE / max_exact)
                         * (nb - max_exact)).astype(jnp.int32)
    large = jnp.minimum(large, nb - 1)
    return ret + jnp.where(n < max_exact, n, large)


def mixer_inputs(h, w_in_l, b_forget_l):
    B, T, _ = h.shape
    z = h @ w_in_l
    qa, ka, va, qb, kb, vb, f_logit, gates = jnp.split(z, SPLITS, axis=-1)
    heads_a = lambda a: a.reshape(B, T, N_DIFF_HEADS, 2 * HEAD_DIM)
    heads_b = lambda a: a.reshape(B, T, N_FOX_HEADS, HEAD_DIM)
    logf = jax.nn.log_sigmoid((f_logit + b_forget_l).astype(jnp.float32))
    ga, gb = jnp.split(gates, 2, axis=-1)
    return (heads_a(qa), heads_a(ka), heads_a(va), heads_b(qb), heads_b(kb), heads_b(vb), logf, ga, gb)


def diff_attention(qa, ka, va, q_pos, k_pos, rel_bias, lam):
    mask = (k_pos[None, :] // CHUNK) <= (q_pos[:, None] // CHUNK)
    bias = jnp.transpose(rel_bias[t5_bucket(k_pos[None, :] - q_pos[:, None])], (2, 0, 1)).astype(jnp.float32)
    scale = HEAD_DIM ** -0.5

    def probs(q, k):
        s = jnp.einsum('bqhd,bkhd->bhqk', q, k).astype(jnp.float32) * scale + bias
        return jax.nn.softmax(jnp.where(mask, s, NEG), axis=-1)

    p = probs(qa[..., :HEAD_DIM], ka[..., :HEAD_DIM]) - lam * probs(qa[..., HEAD_DIM:], ka[..., HEAD_DIM:])
    return jnp.einsum('bhqk,bkhe->bqhe', p.astype(va.dtype), va)


def fox_attention(q, k, v, fq, fk, q_pos, k_pos):
    mask = k_pos[None, :] <= q_pos[:, None]
    decay = jnp.transpose(fq, (0, 2, 1))[..., :, None] - jnp.transpose(fk, (0, 2, 1))[..., None, :]
    s = jnp.einsum('bqhd,bkhd->bhqk', q, k).astype(jnp.float32) * (HEAD_DIM ** -0.5) + decay
    p = jax.nn.softmax(jnp.where(mask, s, NEG), axis=-1)
    return jnp.einsum('bhqk,bkhd->bqhd', p.astype(v.dtype), v)


def prompt_attention(qa, ka, va, qb, kb, vb, F, rel_bias, lam):
    B, S = qa.shape[:2]
    k_pos = jnp.arange(S, dtype=jnp.int32)

    def block(i):
        start = i * Q_BLOCK
        sl = lambda a: lax.dynamic_slice_in_dim(a, start, Q_BLOCK, axis=1)
        q_pos = start + jnp.arange(Q_BLOCK, dtype=jnp.int32)
        oa = diff_attention(sl(qa), ka, va, q_pos, k_pos, rel_bias, lam)
        ob = fox_attention(sl(qb), kb, vb, sl(F), F, q_pos, k_pos)
        return oa, ob

    oa, ob = lax.map(block, jnp.arange(S // Q_BLOCK, dtype=jnp.int32))
    unblock = lambda o: jnp.moveaxis(o, 0, 1).reshape(B, S, o.shape[3], o.shape[4])
    return unblock(oa), unblock(ob)


def mixer_output(oa, ob, ga, gb, subln_g_l, w_a, w_b, w_o_l, lam_init):
    B, T = oa.shape[:2]
    oa32 = oa.astype(jnp.float32)
    oa = (oa32 * lax.rsqrt(jnp.mean(jnp.square(oa32), axis=-1, keepdims=True) + LN_EPS)
          * subln_g_l * (1.0 - lam_init)).astype(ob.dtype)
    pa = oa.reshape(B, T, A_WIDTH) @ w_a
    pb = ob.reshape(B, T, B_WIDTH) @ w_b
    return (jax.nn.sigmoid(ga) * pa + jax.nn.sigmoid(gb) * pb) @ w_o_l


def post_mixer(x, y_mix, mod, ln1_g_l, ln1_b_l, ln2_g_l, ln2_b_l, w_ffn_in_l, w_ffn_out_l):
    _, _, gate1, shift2, scale2, gate2 = mod
    x1 = layer_norm(ALPHA * x + gate1 * y_mix, ln1_g_l, ln1_b_l)
    h2 = x1 * (1 + scale2) + shift2
    g, u = jnp.split(h2 @ w_ffn_in_l, 2, axis=-1)
    f = (jax.nn.silu(g) * u) @ w_ffn_out_l
    return layer_norm(ALPHA * x1 + gate2 * f, ln2_g_l, ln2_b_l)


def setup_inputs(seed: int = 0) -> dict:
    key = jax.random.key(seed)
    ks = jax.random.split(key, 32)
    nrm = lambda k, shape, s=1.0: jax.random.normal(k, shape, jnp.float32) * s
    col_scale = jnp.concatenate([
        jnp.ones((2 * A_WIDTH,), jnp.float32), jnp.full((A_WIDTH,), BETA, jnp.float32),
        jnp.ones((2 * B_WIDTH,), jnp.float32), jnp.full((B_WIDTH,), BETA, jnp.float32),
        jnp.ones((N_FOX_HEADS + 2 * D_MODEL,), jnp.float32)])
    return {
        'x_prompt': nrm(ks[0], (BATCH, SEQ, D_MODEL)),
        'x_sample': nrm(ks[1], (DEC_BATCH, DEC_SEQ, D_MODEL)),
        'cache_diff_k': nrm(ks[2], (DEPTH, DEC_BATCH, PAST_LEN, N_DIFF_HEADS, 2 * HEAD_DIM)),
        'cache_diff_v': nrm(ks[3], (DEPTH, DEC_BATCH, PAST_LEN, N_DIFF_HEADS, 2 * HEAD_DIM), BETA),
        'cache_fox_k': nrm(ks[4], (DEPTH, DEC_BATCH, PAST_LEN, N_FOX_HEADS, HEAD_DIM)),
        'cache_fox_v': nrm(ks[5], (DEPTH, DEC_BATCH, PAST_LEN, N_FOX_HEADS, HEAD_DIM), BETA),
        'cache_fox_logf': jax.nn.log_sigmoid(FORGET_BIAS + nrm(ks[6], (DEPTH, DEC_BATCH, PAST_LEN, N_FOX_HEADS))),
        'c_prompt': nrm(ks[7], (BATCH, D_MODEL)),
        'c_sample': nrm(ks[8], (DEC_BATCH, D_MODEL)),
        'w_ada': nrm(ks[9], (DEPTH, D_MODEL, 6 * D_MODEL), D_MODEL ** -0.5),
        'b_ada': nrm(ks[10], (DEPTH, 6 * D_MODEL), 0.02),
        'w_in': nrm(ks[11], (DEPTH, D_MODEL, IN_COLS), D_MODEL ** -0.5) * col_scale,
        'b_forget': FORGET_BIAS + nrm(ks[12], (DEPTH, N_FOX_HEADS), 0.1),
        'lambda_q1': nrm(ks[13], (DEPTH, HEAD_DIM), 0.1),
        'lambda_k1': nrm(ks[14], (DEPTH, HEAD_DIM), 0.1),
        'lambda_q2': nrm(ks[15], (DEPTH, HEAD_DIM), 0.1),
        'lambda_k2': nrm(ks[16], (DEPTH, HEAD_DIM), 0.1),
        'subln_g': 1.0 + nrm(ks[17], (DEPTH, 2 * HEAD_DIM), 0.02),
        'rel_bias': nrm(ks[18], (NUM_BUCKETS, N_DIFF_HEADS), 0.5),
        'w_branch_a': nrm(ks[19], (DEPTH, A_WIDTH, D_MODEL), A_WIDTH ** -0.5 * BETA),
        'w_branch_b': nrm(ks[20], (DEPTH, B_WIDTH, D_MODEL), B_WIDTH ** -0.5 * BETA),
        'w_o': nrm(ks[21], (DEPTH, D_MODEL, D_MODEL), D_MODEL ** -0.5 * BETA),
        'ln1_g': 1.0 + nrm(ks[22], (DEPTH, D_MODEL), 0.02),
        'ln1_b': nrm(ks[23], (DEPTH, D_MODEL), 0.02),
        'ln2_g': 1.0 + nrm(ks[24], (DEPTH, D_MODEL), 0.02),
        'ln2_b': nrm(ks[25], (DEPTH, D_MODEL), 0.02),
        'w_ffn_in': nrm(ks[26], (DEPTH, D_MODEL, 2 * D_FF), D_MODEL ** -0.5 * BETA),
        'w_ffn_out': nrm(ks[27], (DEPTH, D_FF, D_MODEL), D_FF ** -0.5 * BETA),
    }


def reference(x_prompt, x_sample, cache_diff_k, cache_diff_v, cache_fox_k, cache_fox_v, cache_fox_logf,
              c_prompt, c_sample, w_ada, b_ada, w_in, b_forget, lambda_q1, lambda_k1, lambda_q2, lambda_k2,
              subln_g, rel_bias, w_branch_a, w_branch_b, w_o, ln1_g, ln1_b, ln2_g, ln2_b, w_ffn_in, w_ffn_out):
    f32 = jnp.float32
    xp, xs = x_prompt, x_sample
    past = cache_diff_k.shape[2]
    dk_p, dv_p, fk_p, fv_p, fl_p = [], [], [], [], []
    dk_s, dv_s, fk_s, fv_s, fl_s = [], [], [], [], []
    for l in range(DEPTH):
        lam_init = 0.8 - 0.6 * math.exp(-0.3 * l)
        lam = (jnp.exp(jnp.sum(lambda_q1[l].astype(f32) * lambda_k1[l].astype(f32)))
               - jnp.exp(jnp.sum(lambda_q2[l].astype(f32) * lambda_k2[l].astype(f32))) + lam_init)
        ffn_w = (ln1_g[l], ln1_b[l], ln2_g[l], ln2_b[l], w_ffn_in[l], w_ffn_out[l])

        mod_p = modulation(c_prompt, w_ada[l], b_ada[l])
        hp = xp * (1 + mod_p[1]) + mod_p[0]
        qa, ka, va, qb, kb, vb, logf, ga, gb = mixer_inputs(hp, w_in[l], b_forget[l])
        F = jnp.cumsum(logf, axis=1)
        oa, ob = prompt_attention(qa, ka, va, qb, kb, vb, F, rel_bias, lam)
        y_mix = mixer_output(oa, ob, ga, gb, subln_g[l], w_branch_a[l], w_branch_b[l], w_o[l], lam_init)
        xp_next = post_mixer(xp, y_mix, mod_p, *ffn_w)
        dk_p.append(ka); dv_p.append(va); fk_p.append(kb); fv_p.append(vb); fl_p.append(logf.astype(xp.dtype))

        mod_s = modulation(c_sample, w_ada[l], b_ada[l])
        hs = xs * (1 + mod_s[1]) + mod_s[0]
        qa_s, ka_s, va_s, qb_s, kb_s, vb_s, logf_s, ga_s, gb_s = mixer_inputs(hs, w_in[l], b_forget[l])
        T = xs.shape[1]
        k_pos = jnp.arange(past + T, dtype=jnp.int32)
        q_pos = past + jnp.arange(T, dtype=jnp.int32)
        ka_all = jnp.concatenate([cache_diff_k[l], ka_s], axis=1)
        va_all = jnp.concatenate([cache_diff_v[l], va_s], axis=1)
        kb_all = jnp.concatenate([cache_fox_k[l], kb_s], axis=1)
        vb_all = jnp.concatenate([cache_fox_v[l], vb_s], axis=1)
        F_all = jnp.cumsum(jnp.concatenate([cache_fox_logf[l].astype(f32), logf_s], axis=1), axis=1)
        oa_s = diff_attention(qa_s, ka_all, va_all, q_pos, k_pos, rel_bias, lam)
        ob_s = fox_attention(qb_s, kb_all, vb_all, F_all[:, past:], F_all, q_pos, k_pos)
        y_mix_s = mixer_output(oa_s, ob_s, ga_s, gb_s, subln_g[l], w_branch_a[l], w_branch_b[l], w_o[l], lam_init)
        xs_next = post_mixer(xs, y_mix_s, mod_s, *ffn_w)
        dk_s.append(ka_s); dv_s.append(va_s); fk_s.append(kb_s); fv_s.append(vb_s); fl_s.append(logf_s.astype(xs.dtype))

        xp, xs = xp_next, xs_next

    st = lambda lst: jnp.stack(lst, axis=0)
    return (xp, xs, st(dk_p), st(dv_p), st(fk_p), st(fv_p), st(fl_p),
            st(dk_s), st(dv_s), st(fk_s), st(fv_s), st(fl_s))
```

```python
from contextlib import ExitStack
import concourse.bass as bass
import concourse.mybir as mybir

F32 = mybir.dt.float32
BF16 = mybir.dt.bfloat16
AF = mybir.ActivationFunctionType
ALU = mybir.AluOpType
AX = mybir.AxisListType

ENGS = ("pe", "act", "dve", "pool", "sp")


class Grp:
    __slots__ = ("sem", "final")

    def __init__(self, sem):
        self.sem = sem
        self.final = 0


class Op:
    __slots__ = ("eng", "fn", "waits", "signal", "value", "grp", "dval")

    def __init__(self, eng, fn):
        self.eng = eng
        self.fn = fn
        self.waits = []
        self.signal = False
        self.value = None
        self.grp = None
        self.dval = None


class Buf:
    def __init__(self, P, name, t=None):
        self.P = P
        self.name = name
        self.t = t
        self.w = []
        self.r_eng = {}
        self.r_dma = []
        self.prev_r = []
        self.ld = None
        self.st = None
        self.ldp = None
        self.stp = None
        P.bufs.append(self)

    def __getitem__(self, k):
        return self.t[k]

    def _dsem(self, which):
        d = getattr(self, which)
        if d is None:
            if self.P.free_sems and not which.endswith("p"):
                sem, cnt = self.P.free_sems.pop()
                d = [sem, cnt, None]
            else:
                sem = self.P.new_sem(f"{which}_{self.name}")
                d = [sem, 0, None]
            setattr(self, which, d)
        return d


class Prog:
    def __init__(self, nc, es):
        self.nc = nc
        self.es = es
        self.ops = {e: [] for e in ENGS}
        self.nsem = 0
        self.esem = {e: self.new_sem("eng_" + e) for e in ENGS}
        self.nbuf = 0
        self.pending_dma = {}
        self.scopes = []
        self.bufs = []
        self.free_sems = []
        self.scope_bufs = []

    def new_sem(self, name):
        self.nsem += 1
        return self.es.enter_context(self.nc.semaphore(f"s{self.nsem}_{name}"))

    def sb(self, name, shape, dt):
        st = self.scopes[-1] if self.scopes else self.es
        t = st.enter_context(self.nc.sbuf_tensor(name, list(shape), dt))
        b = Buf(self, name, t)
        if self.scope_bufs:
            self.scope_bufs[-1].append(b)
        return b

    def push(self):
        self.scopes.append(ExitStack())
        self.scope_bufs.append([])

    def pop(self):
        self.fence()
        self.scopes.pop().close()
        for b in self.scope_bufs.pop():
            for d in (b.ld, b.st):
                if d is not None:
                    self.free_sems.append((d[0], d[1]))
            b.ld = b.st = None

    def ps(self, name, shape, dt):
        t = self.es.enter_context(self.nc.psum_tensor(name, list(shape), dt))
        return Buf(self, name, t)

    def dram(self, name, shape, dt):
        t = self.nc.dram_tensor(name, list(shape), dt).ap()
        return Buf(self, name, t)

    def _deps(self, op, reads, writes, nowaw):
        waits = op.waits
        for b in reads:
            for w in b.w:
                waits.append(w)
        for b in writes:
            if not nowaw:
                for w in b.w:
                    if not (w.eng == "pe" and op.eng == "pe"):
                        waits.append(w)
            for r in b.r_eng.values():
                if not (r.eng == op.eng == "pe"):
                    waits.append(r)
            waits.extend(b.r_dma)
            for r in b.prev_r:
                if not (r.eng == op.eng == "pe"):
                    waits.append(r)
        for w in waits:
            if w.grp is None:
                w.signal = True
        for b in writes:
            if b.r_eng or b.r_dma:
                b.prev_r = list(b.r_eng.values()) + list(b.r_dma)
                b.w = [op]
                b.r_eng = {}
                b.r_dma = []
                if b.st is not None:
                    b.st[2] = None
                if b.stp is not None:
                    b.stp[2] = None
            else:
                if op.grp is None:
                    b.w = [x for x in b.w if x.grp is not None or x.eng != op.eng]
                b.w.append(op)
        for b in reads:
            if op.grp is not None:
                b.r_dma.append(op)
            else:
                b.r_eng[op.eng] = op
            if b.ld is not None:
                b.ld[2] = None
            if b.ldp is not None:
                b.ldp[2] = None

    def op(self, eng, fn, reads=(), writes=(), waits=(), nowaw=False):
        o = Op(eng, fn)
        o.waits.extend(waits)
        self._deps(o, reads, writes, nowaw)
        self.ops[eng].append(o)
        return o

    def dma(self, eng, out, in_, src=None, dst=None, waits=(), nowaw=True, sembuf=None, **kw):
        o = Op(eng, lambda e: e.dma_start(out=out, in_=in_, **kw))
        o.waits.extend(waits)
        sfx = "p" if eng == "pool" else ""
        if sembuf is not None:
            d = sembuf[0]._dsem(sembuf[1])
        elif dst is not None and not dst.name.startswith("D_"):
            d = dst._dsem("ld" + sfx)
        elif src is not None:
            d = src._dsem("st" + sfx)
        else:
            d = dst._dsem("ld" + sfx)
        reads = [src] if src is not None else []
        writes = [dst] if dst is not None else []
        if d[2] is None:
            d[2] = Grp(d[0])
        o.grp = d[2]
        d[1] += 16
        o.dval = d[1]
        o.grp.final = d[1]
        self._deps(o, reads, writes, nowaw)
        if dst is not None and (dst.ld is d or dst.ldp is d):
            d[2] = o.grp
        if src is not None and (src.st is d or src.stp is d):
            d[2] = o.grp
        self.ops[eng].append(o)
        self.pending_dma[id(o.grp)] = o
        return o

    def fence(self):
        lasts = []
        for e in ENGS:
            for o in reversed(self.ops[e]):
                if o.grp is None:
                    lasts.append(o)
                    break
        dmas = list(self.pending_dma.values())
        for o in lasts:
            o.signal = True
        for e in ENGS:
            f = Op(e, lambda en: en.nop())
            f.waits = lasts + dmas
            self.ops[e].append(f)
        self.pending_dma = {}
        for b in self.bufs:
            for d in (b.ld, b.st, b.ldp, b.stp):
                if d is not None:
                    d[2] = None

    def emit(self, final_waits):
        nc = self.nc
        for e in ENGS:
            n = 0
            for o in self.ops[e]:
                if o.signal:
                    n += 1
                    o.value = n
        fin = Op("sp", lambda e: e.nop())
        fin.waits.extend(final_waits)
        for w in final_waits:
            if w.grp is None:
                w.signal = True
        for e in ENGS:
            n = 0
            for o in self.ops[e]:
                if o.signal:
                    n += 1
                    o.value = n
        self.ops["sp"].append(fin)
        esem = self.esem

        def run(engname, eng):
            seen = {}
            for o in self.ops[engname]:
                for w in o.waits:
                    if w.grp is not None:
                        sem, val = w.grp.sem, w.grp.final
                    else:
                        sem, val = esem[w.eng], w.value
                    if seen.get(sem, 0) < val:
                        eng.wait_ge(sem, val)
                        seen[sem] = val
                ins = o.fn(eng)
                if o.grp is not None:
                    ins.then_inc(o.grp.sem, 16)
                elif o.signal:
                    ins.then_inc(esem[engname], 1)

        with nc.Block() as block:
            @block.tensor
            def _(e):
                run("pe", e)

            @block.scalar
            def _(e):
                run("act", e)

            @block.vector
            def _(e):
                run("dve", e)

            @block.gpsimd
            def _(e):
                run("pool", e)

            @block.sync
            def _(e):
                run("sp", e)
import math
import numpy as np

D = 1024
NCOL = 5128
C_QA, C_KA, C_VA, C_QB, C_KB, C_VB, C_F, C_GA, C_GB = 0, 512, 1024, 1536, 2048, 2560, 3072, 3080, 4104
ALPHA = 2.0 ** 0.25
LN_EPS = 1e-5
LAM_INIT = 0.8 - 0.6 * math.exp(0.0)
NEGM = -30000.0


def t5_thresholds():
    import jax, jax.numpy as jnp
    with jax.default_device(jax.devices("cpu")[0]):
        rel = jnp.arange(-255, 128, dtype=jnp.int32)
        nb = 16
        ret = jnp.where(rel > 0, nb, 0)
        n = jnp.abs(rel)
        max_exact = nb // 2
        nf = jnp.maximum(n, 1).astype(jnp.float32)
        large = max_exact + (jnp.log(nf / max_exact) / math.log(128 / max_exact) * (nb - max_exact)).astype(jnp.int32)
        large = jnp.minimum(large, nb - 1)
        bk = np.asarray(ret + jnp.where(n < max_exact, n, large))
    rels = np.arange(-255, 128)
    th = []
    for i in range(1, len(rels)):
        if bk[i] != bk[i - 1]:
            th.append((int(rels[i]), int(bk[i - 1]), int(bk[i])))
    return int(bk[0]), th


def build(S=16384, PAST=2048, phases=(0, 1, 2, 3), dbg=False):
    from contextlib import ExitStack
    import os
    SKIP = os.environ.get('SKIP', '').split(',')
    NB = S // 128
    NR = NB // 8
    NOWN = NR + 1
    TOWN = NOWN * 128
    NG = NB // 4
    NKB_S = PAST // 128

    nc = bass.Bass("TRN2", target_bir_lowering=False)

    def ein(name, shape, dt=F32):
        return nc.dram_tensor(name, list(shape), dt, kind="ExternalInput").ap()

    def eout(name, shape, dt=F32):
        return nc.dram_tensor(name, list(shape), dt, kind="ExternalOutput").ap()

    xT_all = ein("xT_all", [D, S])
    xT_own = ein("xT_own", [D, TOWN])
    x_own = ein("x_own", [TOWN, D])
    cT = ein("cT", [128, 24])
    cT_rep = ein("cT_rep", [128, 8 * 256])
    b_ada_fm = ein("b_ada_fm", [128, 48])
    b_ada = ein("b_ada", [1, 6144])
    w_ada = ein("w_ada", [D, 6144])
    w_in = ein("w_in", [D, NCOL])
    b_forget = ein("b_forget", [1, 8])
    lam_in = ein("lam", [1, 256])
    subln_g = ein("subln_g", [1, 128])
    rel_bias = ein("rel_bias", [1, 128])
    w_a = ein("w_a", [512, D])
    w_b = ein("w_b", [512, D])
    w_o = ein("w_o", [D, D])
    ln_in = ein("ln", [1, 4096])
    w_f1 = ein("w_f1", [D, 5632])
    w_f2 = ein("w_f2", [2816, D])
    ckdT = ein("ckdT", [2, 512, PAST])
    ckfT = ein("ckfT", [2, 512, PAST])
    cvd = ein("cvd", [2, PAST, 512])
    cvf = ein("cvf", [2, PAST, 512])
    clfT = ein("clfT", [2, 8, PAST])
    sel_in = ein("sel", [1, 27])
    selB = ein("selB", [1, NR * NB])

    y_own = eout("y_own", [TOWN, D])
    kd_own = eout("kd_own", [TOWN, 512])
    vd_own = eout("vd_own", [TOWN, 512])
    kf_own = eout("kf_own", [TOWN, 512])
    vf_own = eout("vf_own", [TOWN, 512])
    lf_own = eout("lf_own", [TOWN, 8])
    if dbg:
        dbg_oa = eout("dbg_oa", [TOWN, 512], BF16)
        dbg_ob = eout("dbg_ob", [TOWN, 512], BF16)

    es = ExitStack()
    P = Prog(nc, es)
    out_ops = []

    D_KT = P.dram("D_KT", [16, 72, S], BF16)
    D_QT = P.dram("D_QT", [16, 72, TOWN], BF16)
    D_VD = P.dram("D_VD", [4, NG, 128, 4 * 130], BF16)
    D_VF = P.dram("D_VF", [8, NG, 128, 4 * 66], BF16)
    D_GATES = P.dram("D_GATES", [TOWN, 2048], F32)
    D_X1 = P.dram("D_X1", [TOWN, D], F32)
    D_G2 = P.dram("D_G2", [2, 128, 1024], F32)
    D_OA = P.dram("D_OA", [TOWN, 512], BF16)
    D_OB = P.dram("D_OB", [TOWN, 512], BF16)
    D_WA = P.dram("D_WA", [512, D], BF16)
    D_WB = P.dram("D_WB", [512, D], BF16)
    D_WO = P.dram("D_WO", [D, D], BF16)
    D_WF1 = P.dram("D_WF1", [D, 5632], BF16)
    D_WF2 = P.dram("D_WF2", [2816, D], BF16)

    psall = es.enter_context(nc.psum_tensor("psall", [128, 4096], F32))
    banks = [Buf(P, f"bank{i}", psall[:, i * 512:(i + 1) * 512]) for i in range(8)]

    ident = P.sb("ident", [128, 128], BF16)
    ones_f = P.sb("ones_f", [128, 512], F32)
    ones_b = P.sb("ones_b", [128, 512], BF16)
    P.op("pool", lambda e: e.memset(ident[:], 0.0), writes=[ident])
    P.op("pool", lambda e: e.affine_select(out=ident[:], in_=ident[:], pattern=[[-1, 128]],
                                            compare_op=ALU.not_equal, fill=1.0, base=0, channel_multiplier=1),
         reads=[ident], writes=[ident])
    P.op("pool", lambda e: e.memset(ones_f[:], 1.0), writes=[ones_f])
    P.op("pool", lambda e: e.memset(ones_b[:], 1.0), writes=[ones_b])

    bfm = P.sb("bfm", [128, 48], F32)
    P.dma("sp", bfm[:], b_ada_fm[:, :], dst=bfm)
    nbfo = P.sb("nbfo", [8, 1], F32)
    P.dma("sp", nbfo[:], b_forget.rearrange("o h -> h o"), dst=nbfo)
    P.op("dve", lambda e: e.tensor_scalar(out=nbfo[:], in0=nbfo[:], scalar1=-1.0, scalar2=None, op0=ALU.mult),
         reads=[nbfo], writes=[nbfo])
    bfo_bc = P.sb("bfo_bc", [128, 8], F32)
    P.dma("sp", bfo_bc[:], b_forget[0:1, :].partition_broadcast(128), dst=bfo_bc)

    cT_sb = P.sb("cT_sb", [128, 8, 3], F32)
    P.dma("sp", cT_sb[:].rearrange("p k j -> p (k j)"), cT[:, :], dst=cT_sb)
    sT = P.sb("sT", [128, 8, 3], F32)
    P.op("act", lambda e: e.activation(out=sT[:], in_=cT_sb[:], func=AF.Silu), reads=[cT_sb], writes=[sT])

    SSQ = P.sb("SSQ", [128, NOWN, 4], F32)
    P.op("pool", lambda e: e.memset(SSQ[:], 1.0), writes=[SSQ])
    P.push()
    fTo = P.sb("fTo", [8, NOWN, 128], F32)
    vsn_d = P.sb("vsn_d", [128, 4, 130], BF16)
    vsn_f = P.sb("vsn_f", [128, 8, 66], BF16)
    KTn = P.sb("KTn", [128, 8, 128], BF16)
    Gend = P.sb("Gend", [8, NB + 1], F32)
    sh1 = P.sb("sh1", [128, 8, 3], F32)
    sc1 = P.sb("sc1", [128, 8, 3], F32)
    rb_bc = P.sb("rb_bc", [128, 128], F32)
    zero_f = P.sb("zero_f", [128, 128], F32)
    TP = P.sb("TP", [128, 4, 128], F32)
    TDg = P.sb("TDg", [128, 4, 128], F32)
    Gt = [P.sb(f"Gt{i}", [128, 128], F32) for i in range(2)]
    dl = P.sb("dl", [128, 4], F32)
    P.push()
    win = P.sb("win", [128, 8, NCOL], BF16)
    P.push()
    wada = [P.sb(f"wada{i}", [128, 8, 512], F32) for i in range(2)]
    w_ada_v = w_ada.rearrange("(k p) n -> p k n", p=128)
    mps = banks[7]
    for g in range(4):
        wt = wada[g % 2]
        P.dma("sp", wt[:], w_ada_v[:, :, g * 512:(g + 1) * 512], dst=wt)
        for j in range(4):
            c = (g * 4 + j) * 3
            for k in range(8):
                if 'mod' in SKIP:
                    continue
                P.op("pe", lambda e, wt=wt, j=j, k=k, c=c: e.matmul(
                    mps[:, c:c + 3], lhsT=wt[:, k, j * 128:(j + 1) * 128], rhs=sT[:, k, :],
                    start=(k == 0), stop=(k == 7)), reads=[wt, sT], writes=[mps])
    P.op("dve", lambda e: e.tensor_tensor(out=sh1[:], in0=mps[:, 0:24].rearrange("p (k j) -> p k j", j=3),
                                          in1=bfm[:, 0:8].unsqueeze(2).to_broadcast([128, 8, 3]), op=ALU.add),
         reads=[mps, bfm], writes=[sh1])
    P.op("dve", lambda e: e.scalar_tensor_tensor(out=sc1[:], in0=mps[:, 24:48].rearrange("p (k j) -> p k j", j=3),
                                                 scalar=1.0, in1=bfm[:, 8:16].unsqueeze(2).to_broadcast([128, 8, 3]),
                                                 op0=ALU.add, op1=ALU.add),
         reads=[mps, bfm], writes=[sc1])

    stg32 = [P.sb(f"stg32_{i}", [128, 2048], F32) for i in range(2)]
    w_in_v = w_in.rearrange("(k p) n -> p k n", p=128)
    pieces = [(0, 2048), (2048, 4096), (4096, NCOL)]
    i = 0
    for k in range(8):
        for (c0, c1) in pieces:
            st = stg32[i % 2]
            P.dma("sp", st[:, 0:c1 - c0], w_in_v[:, k, c0:c1], dst=st)
            if i % 2 == 0:
                P.op("pool", lambda e, st=st, k=k, c0=c0, c1=c1: e.tensor_copy(out=win[:, k, c0:c1], in_=st[:, 0:c1 - c0]),
                     reads=[st], writes=[win], nowaw=True)
            else:
                P.op("act", lambda e, st=st, k=k, c0=c0, c1=c1: e.copy(out=win[:, k, c0:c1], in_=st[:, 0:c1 - c0]),
                     reads=[st], writes=[win], nowaw=True)
            i += 1

    def split3(src, hi, mid, lo, r_, npart, n, neg=False):
        if neg:
            P.op("dve", lambda e: e.tensor_scalar(out=r_[0:npart, 0:n], in0=src[0:npart, 0:n], scalar1=-1.0, scalar2=None, op0=ALU.mult),
                 reads=[src], writes=[r_])
            base = r_
        else:
            base = src
        P.op("dve", lambda e: e.tensor_copy(out=hi[0:npart, 0:n], in_=base[0:npart, 0:n]), reads=[base], writes=[hi])
        P.op("dve", lambda e: e.tensor_tensor(out=r_[0:npart, 0:n], in0=base[0:npart, 0:n], in1=hi[0:npart, 0:n], op=ALU.subtract),
             reads=[base, hi], writes=[r_])
        P.op("dve", lambda e: e.tensor_copy(out=mid[0:npart, 0:n], in_=r_[0:npart, 0:n]), reads=[r_], writes=[mid])
        P.op("dve", lambda e: e.tensor_tensor(out=r_[0:npart, 0:n], in0=r_[0:npart, 0:n], in1=mid[0:npart, 0:n], op=ALU.subtract),
             reads=[r_, mid], writes=[r_])
        P.op("dve", lambda e: e.tensor_copy(out=lo[0:npart, 0:n], in_=r_[0:npart, 0:n]), reads=[r_], writes=[lo])

    P.pop()
    P.dma("sp", rb_bc[:], rel_bias[0:1, :].partition_broadcast(128), dst=rb_bc)
    bk0, ths = t5_thresholds()
    P.op("pool", lambda e: e.memset(zero_f[:], 0.0), writes=[zero_f])
    P.op("dve", lambda e: e.memset(TP[:], 0.0), writes=[TP])
    P.op("dve", lambda e: e.memset(TDg[:], 0.0), writes=[TDg])
    gi_ = 0
    for (t, bb, ba) in ths:
        for which, T_, off, lo_, hi_ in (("p", TP, -128, -255, -1), ("d", TDg, 0, -127, 127)):
            if not (lo_ < t <= hi_):
                continue
            G_ = Gt[gi_ % 2]
            gi_ += 1
            P.op("pool", lambda e, G_=G_, off=off, t=t: e.affine_select(
                out=G_[:], in_=ones_f[:, 0:128], pattern=[[-1, 128]], compare_op=ALU.is_ge, fill=0.0,
                base=off - t, channel_multiplier=1), reads=[ones_f], writes=[G_])
            P.op("dve", lambda e, bb=bb, ba=ba: e.tensor_tensor(out=dl[:], in0=rb_bc[:, ba * 4:ba * 4 + 4], in1=rb_bc[:, bb * 4:bb * 4 + 4],
                                                              op=ALU.subtract), reads=[rb_bc], writes=[dl])
            for h in range(4):
                P.op("dve", lambda e, G_=G_, T_=T_, h=h: e.scalar_tensor_tensor(
                    out=T_[:, h, :], in0=G_[:], scalar=dl[:, h:h + 1], in1=T_[:, h, :], op0=ALU.mult, op1=ALU.add),
                    reads=[G_, dl, T_], writes=[T_])
    P.push()
    xto = [P.sb(f"xto{i}", [128, 8, 128], F32) for i in range(2)]
    hTo = [P.sb(f"hTo{i}", [128, 8, 128], BF16) for i in range(2)]
    stgF = [P.sb(f"stgF{i}", [128, 512], F32) for i in range(4)]
    stgL = [P.sb(f"stgL{i}", [128, 8], F32) for i in range(2)]
    stgE = [P.sb(f"stgE{i}", [128, 8], F32) for i in range(2)]
    QTs = [P.sb(f"QTs{i}", [128, 4, 128], BF16) for i in range(2)]
    xT_own_v = xT_own.rearrange("(k p) t -> p k t", p=128)
    P.op("pool", lambda e: e.memset(vsn_d[:], 1.0), writes=[vsn_d])
    P.op("pool", lambda e: e.memset(vsn_f[:], 1.0), writes=[vsn_f])
    bk = [0]

    def nbank(lo=0, hi=8):
        b = banks[lo + bk[0] % (hi - lo)]
        bk[0] += 1
        return b

    evi = [0]

    def evac(out_ap, in_ap, reads, writes, scale=None, eng=None):
        en = eng or ("act" if evi[0] % 2 == 0 else "dve")
        evi[0] += 1
        if en == "act":
            if scale is None:
                return P.op("act", lambda e: e.copy(out=out_ap, in_=in_ap), reads=reads, writes=writes, nowaw=True)
            return P.op("act", lambda e: e.mul(out=out_ap, in_=in_ap, mul=scale), reads=reads, writes=writes, nowaw=True)
        if scale is None:
            return P.op("dve", lambda e: e.tensor_copy(out=out_ap, in_=in_ap), reads=reads, writes=writes, nowaw=True)
        return P.op("dve", lambda e: e.tensor_scalar(out=out_ap, in0=in_ap, scalar1=scale, scalar2=None, op0=ALU.mult),
                    reads=reads, writes=writes, nowaw=True)

    sF = [0]
    if 1 in phases:
        for b in range(0 if 'tm' in SKIP else NOWN):
            xt = xto[b % 2]
            hT = hTo[b % 2]
            if b == 0:
                P.dma("sp", xt[:], xT_own_v[:, :, 0:128], dst=xt)
            if b + 1 < NOWN:
                P.dma("sp", xto[(b + 1) % 2][:], xT_own_v[:, :, (b + 1) * 128:(b + 2) * 128], dst=xto[(b + 1) % 2])
            def modulate_blk(bb):
                xt_, hT_ = xto[bb % 2], hTo[bb % 2]
                for k in range(8):
                    segs = [(0, 128, 0)] if bb < NR else [(0, 16, 1), (16, 128, 2)]
                    for (a0, a1, j) in segs:
                        P.op("dve", lambda e, xt_=xt_, hT_=hT_, k=k, a0=a0, a1=a1, j=j: e.tensor_scalar(
                            out=hT_[:, k, a0:a1], in0=xt_[:, k, a0:a1], scalar1=sc1[:, k, j:j + 1], scalar2=sh1[:, k, j:j + 1],
                            op0=ALU.mult, op1=ALU.add), reads=[xt_, sc1, sh1], writes=[hT_], nowaw=True)

            if b == 0:
                modulate_blk(0)
            tm = [("ka", C_KA, kd_own), ("va", C_VA, vd_own), ("kb", C_KB, kf_own), ("vb", C_VB, vf_own),
                  ("ga0", C_GA, None), ("ga1", C_GA + 512, None), ("gb0", C_GB, None), ("gb1", C_GB + 512, None)]
            for gi, (nm, c0, dest) in enumerate(tm):
                if 'gates' in SKIP and dest is None:
                    continue
                bank = nbank()
                for k in range(8):
                    P.op("pe", lambda e, bank=bank, hT=hT, k=k, c0=c0: e.matmul(
                        bank[:, :], lhsT=hT[:, k, :], rhs=win[:, k, c0:c0 + 512], start=(k == 0), stop=(k == 7)),
                        reads=[hT, win], writes=[bank])
                st = stgF[sF[0] % 4]
                sF[0] += 1
                evac(st[:], bank[:, :], [bank], [st])
                if dest is not None:
                    out_ops.append(P.dma("pool", dest[b * 128:(b + 1) * 128, :], st[:], src=st))
                else:
                    P.dma("pool", D_GATES[b * 128:(b + 1) * 128, (gi - 4) * 512:(gi - 3) * 512], st[:], src=st, dst=D_GATES)
                if b == NR and nm == "va" and 'vsn' not in SKIP:
                    P.op("pool", lambda e, st=st: e.tensor_copy(out=vsn_d[:, :, 0:128], in_=st[:].rearrange("p (h d) -> p h d", d=128)),
                         reads=[st], writes=[vsn_d])
                if b == NR and nm == "vb" and 'vsn' not in SKIP:
                    P.op("pool", lambda e, st=st: e.tensor_copy(out=vsn_f[:, :, 0:64], in_=st[:].rearrange("p (h d) -> p h d", d=64)),
                         reads=[st], writes=[vsn_f])
            if b + 1 < NOWN:
                modulate_blk(b + 1)
            if 'f' in SKIP:
                continue
            bank = nbank()
            for k in range(8):
                P.op("pe", lambda e, bank=bank, hT=hT, k=k: e.matmul(
                    bank[:, 0:8], lhsT=hT[:, k, :], rhs=win[:, k, C_F:C_F + 8], start=(k == 0), stop=(k == 7)),
                    reads=[hT, win], writes=[bank])
            sl = stgL[b % 2]
            se = stgE[b % 2]
            P.op("dve", lambda e, bank=bank, se=se: e.tensor_tensor(out=se[:], in0=bank[:, 0:8], in1=bfo_bc[:], op=ALU.add),
                 reads=[bank, bfo_bc], writes=[se])
            P.op("act", lambda e, se=se: e.activation(out=se[:], in_=se[:], func=AF.Exp, scale=-1.0), reads=[se], writes=[se])
            P.op("act", lambda e, se=se: e.activation(out=se[:], in_=se[:], func=AF.Ln, bias=1.0), reads=[se], writes=[se])
            P.op("dve", lambda e, se=se, sl=sl: e.tensor_scalar(out=sl[:], in0=se[:], scalar1=-1.0, scalar2=None, op0=ALU.mult),
                 reads=[se], writes=[sl])
            out_ops.append(P.dma("sp", lf_own[b * 128:(b + 1) * 128, :], sl[:], src=sl))
            if 'ffm' in SKIP:
                continue
            bank = nbank()
            for k in range(8):
                P.op("pe", lambda e, bank=bank, hT=hT, k=k: e.matmul(
                    bank[0:8, 0:128], lhsT=win[:, k, C_F:C_F + 8], rhs=hT[:, k, :], start=(k == 0), stop=(k == 7)),
                    reads=[hT, win], writes=[bank])
            P.op("act", lambda e, bank=bank, b=b: e.activation(out=fTo[:, b, :], in_=bank[0:8, 0:128], func=AF.Exp, scale=-1.0, bias=nbfo[:]),
                 reads=[bank, nbfo], writes=[fTo], nowaw=True)
            P.op("act", lambda e, b=b: e.activation(out=fTo[:, b, :], in_=fTo[:, b, :], func=AF.Ln, bias=1.0),
                 reads=[fTo], writes=[fTo])
            if 'q' in SKIP:
                continue
            for half, cbase in ((0, C_QA), (1, C_QB)):
                bank = nbank()
                for cc in range(4):
                    for k in range(8):
                        P.op("pe", lambda e, bank=bank, hT=hT, k=k, cc=cc, cbase=cbase: e.matmul(
                            bank[:, cc * 128:(cc + 1) * 128], lhsT=win[:, k, cbase + cc * 128:cbase + (cc + 1) * 128],
                            rhs=hT[:, k, :], start=(k == 0 and cc == 0), stop=(k == 7), skip_group_check=True),
                            reads=[hT, win], writes=[bank])
                qs = QTs[half]
                evac(qs[:].rearrange("p c t -> p (c t)"), bank[:, :], [bank], [qs], scale=0.125, eng="dve")
                for cc in range(4):
                    for s in range(2):
                        u = half * 8 + cc * 2 + s
                        P.dma("pool", D_QT[u, 0:64, b * 128:(b + 1) * 128], qs[64 * s:64 * s + 64, cc, :], src=qs, dst=D_QT)

            if b == NR:
                for half, cbase in ((0, C_KA), (1, C_KB)):
                    bank = nbank()
                    for cc in range(4):
                        for k in range(8):
                            P.op("pe", lambda e, bank=bank, hT=hT, k=k, cc=cc, cbase=cbase: e.matmul(
                                bank[:, cc * 128:(cc + 1) * 128], lhsT=win[:, k, cbase + cc * 128:cbase + (cc + 1) * 128],
                                rhs=hT[:, k, :], start=(k == 0 and cc == 0), stop=(k == 7), skip_group_check=True),
                                reads=[hT, win], writes=[bank])
                    evac(KTn[:, half * 4:half * 4 + 4, :].rearrange("p c t -> p (c t)"), bank[:, :], [bank], [KTn])

    P.pop()
    P.push()
    if 1 in phases:
        xt4s = [P.sb(f"xt4_{i}", [128, 8, 512], F32) for i in range(2)]
        hT4s = [P.sb(f"hT4_{i}", [128, 8, 512], BF16) for i in range(2)]
        ktss = [P.sb(f"kts{i}", [128, 512], BF16) for i in range(4)]
        VDs = [P.sb(f"VDs{i}", [128, 4, 4, 130], BF16) for i in range(2)]
        VFs = [P.sb(f"VFs{i}", [128, 8, 4, 66], BF16) for i in range(2)]
        for t in VDs + VFs:
            P.op("pool", lambda e, t=t: e.memset(t[:], 1.0), writes=[t])
        P.fence()
        spT = [P.sb(f"spT{i}", [8, 512], F32) for i in range(2)]
        Gc = [P.sb(f"Gc{i}", [8, 512], F32) for i in range(2)]
        P.op("dve", lambda e: e.memset(Gend[:], 0.0), writes=[Gend])
        gsp = [[P.sb(f"gsp{i}_{j}", [8, 512], BF16) for j in range(3)] for i in range(2)]
        gr = [P.sb(f"gr{i}", [8, 512], F32) for i in range(2)]
        xT_all_v = xT_all.rearrange("(k p) t -> p k t", p=128)
        kti = [0]
        for g in range(0 if 'b1' in SKIP else NG):
            xt = xt4s[g % 2]
            hT = hT4s[g % 2]

            def modulate_grp(gg):
                xt_, hT_ = xt4s[gg % 2], hT4s[gg % 2]
                for k in range(8):
                    if k % 2 == 0:
                        P.op("dve", lambda e, xt_=xt_, hT_=hT_, k=k: e.tensor_scalar(
                            out=hT_[:, k, :], in0=xt_[:, k, :], scalar1=sc1[:, k, 0:1], scalar2=sh1[:, k, 0:1],
                            op0=ALU.mult, op1=ALU.add), reads=[xt_, sc1, sh1], writes=[hT_], nowaw=True)
                    else:
                        P.op("act", lambda e, xt_=xt_, hT_=hT_, k=k: e.activation(
                            out=hT_[:, k, :], in_=xt_[:, k, :], func=AF.Identity, scale=sc1[:, k, 0:1], bias=sh1[:, k, 0:1]),
                            reads=[xt_, sc1, sh1], writes=[hT_], nowaw=True)

            def load_grp(gg):
                if gg < NG:
                    P.dma("sp", xt4s[gg % 2][:], xT_all_v[:, :, gg * 512:(gg + 1) * 512], dst=xt4s[gg % 2])

            if g == 0:
                load_grp(0)
                load_grp(1)
                modulate_grp(0)
            for cc in range(8):
                cbase = (C_KA + cc * 128) if cc < 4 else (C_KB + (cc - 4) * 128)
                bank = nbank()
                for k in range(8):
                    P.op("pe", lambda e, bank=bank, hT=hT, k=k, cbase=cbase: e.matmul(
                        bank[:, :], lhsT=win[:, k, cbase:cbase + 128], rhs=hT[:, k, :], start=(k == 0), stop=(k == 7)),
                        reads=[hT, win], writes=[bank])
                kt = ktss[kti[0] % 4]
                kti[0] += 1
                evac(kt[:], bank[:, :], [bank], [kt])
                for s_ in range(2):
                    u = cc * 2 + s_
                    P.dma("pool", D_KT[u, 0:64, g * 512:(g + 1) * 512], kt[64 * s_:64 * s_ + 64, :], src=kt, dst=D_KT)
            if g + 1 < NG:
                modulate_grp(g + 1)
            load_grp(g + 2)
            vd = VDs[g % 2]
            vf = VFs[g % 2]
            for blk in range(4):
                for (cbase, vt, nh, dh) in ((C_VA, vd, 4, 128), (C_VB, vf, 8, 64)):
                    bank = nbank()
                    for k in range(8):
                        P.op("pe", lambda e, bank=bank, hT=hT, k=k, cbase=cbase, blk=blk: e.matmul(
                            bank[:, :], lhsT=hT[:, k, blk * 128:(blk + 1) * 128], rhs=win[:, k, cbase:cbase + 512],
                            start=(k == 0), stop=(k == 7)), reads=[hT, win], writes=[bank])
                    evac(vt[:, :, blk, 0:dh], bank[:, :].rearrange("p (h d) -> p h d", d=dh), [bank], [vt])
            for h in range(4):
                P.dma("act", D_VD[h, g], vd[:, h, :, :].rearrange("p b d -> p (b d)"), src=vd, dst=D_VD)
            for h in range(8):
                P.dma("act", D_VF[h, g], vf[:, h, :, :].rearrange("p b d -> p (b d)"), src=vf, dst=D_VF)
            bank = nbank()
            for k in range(8):
                P.op("pe", lambda e, bank=bank, hT=hT, k=k: e.matmul(
                    bank[0:8, :], lhsT=win[:, k, C_F:C_F + 8], rhs=hT[:, k, :], start=(k == 0), stop=(k == 7)),
                    reads=[hT, win], writes=[bank])
            sp_ = spT[g % 2]
            P.op("act", lambda e, bank=bank, sp_=sp_: e.activation(out=sp_[:], in_=bank[0:8, :], func=AF.Exp, scale=-1.0, bias=nbfo[:]),
                 reads=[bank, nbfo], writes=[sp_])
            P.op("act", lambda e, sp_=sp_: e.activation(out=sp_[:], in_=sp_[:], func=AF.Ln, bias=1.0), reads=[sp_], writes=[sp_])
            gc = Gc[g % 2]
            gprev = Gc[(g - 1) % 2]
            if g == 0:
                P.op("dve", lambda e, gc=gc, sp_=sp_: e.tensor_tensor_scan(out=gc[:], data0=ones_f[0:8, :], data1=sp_[:], initial=0.0,
                                                                             op0=ALU.mult, op1=ALU.add), reads=[sp_, ones_f], writes=[gc])
            else:
                P.op("dve", lambda e, gc=gc, sp_=sp_, gprev=gprev: e.tensor_tensor_scan(
                    out=gc[:], data0=ones_f[0:8, :], data1=sp_[:], initial=gprev[:, 511:512], op0=ALU.mult, op1=ALU.add),
                    reads=[sp_, ones_f, gprev], writes=[gc])
            P.op("dve", lambda e, gc=gc, g=g: e.tensor_copy(out=Gend[:, 4 * g + 1:4 * g + 5],
                                                           in_=gc[:].rearrange("p (b t) -> p b t", t=128)[:, :, 127]),
                 reads=[gc], writes=[Gend], nowaw=True)
            hi, mid, lo = gsp[g % 2]
            r_ = gr[g % 2]
            split3(gc, hi, mid, lo, r_, 8, 512)
            for i_, tl in enumerate((hi, mid, lo)):
                P.dma("sp", D_KT[8:16, 67 + i_, g * 512:(g + 1) * 512], tl[:], src=tl, dst=D_KT)
    P.pop()
    P.pop()

    P.push()
    oa_bf = P.sb("oa_bf", [128, NOWN, 512], BF16)
    ob_bf = P.sb("ob_bf", [128, NOWN, 512], BF16)
    P.op("pool", lambda e: e.memset(oa_bf[:], 0.0), writes=[oa_bf])
    P.op("pool", lambda e: e.memset(ob_bf[:], 0.0), writes=[ob_bf])
    P.fence()
    if 2 in phases:
        P.push()
        TP_hi = P.sb("TP_hi", [128, 4, 128], BF16); TP_lo = P.sb("TP_lo", [128, 4, 128], BF16)
        TD_hi = P.sb("TD_hi", [128, 4, 128], BF16); TD_lo = P.sb("TD_lo", [128, 4, 128], BF16)
        TC_b = P.sb("TC_b", [128, 128], BF16)
        slD_hi = P.sb("slD_hi", [128, 9, 4, 128], BF16)
        slD_lo = P.sb("slD_lo", [128, 9, 4, 128], BF16)
        slF_b = P.sb("slF_b", [128, 9, 128], BF16)
        nlam = P.sb("nlam", [128, 1], F32)
        gcb = [P.sb(f"gcb{j}", [8, 2, PAST + 16], BF16) for j in range(3)]
        P.push()
        sel_bc = P.sb("sel_bc", [128, 27], F32)
        P.dma("sp", sel_bc[:], sel_in[0:1, :].partition_broadcast(128), dst=sel_bc)
        lam_bc = P.sb("lam_bc", [128, 256], F32)
        P.dma("sp", lam_bc[:], lam_in[0:1, :].partition_broadcast(128), dst=lam_bc)
        g_bc = P.sb("g_bc", [128, 128], F32)
        P.dma("sp", g_bc[:], subln_g[0:1, :].partition_broadcast(128), dst=g_bc)
        P.op("dve", lambda e: e.tensor_scalar(out=g_bc[:], in0=g_bc[:], scalar1=(1.0 - LAM_INIT), scalar2=None, op0=ALU.mult),
             reads=[g_bc], writes=[g_bc])
        lt = P.sb("lt", [128, 128], F32)
        lsum = P.sb("lsum", [128, 2], F32)
        for i_ in range(2):
            P.op("dve", lambda e, i_=i_: e.tensor_tensor(out=lt[:, i_ * 64:(i_ + 1) * 64], in0=lam_bc[:, i_ * 128:i_ * 128 + 64],
                                                         in1=lam_bc[:, i_ * 128 + 64:i_ * 128 + 128], op=ALU.mult),
                 reads=[lam_bc], writes=[lt])
        P.op("dve", lambda e: e.tensor_reduce(out=lsum[:], in_=lt[:].rearrange("p (a d) -> p a d", d=64), axis=AX.X, op=ALU.add),
             reads=[lt], writes=[lsum])
        P.op("act", lambda e: e.activation(out=lsum[:], in_=lsum[:], func=AF.Exp), reads=[lsum], writes=[lsum])
        P.op("dve", lambda e: e.tensor_tensor(out=nlam[:], in0=lsum[:, 1:2], in1=lsum[:, 0:1], op=ALU.subtract), reads=[lsum], writes=[nlam])
        P.op("dve", lambda e: e.tensor_scalar(out=nlam[:], in0=nlam[:], scalar1=-LAM_INIT, scalar2=None, op0=ALU.add), reads=[nlam], writes=[nlam])

        def hilo(src, hi, lo, tmp, shape_ap=lambda t: t[:]):
            P.op("dve", lambda e: e.tensor_copy(out=shape_ap(hi), in_=shape_ap(src)), reads=[src], writes=[hi])
            P.op("dve", lambda e: e.tensor_tensor(out=shape_ap(tmp), in0=shape_ap(src), in1=shape_ap(hi), op=ALU.subtract),
                 reads=[src, hi], writes=[tmp])
            P.op("dve", lambda e: e.tensor_copy(out=shape_ap(lo), in_=shape_ap(tmp)), reads=[tmp], writes=[lo])
        ttmp = P.sb("ttmp", [128, 4, 128], F32)
        hilo(TP, TP_hi, TP_lo, ttmp)
        hilo(TDg, TD_hi, TD_lo, ttmp)
        P.op("dve", lambda e: e.memset(TDg[64:128, :, 0:64], NEGM), reads=[TDg], writes=[TDg])
        TC = P.sb("TC", [128, 128], F32)
        P.op("pool", lambda e: e.affine_select(out=TC[:], in_=zero_f[:], pattern=[[1, 128]], compare_op=ALU.is_ge, fill=NEGM,
                                                base=0, channel_multiplier=-1), reads=[zero_f], writes=[TC])
        P.op("dve", lambda e: e.tensor_copy(out=TC_b[:], in_=TC[:]), reads=[TC], writes=[TC_b])
        slD = P.sb("slD", [128, 9, 4, 128], F32)
        slF = P.sb("slF", [128, 9, 128], F32)
        sm = P.sb("sm", [128, 9], F32)
        P.op("dve", lambda e: e.tensor_scalar(out=sm[:], in0=sel_bc[:].rearrange("p (s t) -> p s t", t=3)[:, :, 2], scalar1=NEGM, scalar2=None,
                                              op0=ALU.mult), reads=[sel_bc], writes=[sm])
        for s_ in range(9):
            P.op("dve", lambda e, s_=s_: e.tensor_scalar(out=slD[:, s_, :, :], in0=TP[:], scalar1=sel_bc[:, 3 * s_:3 * s_ + 1],
                                                         scalar2=sm[:, s_:s_ + 1], op0=ALU.mult, op1=ALU.add),
                 reads=[TP, sel_bc, sm], writes=[slD], nowaw=True)
            P.op("dve", lambda e, s_=s_: e.scalar_tensor_tensor(out=slD[:, s_, :, :], in0=TDg[:], scalar=sel_bc[:, 3 * s_ + 1:3 * s_ + 2],
                                                                in1=slD[:, s_, :, :], op0=ALU.mult, op1=ALU.add),
                 reads=[TDg, sel_bc, slD], writes=[slD])
            P.op("dve", lambda e, s_=s_: e.tensor_scalar(out=slF[:, s_, :], in0=TC[:], scalar1=sel_bc[:, 3 * s_ + 1:3 * s_ + 2],
                                                         scalar2=sm[:, s_:s_ + 1], op0=ALU.mult, op1=ALU.add),
                 reads=[TC, sel_bc, sm], writes=[slF], nowaw=True)
        sltmp = P.sb("sltmp", [128, 9, 4, 128], F32)
        hilo(slD, slD_hi, slD_lo, sltmp)
        P.op("dve", lambda e: e.tensor_copy(out=slF_b[:], in_=slF[:]), reads=[slF], writes=[slF_b])

        P.pop()
        Gsel = P.sb("Gsel", [8, NOWN], F32)
        onesQ = P.sb("onesQ", [8, TOWN], BF16)
        P.op("pool", lambda e: e.memset(onesQ[:], 1.0), writes=[onesQ])
        P.push()
        selB_bc = P.sb("selB_bc", [8, NR, NB], F32)
        P.dma("sp", selB_bc[:].rearrange("p r j -> p (r j)"), selB[0:1, :].partition_broadcast(8), dst=selB_bc)
        gprod = P.sb("gprod", [8, NR, NB], F32)
        P.op("dve", lambda e: e.tensor_tensor(out=gprod[:], in0=selB_bc[:], in1=Gend[:, 0:NB].unsqueeze(1).to_broadcast([8, NR, NB]),
                                              op=ALU.mult), reads=[selB_bc, Gend], writes=[gprod])
        P.op("dve", lambda e: e.memset(Gsel[:], 0.0), writes=[Gsel])
        P.op("dve", lambda e: e.tensor_reduce(out=Gsel[:, 0:NR], in_=gprod[:], axis=AX.X, op=ALU.add), reads=[gprod, Gsel], writes=[Gsel])
        P.pop()
        P.push()
        Gq = P.sb("Gq", [8, NOWN, 128], F32)
        W_ = NOWN * 128
        qh = [P.sb(f"qh{j}", [8, W_], BF16) for j in range(3)]
        qr = P.sb("qr", [8, W_], F32)
        for b_ in range(NR):
            P.op("dve", lambda e, b_=b_: e.tensor_tensor_scan(out=Gq[:, b_, :], data0=ones_f[0:8, 0:128], data1=fTo[:, b_, :],
                                                             initial=Gsel[:, b_:b_ + 1], op0=ALU.mult, op1=ALU.add),
                 reads=[fTo, ones_f, Gsel], writes=[Gq], nowaw=True)
        P.op("dve", lambda e: e.memset(Gq[:, NR, :], 0.0), writes=[Gq], nowaw=True)
        P.push()
        clf = P.sb("clf", [8, PAST], F32)
        Gcs = P.sb("Gcs", [8, PAST + 16], F32)
        gcr = P.sb("gcr", [8, PAST + 16], F32)
        gtmp = [P.sb(f"gtmp{j}", [8, PAST + 16], BF16) for j in range(3)]
        for sq in range(2):
            P.dma("sp", clf[:], clfT[sq], dst=clf)
            for cch in range(PAST // 512):
                init = 0.0 if cch == 0 else Gcs[:, cch * 512 - 1:cch * 512]
                P.op("dve", lambda e, cch=cch, init=init: e.tensor_tensor_scan(
                    out=Gcs[:, cch * 512:(cch + 1) * 512], data0=ones_f[0:8, 0:512], data1=clf[:, cch * 512:(cch + 1) * 512],
                    initial=init, op0=ALU.mult, op1=ALU.subtract), reads=[clf, ones_f, Gcs], writes=[Gcs])
            P.op("dve", lambda e, sq=sq: e.tensor_tensor_scan(out=Gq[:, NR, sq * 16:(sq + 1) * 16], data0=ones_f[0:8, 0:16],
                                                             data1=fTo[:, NR, sq * 16:(sq + 1) * 16], initial=Gcs[:, PAST - 1:PAST],
                                                             op0=ALU.mult, op1=ALU.add), reads=[fTo, ones_f, Gcs, Gq], writes=[Gq])
            P.op("dve", lambda e, sq=sq: e.tensor_copy(out=Gcs[:, PAST:PAST + 16], in_=Gq[:, NR, sq * 16:(sq + 1) * 16]),
                 reads=[Gq], writes=[Gcs])
            split3(Gcs, gtmp[0], gtmp[1], gtmp[2], gcr, 8, PAST + 16)
            for j in range(3):
                P.op("pool", lambda e, j=j, sq=sq: e.tensor_copy(out=gcb[j][:, sq, :], in_=gtmp[j][:]), reads=[gtmp[j]], writes=[gcb[j]])
        P.pop()
        Gq2 = Buf(P, "Gq2", Gq[:].rearrange("p b t -> p (b t)"))
        Gq2.w = Gq.w
        split3(Gq2, qh[0], qh[1], qh[2], qr, 8, W_, neg=True)
        for j in range(3):
            P.dma("sp", D_QT[8:16, 64 + j, :], qh[j][:], src=qh[j], dst=D_QT)
        gcb2 = gcb
        P.pop()

        NCH = 8
        ktile = [P.sb(f"ktile{i}", [128, NCH * 128], BF16) for i in range(3)]
        vtile = [P.sb(f"vtile{i}", [128, NCH * 130], BF16) for i in range(3)]
        qtileD = [P.sb(f"qtileD{i}", [128, TOWN], BF16) for i in range(2)]
        qtileF = [P.sb(f"qtileF{i}", [128, TOWN], BF16) for i in range(2)]
        qtile = qtileF
        PT = [P.sb(f"PT{i}", [128, 1024], BF16) for i in range(2)]
        O1 = P.sb("O1", [128, NOWN, 128], F32)
        for kt in ktile:
            P.op("pool", lambda e, kt=kt: e.memset(kt[:], 0.0), writes=[kt])
            P.op("pool", lambda e, kt=kt: e.memset(kt[64:67, :], 1.0), writes=[kt])
        for qt in qtileD + qtileF:
            P.op("pool", lambda e, qt=qt: e.memset(qt[:], 0.0), writes=[qt])
        P.fence()
        for qt in qtileF:
            P.dma("sp", qt[67:70, :], onesQ[0:3, 0:TOWN], src=onesQ, dst=qt)
        P.fence()
        SB2 = [Buf(P, f"SB2_{i}", psall[:, i * 1024:(i + 1) * 1024]) for i in range(2)]
        ACC = Buf(P, "ACC", psall[:, 2048:3584])
        _ACCbank = [Buf(P, f"ACCbank{i}", None) for i in range(3)]
        ACCb = [_ACCbank[a // 3] for a in range(8)]
        B7 = Buf(P, "B7", psall[:, 3584:4096])

        def acc_col(a, W):
            return (a // 3) * 512 + (a % 3) * W

        Ubuf = P.sb("Ubuf", [128, NOWN, 130], F32)
        rsall = P.sb("rsall", [128, NOWN], F32)
        P.op("pool", lambda e: e.memset(Ubuf[:], 1.0), writes=[Ubuf])
        P.fence()

        def normalize(u, blk, acc_ap, npart, AB):
            Wc = 130 if u < 8 else 66
            P.op("dve", lambda e: e.tensor_copy(out=Ubuf[0:npart, blk, 0:Wc], in_=acc_ap[:, 0:Wc]), reads=[AB], writes=[Ubuf], nowaw=True)

        def unit_finish(u):
            if u < 8:
                h, s_ = u // 2, u % 2
                P.op("dve", lambda e: e.reciprocal(out=rsall[:], in_=Ubuf[:, :, 128]), reads=[Ubuf], writes=[rsall])
                rb_ = lambda: rsall[:].unsqueeze(2).to_broadcast([128, NOWN, 128])
                if s_ == 0:
                    P.op("dve", lambda e: e.tensor_tensor(out=O1[:], in0=Ubuf[:, :, 0:128], in1=rb_(), op=ALU.mult),
                         reads=[Ubuf, rsall], writes=[O1])
                else:
                    U_ = lambda: Ubuf[:, :, 0:128]
                    P.op("dve", lambda e: e.tensor_tensor(out=U_(), in0=U_(), in1=rb_(), op=ALU.mult),
                         reads=[Ubuf, rsall], writes=[Ubuf])
                    P.op("dve", lambda e: e.scalar_tensor_tensor(out=U_(), in0=U_(), scalar=nlam[:, 0:1], in1=O1[:], op0=ALU.mult, op1=ALU.add),
                         reads=[Ubuf, nlam, O1], writes=[Ubuf])
                    P.op("dve", lambda e: e.tensor_copy(out=oa_bf[:, :, h * 128:(h + 1) * 128], in_=U_()), reads=[Ubuf], writes=[oa_bf], nowaw=True)
                    P.op("dve", lambda e: e.tensor_tensor(out=U_(), in0=U_(), in1=U_(), op=ALU.mult), reads=[Ubuf], writes=[Ubuf])
                    P.op("dve", lambda e: e.tensor_reduce(out=SSQ[:, :, h], in_=U_(), axis=AX.X, op=ALU.add), reads=[Ubuf], writes=[SSQ], nowaw=True)
            else:
                h = u - 8
                P.op("dve", lambda e: e.reciprocal(out=rsall[:], in_=Ubuf[:, :, 64]), reads=[Ubuf], writes=[rsall])
                P.op("dve", lambda e: e.tensor_tensor(out=ob_bf[:, :, h * 64:(h + 1) * 64], in0=Ubuf[:, :, 0:64],
                                                      in1=rsall[:].unsqueeze(2).to_broadcast([128, NOWN, 64]), op=ALU.mult),
                     reads=[Ubuf, rsall], writes=[ob_bf], nowaw=True)

        kS32 = [P.sb(f"kS32_{i}", [64, PAST], F32) for i in range(1)] * 2
        kS = [P.sb(f"kS{i}", [70, PAST + 16], BF16) for i in range(2)]
        vS32 = [P.sb(f"vS32_{i}", [128, NKB_S, 128], F32) for i in range(1)] * 2
        vSbD = P.sb("vSbD", [128, NKB_S * 130], BF16)
        vSbF = P.sb("vSbF", [128, NKB_S * 66], BF16)
        PTs = [P.sb(f"PTs{i}", [128, NKB_S + 1, 32], BF16) for i in range(2)]
        vsnD = [P.sb(f"vsnD{i}", [16, 4, 130], BF16) for i in range(2)]
        vsnF = [P.sb(f"vsnF{i}", [16, 8, 66], BF16) for i in range(2)]
        for i2 in range(2):
            P.op("pool", lambda e, i2=i2: e.memset(kS[i2][64:67, :], 1.0), writes=[kS[i2]])
            P.op("pool", lambda e, i2=i2: e.memset((vSbD, vSbF)[i2][:], 1.0), writes=[(vSbD, vSbF)[i2]])
            P.op("pool", lambda e, i2=i2: e.memset(PTs[i2][:], 0.0), writes=[PTs[i2]])
            P.dma("sp", vsnD[i2][:], vsn_d[i2 * 16:(i2 + 1) * 16, :, :], src=vsn_d, dst=vsnD[i2])
            P.dma("sp", vsnF[i2][:], vsn_f[i2 * 16:(i2 + 1) * 16, :, :], src=vsn_f, dst=vsnF[i2])
        P.fence()
        if 3 in phases:
            wc32 = [P.sb(f"wc32_{i}", [128, 512], F32) for i in range(1)] * 2
            wc16 = [P.sb(f"wc16_{i}", [128, 512], BF16) for i in range(1)] * 2
            wjobs = []
            for (src_, dstb, rows, cols) in ((w_a, D_WA, 512, 1024), (w_b, D_WB, 512, 1024), (w_o, D_WO, 1024, 1024),
                                             (w_f1, D_WF1, 1024, 5632), (w_f2, D_WF2, 2816, 1024)):
                for r_ in range(rows // 128):
                    for c0_ in range(0, cols, 512):
                        c1_ = min(cols, c0_ + 512)
                        wjobs.append((src_, dstb, r_, c0_, c1_))
            for i_, (src_, dstb, r_, c0_, c1_) in enumerate(wjobs):
                a32, a16 = wc32[i_ % 2], wc16[i_ % 2]
                n_ = c1_ - c0_
                P.dma("pool", a32[:, 0:n_], src_[r_ * 128:(r_ + 1) * 128, c0_:c1_], dst=a32)
                P.op("pool", lambda e, a32=a32, a16=a16, n_=n_: e.tensor_copy(out=a16[:, 0:n_], in_=a32[:, 0:n_]), reads=[a32], writes=[a16])
                P.dma("pool", dstb[r_ * 128:(r_ + 1) * 128, c0_:c1_], a16[:, 0:n_], src=a16, dst=dstb)
        passes = [list(range(p0, min(p0 + 8, NR))) for p0 in range(0, NR, 8)]
        ci = [0]
        for u in range(16):
            isd = u < 8
            K_ = 64 if isd else 70
            W = 130 if isd else 66
            hv = (u // 2) if isd else (u - 8)
            qt = (qtileD if isd else qtileF)[u % 2]
            def load_q(uu):
                if uu >= 16:
                    return
                d_ = uu < 8
                qt_ = (qtileD if d_ else qtileF)[uu % 2]
                nrow = 64 if d_ else 67
                P.dma("sp", qt_[0:nrow, :], D_QT[uu, 0:nrow, :], src=D_QT, dst=qt_)

            if u == 0:
                load_q(0)
            load_q(u + 1)
            if isd:
                ckT, crow, cv, dh = ckdT, u * 64, cvd, 128
                cc_, s__ = u // 2, u % 2
            else:
                ckT, crow, cv, dh = ckfT, (u - 8) * 64, cvf, 64
                cc_, s__ = 4 + (u - 8) // 2, (u - 8) % 2

            def samp_prep_k(u, sq, isd=isd, hv=hv, ckT=ckT, crow=crow, cc_=cc_, s__=s__):
                kst, ks = kS32[sq], kS[sq]
                P.dma("sp", kst[:], ckT[sq, crow:crow + 64, :], dst=kst)
                P.op("dve", lambda e, kst=kst, ks=ks: e.tensor_copy(out=ks[0:64, 0:PAST], in_=kst[:]), reads=[kst], writes=[ks], nowaw=True)
                P.dma("sp", ks[0:64, PAST:PAST + 16], KTn[64 * s__:64 * s__ + 64, cc_, sq * 16:(sq + 1) * 16], src=KTn, dst=ks)
                if not isd:
                    for j3 in range(3):
                        P.dma("sp", ks[67 + j3:68 + j3, :], gcb[j3][hv:hv + 1, sq, :], src=gcb2[j3], dst=ks)

            def samp_prep_v(u, sq, isd=isd, hv=hv, cv=cv, dh=dh, W=W):
                vst, vs = vS32[sq], (vSbD if isd else vSbF)
                P.dma("sp", vst[:, :, 0:dh], cv[sq, :, hv * dh:(hv + 1) * dh].rearrange("(b p) d -> p b d", p=128), dst=vst)
                P.op("dve", lambda e, vst=vst, vs=vs, dh=dh, W=W: e.tensor_copy(
                    out=vs[:, 0:NKB_S * W].rearrange("p (b w) -> p b w", w=W)[:, :, 0:dh], in_=vst[:, :, 0:dh]), reads=[vst], writes=[vs])

            for R in passes:
                r0, nr = R[0], len(R)
                nj = 8 * (r0 + nr)
                nchunks = (nj + NCH - 1) // NCH
                chunk_tiles = {}

                def load_chunk(ch, u=u, isd=isd, hv=hv, W=W):
                    if ch in chunk_tiles or ch >= nchunks:
                        return
                    kt = ktile[ci[0] % 3]
                    vt = vtile[ci[0] % 3]
                    ci[0] += 1
                    chunk_tiles[ch] = (kt, vt)
                    k0 = ch * NCH * 128
                    P.dma("sp", kt[0:64, :], D_KT[u, 0:64, k0:k0 + NCH * 128], src=D_KT, dst=kt)
                    if not isd:
                        P.dma("sp", kt[67:70, :], D_KT[u, 67:70, k0:k0 + NCH * 128], src=D_KT, dst=kt)
                    DV = D_VD if isd else D_VF
                    P.dma("sp", vt[:, 0:NCH * W].rearrange("p (g x) -> p g x", g=2),
                          DV[hv, 2 * ch:2 * ch + 2].rearrange("g p x -> p g x"), src=DV, dst=vt)

                def stage_qk(j):
                    ch, jl = j // NCH, j % NCH
                    load_chunk(ch)
                    kt, vt = chunk_tiles[ch]
                    a0 = max(j // 8, r0) - r0
                    sb2 = SB2[j % 2]
                    for half in range(2):
                        lo_ = max(a0, 4 * half)
                        hi_ = min(nr, 4 * half + 4)
                        if lo_ >= hi_:
                            continue
                        P.op("pe", lambda e, sb2=sb2, kt=kt, jl=jl, lo_=lo_, hi_=hi_, qt=qt, r0=r0: e.matmul(
                            sb2[:, lo_ * 128:hi_ * 128], lhsT=kt[:, jl * 128:(jl + 1) * 128],
                            rhs=qt[:, (r0 + lo_) * 128:(r0 + hi_) * 128], start=True, stop=True, skip_group_check=True),
                            reads=[kt, qt], writes=[sb2])
                    cands = []
                    rr = j // 8
                    if rr in R:
                        cands.append((rr, j - (8 * rr - 1)))
                    if (j + 1) % 8 == 0 and (j + 1) // 8 in R:
                        cands.append(((j + 1) // 8, 0))
                    for (rr, sl) in cands:
                        a = rr - r0
                        if isd:
                            for tl in (slD_hi, slD_lo):
                                P.op("pe", lambda e, sb2=sb2, a=a, tl=tl, sl=sl, hv=hv: e.matmul(
                                    sb2[:, a * 128:(a + 1) * 128], lhsT=ident[:], rhs=tl[:, sl, hv, :], start=False, stop=True,
                                    skip_group_check=True), reads=[ident, tl], writes=[sb2])
                        else:
                            P.op("pe", lambda e, sb2=sb2, a=a, sl=sl: e.matmul(
                                sb2[:, a * 128:(a + 1) * 128], lhsT=ident[:], rhs=slF_b[:, sl, :], start=False, stop=True,
                                skip_group_check=True), reads=[ident, slF_b], writes=[sb2])

                def stage_exp(j):
                    a0 = max(j // 8, r0) - r0
                    sb2, pt = SB2[j % 2], PT[j % 2]
                    P.op("act", lambda e, sb2=sb2, pt=pt, a0=a0, nr=nr: e.activation(
                        out=pt[:, a0 * 128:nr * 128], in_=sb2[:, a0 * 128:nr * 128], func=AF.Exp), reads=[sb2], writes=[pt])

                def stage_pv(j):
                    ch, jl = j // NCH, j % NCH
                    kt, vt = chunk_tiles[ch]
                    a0 = max(j // 8, r0) - r0
                    pt = PT[j % 2]
                    for a in range(a0, nr):
                        c = acc_col(a, W)
                        last = (j == 8 * (r0 + a) + 7)
                        P.op("pe", lambda e, pt=pt, vt=vt, a=a, c=c, jl=jl, j=j, last=last, W=W: e.matmul(
                            ACC[:, c:c + W], lhsT=pt[:, a * 128:(a + 1) * 128], rhs=vt[:, jl * W:(jl + 1) * W],
                            start=(j == 0 and a % 3 == 0), stop=last, skip_group_check=True), reads=[pt, vt], writes=[ACCb[a]])
                        if last:
                            normalize(u, r0 + a, ACC[:, c:c + W], 128, ACCb[a])

                load_chunk(0)
                load_chunk(1)
                stage_qk(0)
                for j in range(nj):
                    if R is passes[-1] and 'samp' not in SKIP:
                        if j == 0:
                            samp_prep_k(u, 0)
                            samp_prep_v(u, 0)
                        if j == nj // 2:
                            samp_prep_k(u, 1)
                    stage_exp(j)
                    if j + 1 < nj:
                        if (j + 1) % NCH == 0:
                            load_chunk((j + 1) // NCH + 1)
                        stage_qk(j + 1)
                    stage_pv(j)
            for sq in range(0 if 'samp' in SKIP else 2):
                i2 = sq
                kst, ks, vst, vs, pts = kS32[i2], kS[i2], vS32[i2], (vSbD if isd else vSbF), PTs[sq]
                if sq == 1:
                    samp_prep_v(u, 1)
                qcol = NR * 128 + sq * 16
                for blk in range(NKB_S):
                    P.op("pe", lambda e, ks=ks, qt=qt, blk=blk, K_=K_, qcol=qcol: e.matmul(
                        B7[:, blk * 16:(blk + 1) * 16], lhsT=ks[0:K_, blk * 128:(blk + 1) * 128], rhs=qt[0:K_, qcol:qcol + 16],
                        start=(blk == 0), stop=True, skip_group_check=True), reads=[ks, qt], writes=[B7])
                P.op("pe", lambda e, ks=ks, qt=qt, K_=K_, qcol=qcol: e.matmul(
                    B7[0:16, 256:272], lhsT=ks[0:K_, PAST:PAST + 16], rhs=qt[0:K_, qcol:qcol + 16],
                    start=False, stop=True, skip_group_check=True), reads=[ks, qt], writes=[B7])
                if isd:
                    for (tp, td) in ((TP_hi, TD_hi), (TP_lo, TD_lo)):
                        P.op("pe", lambda e, tp=tp, hv=hv: e.matmul(
                            B7[:, (NKB_S - 1) * 16:NKB_S * 16], lhsT=ident[:], rhs=tp[:, hv, 0:16], start=False, stop=True,
                            skip_group_check=True), reads=[ident, tp], writes=[B7])
                        P.op("pe", lambda e, td=td, hv=hv: e.matmul(
                            B7[0:16, 256:272], lhsT=ident[0:16, 0:16], rhs=td[0:16, hv, 0:16], start=False, stop=True,
                            skip_group_check=True), reads=[ident, td], writes=[B7])
                else:
                    P.op("pe", lambda e: e.matmul(B7[0:16, 256:272], lhsT=ident[0:16, 0:16], rhs=TC_b[0:16, 0:16], start=False, stop=True,
                                                  skip_group_check=True), reads=[ident, TC_b], writes=[B7])
                P.op("act", lambda e, pts=pts, sq=sq: e.activation(
                    out=pts[:, 0:NKB_S, sq * 16:(sq + 1) * 16], in_=B7[:, 0:NKB_S * 16].rearrange("p (b q) -> p b q", q=16), func=AF.Exp),
                    reads=[B7], writes=[pts], nowaw=True)
                P.op("act", lambda e, pts=pts, sq=sq: e.activation(
                    out=pts[0:16, NKB_S, sq * 16:(sq + 1) * 16], in_=B7[0:16, 256:272], func=AF.Exp), reads=[B7], writes=[pts], nowaw=True)
                vn = (vsnD if isd else vsnF)[sq]
                for blk in range(NKB_S):
                    P.op("pe", lambda e, pts=pts, vs=vs, blk=blk, W=W, sq=sq: e.matmul(
                        ACC[0:32, 0:W], lhsT=pts[:, blk, :], rhs=vs[:, blk * W:(blk + 1) * W], start=(sq == 0 and blk == 0), stop=False,
                        skip_group_check=True), reads=[pts, vs], writes=[ACCb[0]])
                P.op("pe", lambda e, pts=pts, vn=vn, W=W, sq=sq, hv=hv: e.matmul(
                    ACC[0:32, 0:W], lhsT=pts[0:16, NKB_S, :], rhs=vn[0:16, hv, 0:W], start=False, stop=(sq == 1),
                    skip_group_check=True), reads=[pts, vn], writes=[ACCb[0]])
            if 'samp' not in SKIP:
                normalize(u, NR, ACC[0:32, 0:W], 32, ACCb[0])
            unit_finish(u)
        if dbg:
            for (src_, dst_) in ((oa_bf, dbg_oa), (ob_bf, dbg_ob)):
                out_ops.append(P.dma("sp", dst_.rearrange("(b p) c -> p b c", p=128), src_[:], src=src_))
        P.dma("sp", D_OA.t.rearrange("(b p) c -> p b c", p=128), oa_bf[:], src=oa_bf, dst=D_OA)
        P.dma("sp", D_OB.t.rearrange("(b p) c -> p b c", p=128), ob_bf[:], src=ob_bf, dst=D_OB)
        P.pop()
    P.pop()
    P.pop()

    h2T = P.sb("h2T", [128, 8, TOWN], BF16)
    if 3 in phases:
        P.push()
        oabs = [P.sb(f"oab{i}", [128, 512], BF16) for i in range(2)]
        obbs = [P.sb(f"obb{i}", [128, 512], BF16) for i in range(2)]
        PS7b = Buf(P, "PS7b", psall[:, 3584:4096].bitcast(BF16))
        lnbc = P.sb("lnbc", [128, 4, 1024], F32)
        P.dma("sp", lnbc[:].rearrange("p a d -> p (a d)"), ln_in[0:1, :].partition_broadcast(128), dst=lnbc)
        g2_bc = P.sb("g2_bc", [128, 128], F32)
        P.dma("sp", g2_bc[:], subln_g[0:1, :].partition_broadcast(128), dst=g2_bc)
        P.op("dve", lambda e: e.tensor_scalar(out=g2_bc[:], in0=g2_bc[:], scalar1=(1.0 - LAM_INIT), scalar2=None, op0=ALU.mult),
             reads=[g2_bc], writes=[g2_bc])
        RSTD = P.sb("RSTD", [128, NOWN, 4], F32)
        epsb = P.sb("epsb", [128, 1], F32)
        P.op("pool", lambda e: e.memset(epsb[:], LN_EPS), writes=[epsb])
        P.op("act", lambda e: e.activation(out=RSTD[:], in_=SSQ[:], func=AF.Ln, scale=1.0 / 128.0, bias=epsb[:]), reads=[SSQ, epsb], writes=[RSTD])
        P.op("act", lambda e: e.activation(out=RSTD[:], in_=RSTD[:], func=AF.Exp, scale=-0.5), reads=[RSTD], writes=[RSTD])
        modB = [P.sb(f"modB{i}", [128, 4, 1024], F32) for i in range(2)]
        P.push()
        cTr = P.sb("cTr", [128, 8, 256], F32)
        P.dma("sp", cTr[:].rearrange("p k j -> p (k j)"), cT_rep[:, :], dst=cTr)
        sTr = P.sb("sTr", [128, 8, 256], F32)
        P.op("act", lambda e: e.activation(out=sTr[:], in_=cTr[:], func=AF.Silu), reads=[cTr], writes=[sTr])
        wad3 = [P.sb(f"wad3_{i}", [128, 8, 512], F32) for i in range(2)]
        bbc = [P.sb(f"bbc{i}", [128, 512], F32) for i in range(2)]
        for g in range(4, 12):
            wt = wad3[g % 2]
            bb = bbc[g % 2]
            P.dma("sp", wt[:], w_ada_v[:, :, g * 512:(g + 1) * 512], dst=wt)
            P.dma("sp", bb[:], b_ada[0:1, g * 512:(g + 1) * 512].partition_broadcast(128), dst=bb)
            ch, hf = (g - 4) // 2, (g - 4) % 2
            for st_ in range(2):
                bank = banks[(2 * g + st_) % 4]
                for k in range(8):
                    P.op("pe", lambda e, bank=bank, wt=wt, k=k, st_=st_: e.matmul(
                        bank[:, :], lhsT=sTr[:, k, st_ * 128:(st_ + 1) * 128], rhs=wt[:, k, :], start=(k == 0), stop=(k == 7)),
                        reads=[sTr, wt], writes=[bank])
                if ch == 2:
                    P.op("dve", lambda e, bank=bank, bb=bb, st_=st_, ch=ch, hf=hf: e.scalar_tensor_tensor(
                        out=modB[st_][:, ch, hf * 512:(hf + 1) * 512], in0=bank[:, :], scalar=1.0, in1=bb[:], op0=ALU.add, op1=ALU.add),
                        reads=[bank, bb], writes=[modB[st_]], nowaw=True)
                else:
                    P.op("dve", lambda e, bank=bank, bb=bb, st_=st_, ch=ch, hf=hf: e.tensor_tensor(
                        out=modB[st_][:, ch, hf * 512:(hf + 1) * 512], in0=bank[:, :], in1=bb[:], op=ALU.add),
                        reads=[bank, bb], writes=[modB[st_]], nowaw=True)
        P.pop()
        for i_ in range(2):
            P.dma("sp", D_G2[i_], modB[i_][:, 3, :], src=modB[i_], dst=D_G2)
        wa_sb = P.sb("wa_sb", [128, 4, 1024], BF16)
        wb_sb = P.sb("wb_sb", [128, 4, 1024], BF16)
        wo_sb = P.sb("wo_sb", [128, 8, 1024], BF16)
        P.dma("sp", wa_sb[:], D_WA.t.rearrange("(k p) n -> p k n", p=128), src=D_WA, dst=wa_sb)
        P.dma("sp", wb_sb[:], D_WB.t.rearrange("(k p) n -> p k n", p=128), src=D_WB, dst=wb_sb)
        P.dma("sp", wo_sb[:], D_WO.t.rearrange("(k p) n -> p k n", p=128), src=D_WO, dst=wo_sb)

        gat = [P.sb(f"gat{i}", [128, 2048], F32) for i in range(2)]
        xres = [P.sb(f"xres{i}", [128, 1024], F32) for i in range(2)]
        oan = P.sb("oan", [128, 512], BF16)
        oT = P.sb("oT", [128, 8, 128], BF16)
        t1 = P.sb("t1", [128, 1024], F32)
        t2 = P.sb("t2", [128, 1024], F32)
        ybf = P.sb("ybf", [128, 1024], BF16)
        yT = P.sb("yT", [128, 8, 128], BF16)
        x1s = [P.sb(f"x1s{i}", [128, 1024], F32) for i in range(2)]
        st1 = P.sb("st1", [128, 4], F32)

        def layer_norm(src, dst, gi, bi, npart=128):
            P.op("dve", lambda e: e.tensor_reduce(out=st1[:, 0:1], in_=src[:], axis=AX.X, op=ALU.add), reads=[src], writes=[st1])
            P.op("dve", lambda e: e.tensor_scalar(out=st1[:, 0:1], in0=st1[:, 0:1], scalar1=-1.0 / 1024.0, scalar2=None, op0=ALU.mult),
                 reads=[st1], writes=[st1])
            P.op("dve", lambda e: e.tensor_scalar(out=src[:], in0=src[:], scalar1=st1[:, 0:1], scalar2=None, op0=ALU.add),
                 reads=[src, st1], writes=[src])
            P.op("dve", lambda e: e.tensor_tensor(out=dst[:], in0=src[:], in1=src[:], op=ALU.mult), reads=[src], writes=[dst])
            P.op("dve", lambda e: e.tensor_reduce(out=st1[:, 1:2], in_=dst[:], axis=AX.X, op=ALU.add), reads=[dst, st1], writes=[st1])
            P.op("act", lambda e: e.activation(out=st1[:, 2:3], in_=st1[:, 1:2], func=AF.Ln, scale=1.0 / 1024.0, bias=epsb[:]),
                 reads=[st1, epsb], writes=[st1])
            P.op("act", lambda e: e.activation(out=st1[:, 3:4], in_=st1[:, 2:3], func=AF.Exp, scale=-0.5), reads=[st1], writes=[st1])
            P.op("dve", lambda e: e.scalar_tensor_tensor(out=dst[:], in0=src[:], scalar=st1[:, 3:4], in1=lnbc[:, gi, :], op0=ALU.mult, op1=ALU.mult),
                 reads=[src, st1, lnbc], writes=[dst])
            P.op("dve", lambda e: e.tensor_tensor(out=dst[:], in0=dst[:], in1=lnbc[:, bi, :], op=ALU.add), reads=[dst, lnbc], writes=[dst])

        def transposes(src_aps, dst, reads):
            n = len(src_aps)
            for i_, ap_ in enumerate(src_aps):
                P.op("pe", lambda e, i_=i_, ap_=ap_: e.transpose(out=PS7b[:, i_ * 128:(i_ + 1) * 128], in_=ap_, identity=ident[:]),
                     reads=reads + [ident], writes=[PS7b])
            evac(dst[:, 0:n, :].rearrange("p c t -> p (c t)"), PS7b[:, 0:n * 128], [PS7b], [dst])

        oan2 = [oan, P.sb("oan_b", [128, 512], BF16)]
        oT2 = [oT, P.sb("oT_b", [128, 8, 128], BF16)]
        ybf2 = [ybf, P.sb("ybf_b", [128, 1024], BF16)]
        yT2 = [yT, P.sb("yT_b", [128, 8, 128], BF16)]
        tb1 = P.sb("tb1", [128, 1024], F32)
        tb2 = P.sb("tb2", [128, 1024], F32)
        hb = P.sb("hb", [128, 1024], BF16)
        pa_b = [banks[0], banks[1]]
        pb_b = [banks[2], banks[3]]
        ym = [banks[4], banks[5]]

        def front(b):
            ga_ = gat[b % 2]
            xr = xres[b % 2]
            oab = oabs[b % 2]
            obb = obbs[b % 2]
            oan_, oT_, ybf_, yT_ = oan2[b % 2], oT2[b % 2], ybf2[b % 2], yT2[b % 2]
            P.dma("sp", ga_[:], D_GATES[b * 128:(b + 1) * 128, :], src=D_GATES, dst=ga_)
            P.dma("sp", xr[:], x_own[b * 128:(b + 1) * 128, :], dst=xr)
            P.dma("sp", oab[:], D_OA[b * 128:(b + 1) * 128, :], src=D_OA, dst=oab)
            P.dma("sp", obb[:], D_OB[b * 128:(b + 1) * 128, :], src=D_OB, dst=obb)
            P.op("act", lambda e: e.activation(out=ga_[:], in_=ga_[:], func=AF.Sigmoid), reads=[ga_], writes=[ga_])
            for h in range(4):
                P.op("dve", lambda e, h=h: e.scalar_tensor_tensor(
                    out=oan_[:, h * 128:(h + 1) * 128], in0=oab[:, h * 128:(h + 1) * 128], scalar=RSTD[:, b, h:h + 1], in1=g2_bc[:],
                    op0=ALU.mult, op1=ALU.mult), reads=[oab, RSTD, g2_bc], writes=[oan_], nowaw=(h > 0))
            transposes([oan_[:, h * 128:(h + 1) * 128] for h in range(4)] + [obb[:, h * 128:(h + 1) * 128] for h in range(4)], oT_, [oan_, obb])
            for hf in range(2):
                for k in range(4):
                    P.op("pe", lambda e, hf=hf, k=k: e.matmul(pa_b[hf][:, :], lhsT=oT_[:, k, :], rhs=wa_sb[:, k, hf * 512:(hf + 1) * 512],
                                                              start=(k == 0), stop=(k == 3)), reads=[oT_, wa_sb], writes=[pa_b[hf]])
                for k in range(4):
                    P.op("pe", lambda e, hf=hf, k=k: e.matmul(pb_b[hf][:, :], lhsT=oT_[:, 4 + k, :], rhs=wb_sb[:, k, hf * 512:(hf + 1) * 512],
                                                              start=(k == 0), stop=(k == 3)), reads=[oT_, wb_sb], writes=[pb_b[hf]])
            for hf in range(2):
                sl = slice(hf * 512, (hf + 1) * 512)
                P.op("dve", lambda e, hf=hf, sl=sl: e.tensor_tensor(out=t1[:, sl], in0=pa_b[hf][:, :], in1=ga_[:, sl], op=ALU.mult),
                     reads=[pa_b[hf], ga_], writes=[t1], nowaw=True)
                P.op("dve", lambda e, hf=hf, sl=sl: e.tensor_tensor(out=t2[:, sl], in0=pb_b[hf][:, :], in1=ga_[:, 1024 + hf * 512:1024 + (hf + 1) * 512],
                                                                   op=ALU.mult), reads=[pb_b[hf], ga_], writes=[t2], nowaw=True)
            P.op("dve", lambda e: e.tensor_tensor(out=ybf_[:], in0=t1[:], in1=t2[:], op=ALU.add), reads=[t1, t2], writes=[ybf_])
            transposes([ybf_[:, k * 128:(k + 1) * 128] for k in range(8)], yT_, [ybf_])

        def back(b):
            mB = modB[0] if b < NR else modB[1]
            xr = xres[b % 2]
            x1 = x1s[b % 2]
            yT_ = yT2[b % 2]
            for hf in range(2):
                for k in range(8):
                    P.op("pe", lambda e, hf=hf, k=k: e.matmul(ym[hf][:, :], lhsT=yT_[:, k, :], rhs=wo_sb[:, k, hf * 512:(hf + 1) * 512],
                                                              start=(k == 0), stop=(k == 7)), reads=[yT_, wo_sb], writes=[ym[hf]])
            for hf in range(2):
                sl = slice(hf * 512, (hf + 1) * 512)
                P.op("dve", lambda e, hf=hf, sl=sl: e.tensor_tensor(out=tb1[:, sl], in0=ym[hf][:, :], in1=mB[:, 0, sl], op=ALU.mult),
                     reads=[ym[hf], mB], writes=[tb1], nowaw=(hf > 0))
            P.op("dve", lambda e: e.scalar_tensor_tensor(out=tb1[:], in0=xr[:], scalar=ALPHA, in1=tb1[:], op0=ALU.mult, op1=ALU.add),
                 reads=[xr, tb1], writes=[tb1])
            layer_norm(tb1, x1, 0, 1)
            P.dma("sp", D_X1[b * 128:(b + 1) * 128, :], x1[:], src=x1, dst=D_X1)
            P.op("dve", lambda e: e.tensor_tensor(out=tb2[:], in0=x1[:], in1=mB[:, 2, :], op=ALU.mult), reads=[x1, mB], writes=[tb2])
            P.op("dve", lambda e: e.tensor_tensor(out=hb[:], in0=tb2[:], in1=mB[:, 1, :], op=ALU.add), reads=[tb2, mB], writes=[hb])
            for i_ in range(8):
                P.op("pe", lambda e, i_=i_: e.transpose(out=PS7b[:, i_ * 128:(i_ + 1) * 128], in_=hb[:, i_ * 128:(i_ + 1) * 128], identity=ident[:]),
                     reads=[hb, ident], writes=[PS7b])
            evac(h2T[:, :, b * 128:(b + 1) * 128], PS7b[:, :].rearrange("p (c t) -> p c t", t=128), [PS7b], [h2T])

        front(0)
        for b in range(NOWN):
            if b + 1 < NOWN:
                front(b + 1)
            back(b)
        P.pop()

    if 3 in phases:
        P.push()
        lnbc2 = P.sb("lnbc2", [128, 2, 1024], F32)
        P.dma("sp", lnbc2[:].rearrange("p a d -> p (a d)"), ln_in[0:1, 2048:4096].partition_broadcast(128), dst=lnbc2)
        epsb2 = P.sb("epsb2", [128, 1], F32)
        P.op("pool", lambda e: e.memset(epsb2[:], LN_EPS), writes=[epsb2])
        wf1 = P.sb("wf1", [128, 8, 5632], BF16)
        wf2 = P.sb("wf2", [128, 22, 1024], BF16)
        w1v = D_WF1.t.rearrange("(k p) n -> p k n", p=128)
        for k in range(8):
            P.dma("sp" if k % 2 == 0 else "act", wf1[:, k, :], w1v[:, k, :], src=D_WF1, dst=wf1)
        w2v = D_WF2.t.rearrange("(k p) n -> p k n", p=128)
        for k0_ in range(0, 22, 6):
            k1_ = min(22, k0_ + 6)
            P.dma("pool" if (k0_ // 6) % 2 == 0 else "sp", wf2[:, k0_:k1_, :], w2v[:, k0_:k1_, :], src=D_WF2, dst=wf2)
        g2t = [P.sb(f"g2t{i}", [128, 1024], F32) for i in range(2)]
        for i_ in range(2):
            P.dma("sp", g2t[i_][:], D_G2[i_], src=D_G2, dst=g2t[i_])
        eT = [P.sb(f"eT{i}", [128, 256], F32) for i in range(2)]
        aT = [P.sb(f"aT{i}", [128, 256], BF16) for i in range(2)]
        x1r = [P.sb(f"x1r{i}", [128, 1024], F32) for i in range(1)] * 2
        r2 = P.sb("r2", [128, 1024], F32)
        yo = [P.sb(f"yo{i}", [128, 1024], F32) for i in range(2)]
        st2 = P.sb("st2", [128, 4], F32)
        gb_i = [0]
        pairs = [list(range(p0, min(p0 + 2, NOWN))) for p0 in range(0, NOWN, 2)]
        for pr in pairs:
            nt = len(pr) * 128
            t0 = pr[0] * 128
            cbanks = {}

            def st_g(c, nt=nt, t0=t0):
                bank = banks[gb_i[0] % 4]
                gb_i[0] += 1
                cbanks[c] = bank
                for (off, col0) in ((0, c * 128), (256, 2816 + c * 128)):
                    for k in range(8):
                        P.op("pe", lambda e, bank=bank, off=off, col0=col0, k=k, nt=nt, t0=t0: e.matmul(
                            bank[:, off:off + nt], lhsT=wf1[:, k, col0:col0 + 128], rhs=h2T[:, k, t0:t0 + nt], start=(k == 0), stop=(k == 7),
                            skip_group_check=True), reads=[wf1, h2T], writes=[bank])

            def st_act(c, nt=nt):
                bank = cbanks[c]
                et = eT[c % 2]
                at = aT[c % 2]
                P.op("act", lambda e, bank=bank, et=et, nt=nt: e.activation(out=et[:, 0:nt], in_=bank[:, 0:nt], func=AF.Silu),
                     reads=[bank], writes=[et])
                P.op("dve", lambda e, bank=bank, et=et, at=at, nt=nt: e.tensor_tensor(out=at[:, 0:nt], in0=bank[:, 256:256 + nt], in1=et[:, 0:nt], op=ALU.mult),
                     reads=[bank, et], writes=[at])

            def st_out(c, pr=pr):
                at = aT[c % 2]
                for bi_, b in enumerate(pr):
                    for hf in range(2):
                        ab = banks[4 + 2 * bi_ + hf]
                        P.op("pe", lambda e, ab=ab, at=at, bi_=bi_, c=c, hf=hf: e.matmul(
                            ab[:, :], lhsT=at[:, bi_ * 128:(bi_ + 1) * 128], rhs=wf2[:, c, hf * 512:(hf + 1) * 512], start=(c == 0), stop=(c == 21)),
                            reads=[at, wf2], writes=[ab])

            st_g(0)
            for c in range(22):
                st_act(c)
                if c + 1 < 22:
                    st_g(c + 1)
                st_out(c)
            for bi_, b in enumerate(pr):
                xr = x1r[b % 2]
                y_ = yo[b % 2]
                gt = g2t[0] if b < NR else g2t[1]
                P.dma("sp", xr[:], D_X1[b * 128:(b + 1) * 128, :], src=D_X1, dst=xr)
                for hf in range(2):
                    sl = slice(hf * 512, (hf + 1) * 512)
                    ab = banks[4 + 2 * bi_ + hf]
                    P.op("dve", lambda e, ab=ab, sl=sl, gt=gt: e.tensor_tensor(out=r2[:, sl], in0=ab[:, :], in1=gt[:, sl], op=ALU.mult),
                         reads=[ab, gt], writes=[r2], nowaw=(hf > 0))
                P.op("dve", lambda e, xr=xr: e.scalar_tensor_tensor(out=r2[:], in0=xr[:], scalar=ALPHA, in1=r2[:], op0=ALU.mult, op1=ALU.add),
                     reads=[xr, r2], writes=[r2])
                P.op("dve", lambda e: e.tensor_reduce(out=st2[:, 0:1], in_=r2[:], axis=AX.X, op=ALU.add), reads=[r2], writes=[st2])
                P.op("dve", lambda e: e.tensor_scalar(out=st2[:, 0:1], in0=st2[:, 0:1], scalar1=-1.0 / 1024.0, scalar2=None, op0=ALU.mult),
                     reads=[st2], writes=[st2])
                P.op("dve", lambda e: e.tensor_scalar(out=r2[:], in0=r2[:], scalar1=st2[:, 0:1], scalar2=None, op0=ALU.add), reads=[r2, st2], writes=[r2])
                P.op("dve", lambda e, y_=y_: e.tensor_tensor(out=y_[:], in0=r2[:], in1=r2[:], op=ALU.mult), reads=[r2], writes=[y_])
                P.op("dve", lambda e, y_=y_: e.tensor_reduce(out=st2[:, 1:2], in_=y_[:], axis=AX.X, op=ALU.add), reads=[y_, st2], writes=[st2])
                P.op("act", lambda e: e.activation(out=st2[:, 2:3], in_=st2[:, 1:2], func=AF.Ln, scale=1.0 / 1024.0, bias=epsb2[:]),
                     reads=[st2, epsb2], writes=[st2])
                P.op("act", lambda e: e.activation(out=st2[:, 3:4], in_=st2[:, 2:3], func=AF.Exp, scale=-0.5), reads=[st2], writes=[st2])
                P.op("dve", lambda e, y_=y_: e.scalar_tensor_tensor(out=y_[:], in0=r2[:], scalar=st2[:, 3:4], in1=lnbc2[:, 0, :], op0=ALU.mult, op1=ALU.mult),
                     reads=[r2, st2, lnbc2], writes=[y_])
                P.op("dve", lambda e, y_=y_: e.tensor_tensor(out=y_[:], in0=y_[:], in1=lnbc2[:, 1, :], op=ALU.add), reads=[y_, lnbc2], writes=[y_])
                out_ops.append(P.dma("sp", y_own[b * 128:(b + 1) * 128, :], y_[:], src=y_))
        P.pop()
    return nc, P, es, out_ops, locals()


def finish(nc, P, es, out_ops):
    while P.scopes:
        P.pop()
    P.emit(out_ops)
    es.close()
    return nc


def host_inputs(inp, S=16384, PAST=2048):
    NB = S // 128
    NR = NB // 8
    f = lambda a: np.ascontiguousarray(np.asarray(a, dtype=np.float32))
    xp = f(inp["x_prompt"])[0, :S]
    xs = f(inp["x_sample"])
    xT_all = np.ascontiguousarray(xp.T)
    cp = f(inp["c_prompt"])[0]
    cs = f(inp["c_sample"])
    shared = {
        "xT_all": xT_all,
        "b_ada_fm": np.ascontiguousarray(f(inp["b_ada"])[0].reshape(48, 128).T),
        "b_ada": f(inp["b_ada"]),
        "w_ada": f(inp["w_ada"])[0], "w_in": f(inp["w_in"])[0],
        "b_forget": f(inp["b_forget"]),
        "lam": np.concatenate([f(inp[k])[0] for k in ("lambda_q1", "lambda_k1", "lambda_q2", "lambda_k2")])[None, :],
        "subln_g": f(inp["subln_g"]),
        "rel_bias": f(inp["rel_bias"]).reshape(1, 128),
        "w_a": f(inp["w_branch_a"])[0], "w_b": f(inp["w_branch_b"])[0], "w_o": f(inp["w_o"])[0],
        "ln": np.concatenate([f(inp[k])[0] for k in ("ln1_g", "ln1_b", "ln2_g", "ln2_b")])[None, :],
        "w_f1": f(inp["w_ffn_in"])[0], "w_f2": f(inp["w_ffn_out"])[0],
    }
    maps = []
    for c in range(8):
        blocks = [c + 8 * r for r in range(NR)]
        xo = np.zeros((NR * 128 + 128, D), np.float32)
        for r, bl in enumerate(blocks):
            xo[r * 128:(r + 1) * 128] = xp[bl * 128:(bl + 1) * 128]
        xo[NR * 128:NR * 128 + 16] = xs[2 * c]
        xo[NR * 128 + 16:NR * 128 + 32] = xs[2 * c + 1]
        cT = np.stack([cp, cs[2 * c], cs[2 * c + 1]], axis=1)
        cT = cT.reshape(8, 128, 3).transpose(1, 0, 2).reshape(128, 24)
        rep = np.zeros((D, 256), np.float32)
        rep[:, 0:128] = cp[:, None]
        rep[:, 128:144] = cs[2 * c][:, None]
        rep[:, 144:256] = cs[2 * c + 1][:, None]
        rep = rep.reshape(8, 128, 256).transpose(1, 0, 2).reshape(128, 8 * 256)
        sel = np.zeros((9, 3), np.float32)
        for slot in range(9):
            t = slot - 1
            if t == c - 1:
                sel[slot, 0] = 1
            elif t == c:
                sel[slot, 1] = 1
            elif t > c:
                sel[slot, 2] = 1
        selB = np.zeros((NR, NB), np.float32)
        for r, bl in enumerate(blocks):
            selB[r, bl] = 1
        m = dict(shared)
        m.update({
            "xT_own": np.ascontiguousarray(xo.T), "x_own": xo,
            "cT": np.ascontiguousarray(cT), "cT_rep": np.ascontiguousarray(rep),
            "ckdT": np.ascontiguousarray(f(inp["cache_diff_k"])[0, 2 * c:2 * c + 2, :PAST].reshape(2, PAST, 512).transpose(0, 2, 1)),
            "ckfT": np.ascontiguousarray(f(inp["cache_fox_k"])[0, 2 * c:2 * c + 2, :PAST].reshape(2, PAST, 512).transpose(0, 2, 1)),
            "cvd": np.ascontiguousarray(f(inp["cache_diff_v"])[0, 2 * c:2 * c + 2, :PAST].reshape(2, PAST, 512)),
            "cvf": np.ascontiguousarray(f(inp["cache_fox_v"])[0, 2 * c:2 * c + 2, :PAST].reshape(2, PAST, 512)),
            "clfT": np.ascontiguousarray(f(inp["cache_fox_logf"])[0, 2 * c:2 * c + 2, :PAST].transpose(0, 2, 1)),
            "sel": sel.reshape(1, 27), "selB": selB.reshape(1, NR * NB),
        })
        maps.append(m)
    return maps


def assemble(results, S=16384):
    NB = S // 128
    NR = NB // 8
    y_p = np.zeros((1, S, D), np.float32)
    y_s = np.zeros((16, 16, D), np.float32)
    outs_p = {k: np.zeros((S, w), np.float32) for k, w in (("kd", 512), ("vd", 512), ("kf", 512), ("vf", 512), ("lf", 8))}
    outs_s = {k: np.zeros((16, 16, w), np.float32) for k, w in (("kd", 512), ("vd", 512), ("kf", 512), ("vf", 512), ("lf", 8))}
    for c, r in enumerate(results):
        for rr in range(NR):
            bl = c + 8 * rr
            y_p[0, bl * 128:(bl + 1) * 128] = r["y_own"][rr * 128:(rr + 1) * 128]
            for k in outs_p:
                outs_p[k][bl * 128:(bl + 1) * 128] = r[k + "_own"][rr * 128:(rr + 1) * 128]
        o = NR * 128
        y_s[2 * c] = r["y_own"][o:o + 16]
        y_s[2 * c + 1] = r["y_own"][o + 16:o + 32]
        for k in outs_s:
            outs_s[k][2 * c] = r[k + "_own"][o:o + 16]
            outs_s[k][2 * c + 1] = r[k + "_own"][o + 16:o + 32]
    return (y_p, y_s,
            outs_p["kd"].reshape(1, 1, S, 4, 128), outs_p["vd"].reshape(1, 1, S, 4, 128),
            outs_p["kf"].reshape(1, 1, S, 8, 64), outs_p["vf"].reshape(1, 1, S, 8, 64), outs_p["lf"].reshape(1, 1, S, 8),
            outs_s["kd"].reshape(1, 16, 16, 4, 128), outs_s["vd"].reshape(1, 16, 16, 4, 128),
            outs_s["kf"].reshape(1, 16, 16, 8, 64), outs_s["vf"].reshape(1, 16, 16, 8, 64), outs_s["lf"].reshape(1, 16, 16, 8))


def kernel(**inputs):
    from concourse.bass_utils import run_bass_kernel_spmd
    nc, P, es, out_ops, _ = build()
    finish(nc, P, es, out_ops)
    maps = host_inputs(inputs)
    res = run_bass_kernel_spmd(nc, maps, core_ids=list(range(8)))
    return assemble(res.results)
```

```python
from contextlib import ExitStack
import concourse.bass as bass
import concourse.mybir as mybir

F32 = mybir.dt.float32
BF16 = mybir.dt.bfloat16
AF = mybir.ActivationFunctionType
ALU = mybir.AluOpType
AX = mybir.AxisListType

ENGS = ("pe", "act", "dve", "pool", "sp")


class Grp:
    __slots__ = ("sem", "final")

    def __init__(self, sem):
        self.sem = sem
        self.final = 0


class Op:
    __slots__ = ("eng", "fn", "waits", "signal", "value", "grp", "dval")

    def __init__(self, eng, fn):
        self.eng = eng
        self.fn = fn
        self.waits = []
        self.signal = False
        self.value = None
        self.grp = None
        self.dval = None


class Buf:
    def __init__(self, P, name, t=None):
        self.P = P
        self.name = name
        self.t = t
        self.w = []
        self.r_eng = {}
        self.r_dma = []
        self.prev_r = []
        self.ld = None
        self.st = None
        self.ldp = None
        self.stp = None
        P.bufs.append(self)

    def __getitem__(self, k):
        return self.t[k]

    def _dsem(self, which):
        d = getattr(self, which)
        if d is None:
            if self.P.free_sems and not which.endswith("p"):
                sem, cnt = self.P.free_sems.pop()
                d = [sem, cnt, None]
            else:
                sem = self.P.new_sem(f"{which}_{self.name}")
                d = [sem, 0, None]
            setattr(self, which, d)
        return d


class Prog:
    def __init__(self, nc, es):
        self.nc = nc
        self.es = es
        self.ops = {e: [] for e in ENGS}
        self.nsem = 0
        self.esem = {e: self.new_sem("eng_" + e) for e in ENGS}
        self.nbuf = 0
        self.pending_dma = {}
        self.scopes = []
        self.bufs = []
        self.free_sems = []
        self.scope_bufs = []

    def new_sem(self, name):
        self.nsem += 1
        return self.es.enter_context(self.nc.semaphore(f"s{self.nsem}_{name}"))

    def sb(self, name, shape, dt):
        st = self.scopes[-1] if self.scopes else self.es
        t = st.enter_context(self.nc.sbuf_tensor(name, list(shape), dt))
        b = Buf(self, name, t)
        if self.scope_bufs:
            self.scope_bufs[-1].append(b)
        return b

    def push(self):
        self.scopes.append(ExitStack())
        self.scope_bufs.append([])

    def pop(self):
        self.fence()
        self.scopes.pop().close()
        for b in self.scope_bufs.pop():
            for d in (b.ld, b.st):
                if d is not None:
                    self.free_sems.append((d[0], d[1]))
            b.ld = b.st = None

    def ps(self, name, shape, dt):
        t = self.es.enter_context(self.nc.psum_tensor(name, list(shape), dt))
        return Buf(self, name, t)

    def dram(self, name, shape, dt):
        t = self.nc.dram_tensor(name, list(shape), dt).ap()
        return Buf(self, name, t)

    def _deps(self, op, reads, writes, nowaw):
        waits = op.waits
        for b in reads:
            for w in b.w:
                waits.append(w)
        for b in writes:
            if not nowaw:
                for w in b.w:
                    if not (w.eng == "pe" and op.eng == "pe"):
                        waits.append(w)
            for r in b.r_eng.values():
                if not (r.eng == op.eng == "pe"):
                    waits.append(r)
            waits.extend(b.r_dma)
            for r in b.prev_r:
                if not (r.eng == op.eng == "pe"):
                    waits.append(r)
        for w in waits:
            if w.grp is None:
                w.signal = True
        for b in writes:
            if b.r_eng or b.r_dma:
                b.prev_r = list(b.r_eng.values()) + list(b.r_dma)
                b.w = [op]
                b.r_eng = {}
                b.r_dma = []
                if b.st is not None:
                    b.st[2] = None
                if b.stp is not None:
                    b.stp[2] = None
            else:
                if op.grp is None:
                    b.w = [x for x in b.w if x.grp is not None or x.eng != op.eng]
                b.w.append(op)
        for b in reads:
            if op.grp is not None:
                b.r_dma.append(op)
            else:
                b.r_eng[op.eng] = op
            if b.ld is not None:
                b.ld[2] = None
            if b.ldp is not None:
                b.ldp[2] = None

    def op(self, eng, fn, reads=(), writes=(), waits=(), nowaw=False):
        o = Op(eng, fn)
        o.waits.extend(waits)
        self._deps(o, reads, writes, nowaw)
        self.ops[eng].append(o)
        return o

    def dma(self, eng, out, in_, src=None, dst=None, waits=(), nowaw=True, sembuf=None, **kw):
        o = Op(eng, lambda e: e.dma_start(out=out, in_=in_, **kw))
        o.waits.extend(waits)
        sfx = "p" if eng == "pool" else ""
        if sembuf is not None:
            d = sembuf[0]._dsem(sembuf[1])
        elif dst is not None and not dst.name.startswith("D_"):
            d = dst._dsem("ld" + sfx)
        elif src is not None:
            d = src._dsem("st" + sfx)
        else:
            d = dst._dsem("ld" + sfx)
        reads = [src] if src is not None else []
        writes = [dst] if dst is not None else []
        if d[2] is None:
            d[2] = Grp(d[0])
        o.grp = d[2]
        d[1] += 16
        o.dval = d[1]
        o.grp.final = d[1]
        self._deps(o, reads, writes, nowaw)
        if dst is not None and (dst.ld is d or dst.ldp is d):
            d[2] = o.grp
        if src is not None and (src.st is d or src.stp is d):
            d[2] = o.grp
        self.ops[eng].append(o)
        self.pending_dma[id(o.grp)] = o
        return o

    def fence(self):
        lasts = []
        for e in ENGS:
            for o in reversed(self.ops[e]):
                if o.grp is None:
                    lasts.append(o)
                    break
        dmas = list(self.pending_dma.values())
        for o in lasts:
            o.signal = True
        for e in ENGS:
            f = Op(e, lambda en: en.nop())
            f.waits = lasts + dmas
            self.ops[e].append(f)
        self.pending_dma = {}
        for b in self.bufs:
            for d in (b.ld, b.st, b.ldp, b.stp):
                if d is not None:
                    d[2] = None

    def emit(self, final_waits):
        nc = self.nc
        for e in ENGS:
            n = 0
            for o in self.ops[e]:
                if o.signal:
                    n += 1
                    o.value = n
        fin = Op("sp", lambda e: e.nop())
        fin.waits.extend(final_waits)
        for w in final_waits:
            if w.grp is None:
                w.signal = True
        for e in ENGS:
            n = 0
            for o in self.ops[e]:
                if o.signal:
                    n += 1
                    o.value = n
        self.ops["sp"].append(fin)
        esem = self.esem

        def run(engname, eng):
            seen = {}
            for o in self.ops[engname]:
                for w in o.waits:
                    if w.grp is not None:
                        sem, val = w.grp.sem, w.grp.final
                    else:
                        sem, val = esem[w.eng], w.value
                    if seen.get(sem, 0) < val:
                        eng.wait_ge(sem, val)
                        seen[sem] = val
                ins = o.fn(eng)
                if o.grp is not None:
                    ins.then_inc(o.grp.sem, 16)
                elif o.signal:
                    ins.then_inc(esem[engname], 1)

        with nc.Block() as block:
            @block.tensor
            def _(e):
                run("pe", e)

            @block.scalar
            def _(e):
                run("act", e)

            @block.vector
            def _(e):
                run("dve", e)

            @block.gpsimd
            def _(e):
                run("pool", e)

            @block.sync
            def _(e):
                run("sp", e)
import math
import numpy as np

D = 1024
NCOL = 5128
C_QA, C_KA, C_VA, C_QB, C_KB, C_VB, C_F, C_GA, C_GB = 0, 512, 1024, 1536, 2048, 2560, 3072, 3080, 4104
ALPHA = 2.0 ** 0.25
LN_EPS = 1e-5
LAM_INIT = 0.8 - 0.6 * math.exp(0.0)
NEGM = -30000.0


def t5_thresholds():
    import jax, jax.numpy as jnp
    with jax.default_device(jax.devices("cpu")[0]):
        rel = jnp.arange(-255, 128, dtype=jnp.int32)
        nb = 16
        ret = jnp.where(rel > 0, nb, 0)
        n = jnp.abs(rel)
        max_exact = nb // 2
        nf = jnp.maximum(n, 1).astype(jnp.float32)
        large = max_exact + (jnp.log(nf / max_exact) / math.log(128 / max_exact) * (nb - max_exact)).astype(jnp.int32)
        large = jnp.minimum(large, nb - 1)
        bk = np.asarray(ret + jnp.where(n < max_exact, n, large))
    rels = np.arange(-255, 128)
    th = []
    for i in range(1, len(rels)):
        if bk[i] != bk[i - 1]:
            th.append((int(rels[i]), int(bk[i - 1]), int(bk[i])))
    return int(bk[0]), th


def build(S=16384, PAST=2048, phases=(0, 1, 2, 3), dbg=False):
    from contextlib import ExitStack
    import os
    SKIP = os.environ.get('SKIP', '').split(',')
    NB = S // 128
    NR = NB // 8
    NOWN = NR + 1
    TOWN = NOWN * 128
    NG = NB // 4
    NKB_S = PAST // 128

    nc = bass.Bass("TRN2", target_bir_lowering=False)

    def ein(name, shape, dt=F32):
        return nc.dram_tensor(name, list(shape), dt, kind="ExternalInput").ap()

    def eout(name, shape, dt=F32):
        return nc.dram_tensor(name, list(shape), dt, kind="ExternalOutput").ap()

    xT_all = ein("xT_all", [D, S])
    xT_own = ein("xT_own", [D, TOWN])
    x_own = ein("x_own", [TOWN, D])
    cT = ein("cT", [128, 24])
    cT_rep = ein("cT_rep", [128, 8 * 256])
    b_ada_fm = ein("b_ada_fm", [128, 48])
    b_ada = ein("b_ada", [1, 6144])
    w_ada = ein("w_ada", [D, 6144])
    w_in = ein("w_in", [D, NCOL])
    b_forget = ein("b_forget", [1, 8])
    lam_in = ein("lam", [1, 256])
    subln_g = ein("subln_g", [1, 128])
    rel_bias = ein("rel_bias", [1, 128])
    w_a = ein("w_a", [512, D])
    w_b = ein("w_b", [512, D])
    w_o = ein("w_o", [D, D])
    ln_in = ein("ln", [1, 4096])
    w_f1 = ein("w_f1", [D, 5632])
    w_f2 = ein("w_f2", [2816, D])
    ckdT = ein("ckdT", [2, 512, PAST])
    ckfT = ein("ckfT", [2, 512, PAST])
    cvd = ein("cvd", [2, PAST, 512])
    cvf = ein("cvf", [2, PAST, 512])
    clfT = ein("clfT", [2, 8, PAST])
    sel_in = ein("sel", [1, 27])
    selB = ein("selB", [1, NR * NB])

    y_own = eout("y_own", [TOWN, D])
    kd_own = eout("kd_own", [TOWN, 512])
    vd_own = eout("vd_own", [TOWN, 512])
    kf_own = eout("kf_own", [TOWN, 512])
    vf_own = eout("vf_own", [TOWN, 512])
    lf_own = eout("lf_own", [TOWN, 8])
    if dbg:
        dbg_oa = eout("dbg_oa", [TOWN, 512], BF16)
        dbg_ob = eout("dbg_ob", [TOWN, 512], BF16)

    es = ExitStack()
    P = Prog(nc, es)
    out_ops = []

    D_KT = P.dram("D_KT", [16, 72, S], BF16)
    D_QT = P.dram("D_QT", [16, 72, TOWN], BF16)
    D_VD = P.dram("D_VD", [4, NG, 128, 4 * 130], BF16)
    D_VF = P.dram("D_VF", [8, NG, 128, 4 * 66], BF16)
    D_GATES = P.dram("D_GATES", [TOWN, 2048], F32)
    D_X1 = P.dram("D_X1", [TOWN, D], F32)
    D_G2 = P.dram("D_G2", [2, 128, 1024], F32)
    D_OA = P.dram("D_OA", [TOWN, 512], BF16)
    D_OB = P.dram("D_OB", [TOWN, 512], BF16)
    D_WA = P.dram("D_WA", [512, D], BF16)
    D_WB = P.dram("D_WB", [512, D], BF16)
    D_WO = P.dram("D_WO", [D, D], BF16)
    D_WF1 = P.dram("D_WF1", [D, 5632], BF16)
    D_WF2 = P.dram("D_WF2", [2816, D], BF16)

    psall = es.enter_context(nc.psum_tensor("psall", [128, 4096], F32))
    banks = [Buf(P, f"bank{i}", psall[:, i * 512:(i + 1) * 512]) for i in range(8)]

    ident = P.sb("ident", [128, 128], BF16)
    ones_f = P.sb("ones_f", [128, 512], F32)
    ones_b = P.sb("ones_b", [128, 512], BF16)
    P.op("pool", lambda e: e.memset(ident[:], 0.0), writes=[ident])
    P.op("pool", lambda e: e.affine_select(out=ident[:], in_=ident[:], pattern=[[-1, 128]],
                                            compare_op=ALU.not_equal, fill=1.0, base=0, channel_multiplier=1),
         reads=[ident], writes=[ident])
    P.op("pool", lambda e: e.memset(ones_f[:], 1.0), writes=[ones_f])
    P.op("pool", lambda e: e.memset(ones_b[:], 1.0), writes=[ones_b])

    bfm = P.sb("bfm", [128, 48], F32)
    P.dma("sp", bfm[:], b_ada_fm[:, :], dst=bfm)
    nbfo = P.sb("nbfo", [8, 1], F32)
    P.dma("sp", nbfo[:], b_forget.rearrange("o h -> h o"), dst=nbfo)
    P.op("dve", lambda e: e.tensor_scalar(out=nbfo[:], in0=nbfo[:], scalar1=-1.0, scalar2=None, op0=ALU.mult),
         reads=[nbfo], writes=[nbfo])
    bfo_bc = P.sb("bfo_bc", [128, 8], F32)
    P.dma("sp", bfo_bc[:], b_forget[0:1, :].partition_broadcast(128), dst=bfo_bc)

    cT_sb = P.sb("cT_sb", [128, 8, 3], F32)
    P.dma("sp", cT_sb[:].rearrange("p k j -> p (k j)"), cT[:, :], dst=cT_sb)
    sT = P.sb("sT", [128, 8, 3], F32)
    P.op("act", lambda e: e.activation(out=sT[:], in_=cT_sb[:], func=AF.Silu), reads=[cT_sb], writes=[sT])

    SSQ = P.sb("SSQ", [128, NOWN, 4], F32)
    P.op("pool", lambda e: e.memset(SSQ[:], 1.0), writes=[SSQ])
    P.push()
    fTo = P.sb("fTo", [8, NOWN, 128], F32)
    vsn_d = P.sb("vsn_d", [128, 4, 130], BF16)
    vsn_f = P.sb("vsn_f", [128, 8, 66], BF16)
    KTn = P.sb("KTn", [128, 8, 128], BF16)
    Gend = P.sb("Gend", [8, NB + 1], F32)
    sh1 = P.sb("sh1", [128, 8, 3], F32)
    sc1 = P.sb("sc1", [128, 8, 3], F32)
    rb_bc = P.sb("rb_bc", [128, 128], F32)
    zero_f = P.sb("zero_f", [128, 128], F32)
    TP = P.sb("TP", [128, 4, 128], F32)
    TDg = P.sb("TDg", [128, 4, 128], F32)
    Gt = [P.sb(f"Gt{i}", [128, 128], F32) for i in range(2)]
    dl = P.sb("dl", [128, 4], F32)
    P.push()
    win = P.sb("win", [128, 8, NCOL], BF16)
    P.push()
    wada = [P.sb(f"wada{i}", [128, 8, 512], F32) for i in range(2)]
    w_ada_v = w_ada.rearrange("(k p) n -> p k n", p=128)
    mps = banks[7]
    for g in range(4):
        wt = wada[g % 2]
        P.dma("sp", wt[:], w_ada_v[:, :, g * 512:(g + 1) * 512], dst=wt)
        for j in range(4):
            c = (g * 4 + j) * 3
            for k in range(8):
                if 'mod' in SKIP:
                    continue
                P.op("pe", lambda e, wt=wt, j=j, k=k, c=c: e.matmul(
                    mps[:, c:c + 3], lhsT=wt[:, k, j * 128:(j + 1) * 128], rhs=sT[:, k, :],
                    start=(k == 0), stop=(k == 7)), reads=[wt, sT], writes=[mps])
    P.op("dve", lambda e: e.tensor_tensor(out=sh1[:], in0=mps[:, 0:24].rearrange("p (k j) -> p k j", j=3),
                                          in1=bfm[:, 0:8].unsqueeze(2).to_broadcast([128, 8, 3]), op=ALU.add),
         reads=[mps, bfm], writes=[sh1])
    P.op("dve", lambda e: e.scalar_tensor_tensor(out=sc1[:], in0=mps[:, 24:48].rearrange("p (k j) -> p k j", j=3),
                                                 scalar=1.0, in1=bfm[:, 8:16].unsqueeze(2).to_broadcast([128, 8, 3]),
                                                 op0=ALU.add, op1=ALU.add),
         reads=[mps, bfm], writes=[sc1])

    stg32 = [P.sb(f"stg32_{i}", [128, 2048], F32) for i in range(2)]
    w_in_v = w_in.rearrange("(k p) n -> p k n", p=128)
    pieces = [(0, 2048), (2048, 4096), (4096, NCOL)]
    i = 0
    for k in range(8):
        for (c0, c1) in pieces:
            st = stg32[i % 2]
            P.dma("sp", st[:, 0:c1 - c0], w_in_v[:, k, c0:c1], dst=st)
            if i % 2 == 0:
                P.op("pool", lambda e, st=st, k=k, c0=c0, c1=c1: e.tensor_copy(out=win[:, k, c0:c1], in_=st[:, 0:c1 - c0]),
                     reads=[st], writes=[win], nowaw=True)
            else:
                P.op("act", lambda e, st=st, k=k, c0=c0, c1=c1: e.copy(out=win[:, k, c0:c1], in_=st[:, 0:c1 - c0]),
                     reads=[st], writes=[win], nowaw=True)
            i += 1

    def split3(src, hi, mid, lo, r_, npart, n, neg=False):
        if neg:
            P.op("dve", lambda e: e.tensor_scalar(out=r_[0:npart, 0:n], in0=src[0:npart, 0:n], scalar1=-1.0, scalar2=None, op0=ALU.mult),
                 reads=[src], writes=[r_])
            base = r_
        else:
            base = src
        P.op("dve", lambda e: e.tensor_copy(out=hi[0:npart, 0:n], in_=base[0:npart, 0:n]), reads=[base], writes=[hi])
        P.op("dve", lambda e: e.tensor_tensor(out=r_[0:npart, 0:n], in0=base[0:npart, 0:n], in1=hi[0:npart, 0:n], op=ALU.subtract),
             reads=[base, hi], writes=[r_])
        P.op("dve", lambda e: e.tensor_copy(out=mid[0:npart, 0:n], in_=r_[0:npart, 0:n]), reads=[r_], writes=[mid])
        P.op("dve", lambda e: e.tensor_tensor(out=r_[0:npart, 0:n], in0=r_[0:npart, 0:n], in1=mid[0:npart, 0:n], op=ALU.subtract),
             reads=[r_, mid], writes=[r_])
        P.op("dve", lambda e: e.tensor_copy(out=lo[0:npart, 0:n], in_=r_[0:npart, 0:n]), reads=[r_], writes=[lo])

    P.pop()
    P.dma("sp", rb_bc[:], rel_bias[0:1, :].partition_broadcast(128), dst=rb_bc)
    bk0, ths = t5_thresholds()
    P.op("pool", lambda e: e.memset(zero_f[:], 0.0), writes=[zero_f])
    P.op("dve", lambda e: e.memset(TP[:], 0.0), writes=[TP])
    P.op("dve", lambda e: e.memset(TDg[:], 0.0), writes=[TDg])
    gi_ = 0
    for (t, bb, ba) in ths:
        for which, T_, off, lo_, hi_ in (("p", TP, -128, -255, -1), ("d", TDg, 0, -127, 127)):
            if not (lo_ < t <= hi_):
                continue
            G_ = Gt[gi_ % 2]
            gi_ += 1
            P.op("pool", lambda e, G_=G_, off=off, t=t: e.affine_select(
                out=G_[:], in_=ones_f[:, 0:128], pattern=[[-1, 128]], compare_op=ALU.is_ge, fill=0.0,
                base=off - t, channel_multiplier=1), reads=[ones_f], writes=[G_])
            P.op("dve", lambda e, bb=bb, ba=ba: e.tensor_tensor(out=dl[:], in0=rb_bc[:, ba * 4:ba * 4 + 4], in1=rb_bc[:, bb * 4:bb * 4 + 4],
                                                              op=ALU.subtract), reads=[rb_bc], writes=[dl])
            for h in range(4):
                P.op("dve", lambda e, G_=G_, T_=T_, h=h: e.scalar_tensor_tensor(
                    out=T_[:, h, :], in0=G_[:], scalar=dl[:, h:h + 1], in1=T_[:, h, :], op0=ALU.mult, op1=ALU.add),
                    reads=[G_, dl, T_], writes=[T_])
    P.push()
    xto = [P.sb(f"xto{i}", [128, 8, 128], F32) for i in range(2)]
    hTo = [P.sb(f"hTo{i}", [128, 8, 128], BF16) for i in range(2)]
    stgF = [P.sb(f"stgF{i}", [128, 512], F32) for i in range(4)]
    stgL = [P.sb(f"stgL{i}", [128, 8], F32) for i in range(2)]
    stgE = [P.sb(f"stgE{i}", [128, 8], F32) for i in range(2)]
    QTs = [P.sb(f"QTs{i}", [128, 4, 128], BF16) for i in range(2)]
    xT_own_v = xT_own.rearrange("(k p) t -> p k t", p=128)
    P.op("pool", lambda e: e.memset(vsn_d[:], 1.0), writes=[vsn_d])
    P.op("pool", lambda e: e.memset(vsn_f[:], 1.0), writes=[vsn_f])
    bk = [0]

    def nbank(lo=0, hi=8):
        b = banks[lo + bk[0] % (hi - lo)]
        bk[0] += 1
        return b

    evi = [0]

    def evac(out_ap, in_ap, reads, writes, scale=None, eng=None):
        en = eng or ("act" if evi[0] % 2 == 0 else "dve")
        evi[0] += 1
        if en == "act":
            if scale is None:
                return P.op("act", lambda e: e.copy(out=out_ap, in_=in_ap), reads=reads, writes=writes, nowaw=True)
            return P.op("act", lambda e: e.mul(out=out_ap, in_=in_ap, mul=scale), reads=reads, writes=writes, nowaw=True)
        if scale is None:
            return P.op("dve", lambda e: e.tensor_copy(out=out_ap, in_=in_ap), reads=reads, writes=writes, nowaw=True)
        return P.op("dve", lambda e: e.tensor_scalar(out=out_ap, in0=in_ap, scalar1=scale, scalar2=None, op0=ALU.mult),
                    reads=reads, writes=writes, nowaw=True)

    sF = [0]
    if 1 in phases:
        for b in range(0 if 'tm' in SKIP else NOWN):
            xt = xto[b % 2]
            hT = hTo[b % 2]
            if b == 0:
                P.dma("sp", xt[:], xT_own_v[:, :, 0:128], dst=xt)
            if b + 1 < NOWN:
                P.dma("sp", xto[(b + 1) % 2][:], xT_own_v[:, :, (b + 1) * 128:(b + 2) * 128], dst=xto[(b + 1) % 2])
            def modulate_blk(bb):
                xt_, hT_ = xto[bb % 2], hTo[bb % 2]
                for k in range(8):
                    segs = [(0, 128, 0)] if bb < NR else [(0, 16, 1), (16, 128, 2)]
                    for (a0, a1, j) in segs:
                        P.op("dve", lambda e, xt_=xt_, hT_=hT_, k=k, a0=a0, a1=a1, j=j: e.tensor_scalar(
                            out=hT_[:, k, a0:a1], in0=xt_[:, k, a0:a1], scalar1=sc1[:, k, j:j + 1], scalar2=sh1[:, k, j:j + 1],
                            op0=ALU.mult, op1=ALU.add), reads=[xt_, sc1, sh1], writes=[hT_], nowaw=True)

            if b == 0:
                modulate_blk(0)
            tm = [("ka", C_KA, kd_own), ("va", C_VA, vd_own), ("kb", C_KB, kf_own), ("vb", C_VB, vf_own),
                  ("ga0", C_GA, None), ("ga1", C_GA + 512, None), ("gb0", C_GB, None), ("gb1", C_GB + 512, None)]
            for gi, (nm, c0, dest) in enumerate(tm):
                if 'gates' in SKIP and dest is None:
                    continue
                bank = nbank()
                for k in range(8):
                    P.op("pe", lambda e, bank=bank, hT=hT, k=k, c0=c0: e.matmul(
                        bank[:, :], lhsT=hT[:, k, :], rhs=win[:, k, c0:c0 + 512], start=(k == 0), stop=(k == 7)),
                        reads=[hT, win], writes=[bank])
                st = stgF[sF[0] % 4]
                sF[0] += 1
                evac(st[:], bank[:, :], [bank], [st])
                if dest is not None:
                    out_ops.append(P.dma("pool", dest[b * 128:(b + 1) * 128, :], st[:], src=st))
                else:
                    P.dma("pool", D_GATES[b * 128:(b + 1) * 128, (gi - 4) * 512:(gi - 3) * 512], st[:], src=st, dst=D_GATES)
                if b == NR and nm == "va" and 'vsn' not in SKIP:
                    P.op("pool", lambda e, st=st: e.tensor_copy(out=vsn_d[:, :, 0:128], in_=st[:].rearrange("p (h d) -> p h d", d=128)),
                         reads=[st], writes=[vsn_d])
                if b == NR and nm == "vb" and 'vsn' not in SKIP:
                    P.op("pool", lambda e, st=st: e.tensor_copy(out=vsn_f[:, :, 0:64], in_=st[:].rearrange("p (h d) -> p h d", d=64)),
                         reads=[st], writes=[vsn_f])
            if b + 1 < NOWN:
                modulate_blk(b + 1)
            if 'f' in SKIP:
                continue
            bank = nbank()
            for k in range(8):
                P.op("pe", lambda e, bank=bank, hT=hT, k=k: e.matmul(
                    bank[:, 0:8], lhsT=hT[:, k, :], rhs=win[:, k, C_F:C_F + 8], start=(k == 0), stop=(k == 7)),
                    reads=[hT, win], writes=[bank])
            sl = stgL[b % 2]
            se = stgE[b % 2]
            P.op("dve", lambda e, bank=bank, se=se: e.tensor_tensor(out=se[:], in0=bank[:, 0:8], in1=bfo_bc[:], op=ALU.add),
                 reads=[bank, bfo_bc], writes=[se])
            P.op("act", lambda e, se=se: e.activation(out=se[:], in_=se[:], func=AF.Exp, scale=-1.0), reads=[se], writes=[se])
            P.op("act", lambda e, se=se: e.activation(out=se[:], in_=se[:], func=AF.Ln, bias=1.0), reads=[se], writes=[se])
            P.op("dve", lambda e, se=se, sl=sl: e.tensor_scalar(out=sl[:], in0=se[:], scalar1=-1.0, scalar2=None, op0=ALU.mult),
                 reads=[se], writes=[sl])
            out_ops.append(P.dma("sp", lf_own[b * 128:(b + 1) * 128, :], sl[:], src=sl))
            if 'ffm' in SKIP:
                continue
            bank = nbank()
            for k in range(8):
                P.op("pe", lambda e, bank=bank, hT=hT, k=k: e.matmul(
                    bank[0:8, 0:128], lhsT=win[:, k, C_F:C_F + 8], rhs=hT[:, k, :], start=(k == 0), stop=(k == 7)),
                    reads=[hT, win], writes=[bank])
            P.op("act", lambda e, bank=bank, b=b: e.activation(out=fTo[:, b, :], in_=bank[0:8, 0:128], func=AF.Exp, scale=-1.0, bias=nbfo[:]),
                 reads=[bank, nbfo], writes=[fTo], nowaw=True)
            P.op("act", lambda e, b=b: e.activation(out=fTo[:, b, :], in_=fTo[:, b, :], func=AF.Ln, bias=1.0),
                 reads=[fTo], writes=[fTo])
            if 'q' in SKIP:
                continue
            for half, cbase in ((0, C_QA), (1, C_QB)):
                bank = nbank()
                for cc in range(4):
                    for k in range(8):
                        P.op("pe", lambda e, bank=bank, hT=hT, k=k, cc=cc, cbase=cbase: e.matmul(
                            bank[:, cc * 128:(cc + 1) * 128], lhsT=win[:, k, cbase + cc * 128:cbase + (cc + 1) * 128],
                            rhs=hT[:, k, :], start=(k == 0 and cc == 0), stop=(k == 7), skip_group_check=True),
                            reads=[hT, win], writes=[bank])
                qs = QTs[half]
                evac(qs[:].rearrange("p c t -> p (c t)"), bank[:, :], [bank], [qs], scale=0.125, eng="dve")
                for cc in range(4):
                    for s in range(2):
                        u = half * 8 + cc * 2 + s
                        P.dma("pool", D_QT[u, 0:64, b * 128:(b + 1) * 128], qs[64 * s:64 * s + 64, cc, :], src=qs, dst=D_QT)

            if b == NR:
                for half, cbase in ((0, C_KA), (1, C_KB)):
                    bank = nbank()
                    for cc in range(4):
                        for k in range(8):
                            P.op("pe", lambda e, bank=bank, hT=hT, k=k, cc=cc, cbase=cbase: e.matmul(
                                bank[:, cc * 128:(cc + 1) * 128], lhsT=win[:, k, cbase + cc * 128:cbase + (cc + 1) * 128],
                                rhs=hT[:, k, :], start=(k == 0 and cc == 0), stop=(k == 7), skip_group_check=True),
                                reads=[hT, win], writes=[bank])
                    evac(KTn[:, half * 4:half * 4 + 4, :].rearrange("p c t -> p (c t)"), bank[:, :], [bank], [KTn])

    P.pop()
    P.push()
    if 1 in phases:
        xt4s = [P.sb(f"xt4_{i}", [128, 8, 512], F32) for i in range(2)]
        hT4s = [P.sb(f"hT4_{i}", [128, 8, 512], BF16) for i in range(2)]
        ktss = [P.sb(f"kts{i}", [128, 512], BF16) for i in range(4)]
        VDs = [P.sb(f"VDs{i}", [128, 4, 4, 130], BF16) for i in range(2)]
        VFs = [P.sb(f"VFs{i}", [128, 8, 4, 66], BF16) for i in range(2)]
        for t in VDs + VFs:
            P.op("pool", lambda e, t=t: e.memset(t[:], 1.0), writes=[t])
        P.fence()
        spT = [P.sb(f"spT{i}", [8, 512], F32) for i in range(2)]
        Gc = [P.sb(f"Gc{i}", [8, 512], F32) for i in range(2)]
        P.op("dve", lambda e: e.memset(Gend[:], 0.0), writes=[Gend])
        gsp = [[P.sb(f"gsp{i}_{j}", [8, 512], BF16) for j in range(3)] for i in range(2)]
        gr = [P.sb(f"gr{i}", [8, 512], F32) for i in range(2)]
        xT_all_v = xT_all.rearrange("(k p) t -> p k t", p=128)
        kti = [0]
        for g in range(0 if 'b1' in SKIP else NG):
            xt = xt4s[g % 2]
            hT = hT4s[g % 2]

            def modulate_grp(gg):
                xt_, hT_ = xt4s[gg % 2], hT4s[gg % 2]
                for k in range(8):
                    if k % 2 == 0:
                        P.op("dve", lambda e, xt_=xt_, hT_=hT_, k=k: e.tensor_scalar(
                            out=hT_[:, k, :], in0=xt_[:, k, :], scalar1=sc1[:, k, 0:1], scalar2=sh1[:, k, 0:1],
                            op0=ALU.mult, op1=ALU.add), reads=[xt_, sc1, sh1], writes=[hT_], nowaw=True)
                    else:
                        P.op("act", lambda e, xt_=xt_, hT_=hT_, k=k: e.activation(
                            out=hT_[:, k, :], in_=xt_[:, k, :], func=AF.Identity, scale=sc1[:, k, 0:1], bias=sh1[:, k, 0:1]),
                            reads=[xt_, sc1, sh1], writes=[hT_], nowaw=True)

            def load_grp(gg):
                if gg < NG:
                    P.dma("sp", xt4s[gg % 2][:], xT_all_v[:, :, gg * 512:(gg + 1) * 512], dst=xt4s[gg % 2])

            if g == 0:
                load_grp(0)
                load_grp(1)
                modulate_grp(0)
            if g + 1 < NG:
                modulate_grp(g + 1)
            load_grp(g + 2)
            for cc in range(8):
                cbase = (C_KA + cc * 128) if cc < 4 else (C_KB + (cc - 4) * 128)
                bank = nbank()
                for k in range(8):
                    P.op("pe", lambda e, bank=bank, hT=hT, k=k, cbase=cbase: e.matmul(
                        bank[:, :], lhsT=win[:, k, cbase:cbase + 128], rhs=hT[:, k, :], start=(k == 0), stop=(k == 7)),
                        reads=[hT, win], writes=[bank])
                kt = ktss[kti[0] % 4]
                kti[0] += 1
                evac(kt[:], bank[:, :], [bank], [kt])
                for s_ in range(2):
                    u = cc * 2 + s_
                    P.dma("pool", D_KT[u, 0:64, g * 512:(g + 1) * 512], kt[64 * s_:64 * s_ + 64, :], src=kt, dst=D_KT)
            vd = VDs[g % 2]
            vf = VFs[g % 2]
            for blk in range(4):
                for (cbase, vt, nh, dh) in ((C_VA, vd, 4, 128), (C_VB, vf, 8, 64)):
                    bank = nbank()
                    for k in range(8):
                        P.op("pe", lambda e, bank=bank, hT=hT, k=k, cbase=cbase, blk=blk: e.matmul(
                            bank[:, :], lhsT=hT[:, k, blk * 128:(blk + 1) * 128], rhs=win[:, k, cbase:cbase + 512],
                            start=(k == 0), stop=(k == 7)), reads=[hT, win], writes=[bank])
                    evac(vt[:, :, blk, 0:dh], bank[:, :].rearrange("p (h d) -> p h d", d=dh), [bank], [vt])
            for h in range(4):
                P.dma("act", D_VD[h, g], vd[:, h, :, :].rearrange("p b d -> p (b d)"), src=vd, dst=D_VD)
            for h in range(8):
                P.dma("act", D_VF[h, g], vf[:, h, :, :].rearrange("p b d -> p (b d)"), src=vf, dst=D_VF)
            bank = nbank()
            for k in range(8):
                P.op("pe", lambda e, bank=bank, hT=hT, k=k: e.matmul(
                    bank[0:8, :], lhsT=win[:, k, C_F:C_F + 8], rhs=hT[:, k, :], start=(k == 0), stop=(k == 7)),
                    reads=[hT, win], writes=[bank])
            sp_ = spT[g % 2]
            P.op("act", lambda e, bank=bank, sp_=sp_: e.activation(out=sp_[:], in_=bank[0:8, :], func=AF.Exp, scale=-1.0, bias=nbfo[:]),
                 reads=[bank, nbfo], writes=[sp_])
            P.op("act", lambda e, sp_=sp_: e.activation(out=sp_[:], in_=sp_[:], func=AF.Ln, bias=1.0), reads=[sp_], writes=[sp_])
            gc = Gc[g % 2]
            gprev = Gc[(g - 1) % 2]
            if g == 0:
                P.op("dve", lambda e, gc=gc, sp_=sp_: e.tensor_tensor_scan(out=gc[:], data0=ones_f[0:8, :], data1=sp_[:], initial=0.0,
                                                                             op0=ALU.mult, op1=ALU.add), reads=[sp_, ones_f], writes=[gc])
            else:
                P.op("dve", lambda e, gc=gc, sp_=sp_, gprev=gprev: e.tensor_tensor_scan(
                    out=gc[:], data0=ones_f[0:8, :], data1=sp_[:], initial=gprev[:, 511:512], op0=ALU.mult, op1=ALU.add),
                    reads=[sp_, ones_f, gprev], writes=[gc])
            P.op("dve", lambda e, gc=gc, g=g: e.tensor_copy(out=Gend[:, 4 * g + 1:4 * g + 5],
                                                           in_=gc[:].rearrange("p (b t) -> p b t", t=128)[:, :, 127]),
                 reads=[gc], writes=[Gend], nowaw=True)
            hi, mid, lo = gsp[g % 2]
            r_ = gr[g % 2]
            split3(gc, hi, mid, lo, r_, 8, 512)
            for i_, tl in enumerate((hi, mid, lo)):
                P.dma("sp", D_KT[8:16, 67 + i_, g * 512:(g + 1) * 512], tl[:], src=tl, dst=D_KT)
    P.pop()
    P.pop()

    P.push()
    oa_bf = P.sb("oa_bf", [128, NOWN, 512], BF16)
    ob_bf = P.sb("ob_bf", [128, NOWN, 512], BF16)
    P.op("pool", lambda e: e.memset(oa_bf[:], 0.0), writes=[oa_bf])
    P.op("pool", lambda e: e.memset(ob_bf[:], 0.0), writes=[ob_bf])
    P.fence()
    if 2 in phases:
        P.push()
        TP_hi = P.sb("TP_hi", [128, 4, 128], BF16); TP_lo = P.sb("TP_lo", [128, 4, 128], BF16)
        TD_hi = P.sb("TD_hi", [128, 4, 128], BF16); TD_lo = P.sb("TD_lo", [128, 4, 128], BF16)
        TC_b = P.sb("TC_b", [128, 128], BF16)
        slD_hi = P.sb("slD_hi", [128, 9, 4, 128], BF16)
        slD_lo = P.sb("slD_lo", [128, 9, 4, 128], BF16)
        slF_b = P.sb("slF_b", [128, 9, 128], BF16)
        nlam = P.sb("nlam", [128, 1], F32)
        gcb = [P.sb(f"gcb{j}", [8, 2, PAST + 16], BF16) for j in range(3)]
        P.push()
        sel_bc = P.sb("sel_bc", [128, 27], F32)
        P.dma("sp", sel_bc[:], sel_in[0:1, :].partition_broadcast(128), dst=sel_bc)
        lam_bc = P.sb("lam_bc", [128, 256], F32)
        P.dma("sp", lam_bc[:], lam_in[0:1, :].partition_broadcast(128), dst=lam_bc)
        g_bc = P.sb("g_bc", [128, 128], F32)
        P.dma("sp", g_bc[:], subln_g[0:1, :].partition_broadcast(128), dst=g_bc)
        P.op("dve", lambda e: e.tensor_scalar(out=g_bc[:], in0=g_bc[:], scalar1=(1.0 - LAM_INIT), scalar2=None, op0=ALU.mult),
             reads=[g_bc], writes=[g_bc])
        lt = P.sb("lt", [128, 128], F32)
        lsum = P.sb("lsum", [128, 2], F32)
        for i_ in range(2):
            P.op("dve", lambda e, i_=i_: e.tensor_tensor(out=lt[:, i_ * 64:(i_ + 1) * 64], in0=lam_bc[:, i_ * 128:i_ * 128 + 64],
                                                         in1=lam_bc[:, i_ * 128 + 64:i_ * 128 + 128], op=ALU.mult),
                 reads=[lam_bc], writes=[lt])
        P.op("dve", lambda e: e.tensor_reduce(out=lsum[:], in_=lt[:].rearrange("p (a d) -> p a d", d=64), axis=AX.X, op=ALU.add),
             reads=[lt], writes=[lsum])
        P.op("act", lambda e: e.activation(out=lsum[:], in_=lsum[:], func=AF.Exp), reads=[lsum], writes=[lsum])
        P.op("dve", lambda e: e.tensor_tensor(out=nlam[:], in0=lsum[:, 1:2], in1=lsum[:, 0:1], op=ALU.subtract), reads=[lsum], writes=[nlam])
        P.op("dve", lambda e: e.tensor_scalar(out=nlam[:], in0=nlam[:], scalar1=-LAM_INIT, scalar2=None, op0=ALU.add), reads=[nlam], writes=[nlam])

        def hilo(src, hi, lo, tmp, shape_ap=lambda t: t[:]):
            P.op("dve", lambda e: e.tensor_copy(out=shape_ap(hi), in_=shape_ap(src)), reads=[src], writes=[hi])
            P.op("dve", lambda e: e.tensor_tensor(out=shape_ap(tmp), in0=shape_ap(src), in1=shape_ap(hi), op=ALU.subtract),
                 reads=[src, hi], writes=[tmp])
            P.op("dve", lambda e: e.tensor_copy(out=shape_ap(lo), in_=shape_ap(tmp)), reads=[tmp], writes=[lo])
        ttmp = P.sb("ttmp", [128, 4, 128], F32)
        hilo(TP, TP_hi, TP_lo, ttmp)
        hilo(TDg, TD_hi, TD_lo, ttmp)
        P.op("dve", lambda e: e.memset(TDg[64:128, :, 0:64], NEGM), reads=[TDg], writes=[TDg])
        TC = P.sb("TC", [128, 128], F32)
        P.op("pool", lambda e: e.affine_select(out=TC[:], in_=zero_f[:], pattern=[[1, 128]], compare_op=ALU.is_ge, fill=NEGM,
                                                base=0, channel_multiplier=-1), reads=[zero_f], writes=[TC])
        P.op("dve", lambda e: e.tensor_copy(out=TC_b[:], in_=TC[:]), reads=[TC], writes=[TC_b])
        slD = P.sb("slD", [128, 9, 4, 128], F32)
        slF = P.sb("slF", [128, 9, 128], F32)
        sm = P.sb("sm", [128, 9], F32)
        P.op("dve", lambda e: e.tensor_scalar(out=sm[:], in0=sel_bc[:].rearrange("p (s t) -> p s t", t=3)[:, :, 2], scalar1=NEGM, scalar2=None,
                                              op0=ALU.mult), reads=[sel_bc], writes=[sm])
        for s_ in range(9):
            P.op("dve", lambda e, s_=s_: e.tensor_scalar(out=slD[:, s_, :, :], in0=TP[:], scalar1=sel_bc[:, 3 * s_:3 * s_ + 1],
                                                         scalar2=sm[:, s_:s_ + 1], op0=ALU.mult, op1=ALU.add),
                 reads=[TP, sel_bc, sm], writes=[slD], nowaw=True)
            P.op("dve", lambda e, s_=s_: e.scalar_tensor_tensor(out=slD[:, s_, :, :], in0=TDg[:], scalar=sel_bc[:, 3 * s_ + 1:3 * s_ + 2],
                                                                in1=slD[:, s_, :, :], op0=ALU.mult, op1=ALU.add),
                 reads=[TDg, sel_bc, slD], writes=[slD])
            P.op("dve", lambda e, s_=s_: e.tensor_scalar(out=slF[:, s_, :], in0=TC[:], scalar1=sel_bc[:, 3 * s_ + 1:3 * s_ + 2],
                                                         scalar2=sm[:, s_:s_ + 1], op0=ALU.mult, op1=ALU.add),
                 reads=[TC, sel_bc, sm], writes=[slF], nowaw=True)
        sltmp = P.sb("sltmp", [128, 9, 4, 128], F32)
        hilo(slD, slD_hi, slD_lo, sltmp)
        P.op("dve", lambda e: e.tensor_copy(out=slF_b[:], in_=slF[:]), reads=[slF], writes=[slF_b])

        P.pop()
        Gsel = P.sb("Gsel", [8, NOWN], F32)
        onesQ = P.sb("onesQ", [8, TOWN], BF16)
        P.op("pool", lambda e: e.memset(onesQ[:], 1.0), writes=[onesQ])
        P.push()
        selB_bc = P.sb("selB_bc", [8, NR, NB], F32)
        P.dma("sp", selB_bc[:].rearrange("p r j -> p (r j)"), selB[0:1, :].partition_broadcast(8), dst=selB_bc)
        gprod = P.sb("gprod", [8, NR, NB], F32)
        P.op("dve", lambda e: e.tensor_tensor(out=gprod[:], in0=selB_bc[:], in1=Gend[:, 0:NB].unsqueeze(1).to_broadcast([8, NR, NB]),
                                              op=ALU.mult), reads=[selB_bc, Gend], writes=[gprod])
        P.op("dve", lambda e: e.memset(Gsel[:], 0.0), writes=[Gsel])
        P.op("dve", lambda e: e.tensor_reduce(out=Gsel[:, 0:NR], in_=gprod[:], axis=AX.X, op=ALU.add), reads=[gprod, Gsel], writes=[Gsel])
        P.pop()
        P.push()
        Gq = P.sb("Gq", [8, NOWN, 128], F32)
        W_ = NOWN * 128
        qh = [P.sb(f"qh{j}", [8, W_], BF16) for j in range(3)]
        qr = P.sb("qr", [8, W_], F32)
        for b_ in range(NR):
            P.op("dve", lambda e, b_=b_: e.tensor_tensor_scan(out=Gq[:, b_, :], data0=ones_f[0:8, 0:128], data1=fTo[:, b_, :],
                                                             initial=Gsel[:, b_:b_ + 1], op0=ALU.mult, op1=ALU.add),
                 reads=[fTo, ones_f, Gsel], writes=[Gq], nowaw=True)
        P.op("dve", lambda e: e.memset(Gq[:, NR, :], 0.0), writes=[Gq], nowaw=True)
        P.push()
        clf = P.sb("clf", [8, PAST], F32)
        Gcs = P.sb("Gcs", [8, PAST + 16], F32)
        gcr = P.sb("gcr", [8, PAST + 16], F32)
        gtmp = [P.sb(f"gtmp{j}", [8, PAST + 16], BF16) for j in range(3)]
        for sq in range(2):
            P.dma("sp", clf[:], clfT[sq], dst=clf)
            for cch in range(PAST // 512):
                init = 0.0 if cch == 0 else Gcs[:, cch * 512 - 1:cch * 512]
                P.op("dve", lambda e, cch=cch, init=init: e.tensor_tensor_scan(
                    out=Gcs[:, cch * 512:(cch + 1) * 512], data0=ones_f[0:8, 0:512], data1=clf[:, cch * 512:(cch + 1) * 512],
                    initial=init, op0=ALU.mult, op1=ALU.subtract), reads=[clf, ones_f, Gcs], writes=[Gcs])
            P.op("dve", lambda e, sq=sq: e.tensor_tensor_scan(out=Gq[:, NR, sq * 16:(sq + 1) * 16], data0=ones_f[0:8, 0:16],
                                                             data1=fTo[:, NR, sq * 16:(sq + 1) * 16], initial=Gcs[:, PAST - 1:PAST],
                                                             op0=ALU.mult, op1=ALU.add), reads=[fTo, ones_f, Gcs, Gq], writes=[Gq])
            P.op("dve", lambda e, sq=sq: e.tensor_copy(out=Gcs[:, PAST:PAST + 16], in_=Gq[:, NR, sq * 16:(sq + 1) * 16]),
                 reads=[Gq], writes=[Gcs])
            split3(Gcs, gtmp[0], gtmp[1], gtmp[2], gcr, 8, PAST + 16)
            for j in range(3):
                P.op("pool", lambda e, j=j, sq=sq: e.tensor_copy(out=gcb[j][:, sq, :], in_=gtmp[j][:]), reads=[gtmp[j]], writes=[gcb[j]])
        P.pop()
        Gq2 = Buf(P, "Gq2", Gq[:].rearrange("p b t -> p (b t)"))
        Gq2.w = Gq.w
        split3(Gq2, qh[0], qh[1], qh[2], qr, 8, W_, neg=True)
        for j in range(3):
            P.dma("sp", D_QT[8:16, 64 + j, :], qh[j][:], src=qh[j], dst=D_QT)
        gcb2 = gcb
        P.pop()

        NCH = 8
        ktile = [P.sb(f"ktile{i}", [128, NCH * 128], BF16) for i in range(3)]
        vtile = [P.sb(f"vtile{i}", [128, NCH * 130], BF16) for i in range(3)]
        qtileD = [P.sb(f"qtileD{i}", [128, TOWN], BF16) for i in range(2)]
        qtileF = [P.sb(f"qtileF{i}", [128, TOWN], BF16) for i in range(2)]
        qtile = qtileF
        PT = [P.sb(f"PT{i}", [128, 1024], BF16) for i in range(2)]
        O1 = P.sb("O1", [128, NOWN, 128], F32)
        for kt in ktile:
            P.op("pool", lambda e, kt=kt: e.memset(kt[:], 0.0), writes=[kt])
            P.op("pool", lambda e, kt=kt: e.memset(kt[64:67, :], 1.0), writes=[kt])
        for qt in qtileD + qtileF:
            P.op("pool", lambda e, qt=qt: e.memset(qt[:], 0.0), writes=[qt])
        P.fence()
        for qt in qtileF:
            P.dma("sp", qt[67:70, :], onesQ[0:3, 0:TOWN], src=onesQ, dst=qt)
        P.fence()
        SB2 = [Buf(P, f"SB2_{i}", psall[:, i * 1024:(i + 1) * 1024]) for i in range(2)]
        ACC = Buf(P, "ACC", psall[:, 2048:3584])
        _ACCbank = [Buf(P, f"ACCbank{i}", None) for i in range(3)]
        ACCb = [_ACCbank[a // 3] for a in range(8)]
        B7 = Buf(P, "B7", psall[:, 3584:4096])

        def acc_col(a, W):
            return (a // 3) * 512 + (a % 3) * W

        Ubuf = P.sb("Ubuf", [128, NOWN, 130], F32)
        rsall = P.sb("rsall", [128, NOWN], F32)
        P.op("pool", lambda e: e.memset(Ubuf[:], 1.0), writes=[Ubuf])
        P.fence()

        def normalize(u, blk, acc_ap, npart, AB):
            Wc = 130 if u < 8 else 66
            P.op("dve", lambda e: e.tensor_copy(out=Ubuf[0:npart, blk, 0:Wc], in_=acc_ap[:, 0:Wc]), reads=[AB], writes=[Ubuf], nowaw=True)

        def unit_finish(u):
            if u < 8:
                h, s_ = u // 2, u % 2
                P.op("dve", lambda e: e.reciprocal(out=rsall[:], in_=Ubuf[:, :, 128]), reads=[Ubuf], writes=[rsall])
                rb_ = lambda: rsall[:].unsqueeze(2).to_broadcast([128, NOWN, 128])
                if s_ == 0:
                    P.op("dve", lambda e: e.tensor_tensor(out=O1[:], in0=Ubuf[:, :, 0:128], in1=rb_(), op=ALU.mult),
                         reads=[Ubuf, rsall], writes=[O1])
                else:
                    U_ = lambda: Ubuf[:, :, 0:128]
                    P.op("dve", lambda e: e.tensor_tensor(out=U_(), in0=U_(), in1=rb_(), op=ALU.mult),
                         reads=[Ubuf, rsall], writes=[Ubuf])
                    P.op("dve", lambda e: e.scalar_tensor_tensor(out=U_(), in0=U_(), scalar=nlam[:, 0:1], in1=O1[:], op0=ALU.mult, op1=ALU.add),
                         reads=[Ubuf, nlam, O1], writes=[Ubuf])
                    P.op("dve", lambda e: e.tensor_copy(out=oa_bf[:, :, h * 128:(h + 1) * 128], in_=U_()), reads=[Ubuf], writes=[oa_bf], nowaw=True)
                    P.op("dve", lambda e: e.tensor_tensor(out=U_(), in0=U_(), in1=U_(), op=ALU.mult), reads=[Ubuf], writes=[Ubuf])
                    P.op("dve", lambda e: e.tensor_reduce(out=SSQ[:, :, h], in_=U_(), axis=AX.X, op=ALU.add), reads=[Ubuf], writes=[SSQ], nowaw=True)
            else:
                h = u - 8
                P.op("dve", lambda e: e.reciprocal(out=rsall[:], in_=Ubuf[:, :, 64]), reads=[Ubuf], writes=[rsall])
                P.op("dve", lambda e: e.tensor_tensor(out=ob_bf[:, :, h * 64:(h + 1) * 64], in0=Ubuf[:, :, 0:64],
                                                      in1=rsall[:].unsqueeze(2).to_broadcast([128, NOWN, 64]), op=ALU.mult),
                     reads=[Ubuf, rsall], writes=[ob_bf], nowaw=True)

        kS32 = [P.sb(f"kS32_{i}", [64, PAST], F32) for i in range(1)] * 2
        kS = [P.sb(f"kS{i}", [70, PAST + 16], BF16) for i in range(2)]
        vS32 = [P.sb(f"vS32_{i}", [128, NKB_S, 128], F32) for i in range(1)] * 2
        vSbD = P.sb("vSbD", [128, NKB_S * 130], BF16)
        vSbF = P.sb("vSbF", [128, NKB_S * 66], BF16)
        PTs = [P.sb(f"PTs{i}", [128, NKB_S + 1, 32], BF16) for i in range(2)]
        vsnD = [P.sb(f"vsnD{i}", [16, 4, 130], BF16) for i in range(2)]
        vsnF = [P.sb(f"vsnF{i}", [16, 8, 66], BF16) for i in range(2)]
        for i2 in range(2):
            P.op("pool", lambda e, i2=i2: e.memset(kS[i2][64:67, :], 1.0), writes=[kS[i2]])
            P.op("pool", lambda e, i2=i2: e.memset((vSbD, vSbF)[i2][:], 1.0), writes=[(vSbD, vSbF)[i2]])
            P.op("pool", lambda e, i2=i2: e.memset(PTs[i2][:], 0.0), writes=[PTs[i2]])
            P.dma("sp", vsnD[i2][:], vsn_d[i2 * 16:(i2 + 1) * 16, :, :], src=vsn_d, dst=vsnD[i2])
            P.dma("sp", vsnF[i2][:], vsn_f[i2 * 16:(i2 + 1) * 16, :, :], src=vsn_f, dst=vsnF[i2])
        P.fence()
        if 3 in phases:
            wc32 = [P.sb(f"wc32_{i}", [128, 512], F32) for i in range(1)] * 2
            wc16 = [P.sb(f"wc16_{i}", [128, 512], BF16) for i in range(1)] * 2
            wjobs = []
            for (src_, dstb, rows, cols) in ((w_a, D_WA, 512, 1024), (w_b, D_WB, 512, 1024), (w_o, D_WO, 1024, 1024),
                                             (w_f1, D_WF1, 1024, 5632), (w_f2, D_WF2, 2816, 1024)):
                for r_ in range(rows // 128):
                    for c0_ in range(0, cols, 512):
                        c1_ = min(cols, c0_ + 512)
                        wjobs.append((src_, dstb, r_, c0_, c1_))
            for i_, (src_, dstb, r_, c0_, c1_) in enumerate(wjobs):
                a32, a16 = wc32[i_ % 2], wc16[i_ % 2]
                n_ = c1_ - c0_
                P.dma("pool", a32[:, 0:n_], src_[r_ * 128:(r_ + 1) * 128, c0_:c1_], dst=a32)
                P.op("pool", lambda e, a32=a32, a16=a16, n_=n_: e.tensor_copy(out=a16[:, 0:n_], in_=a32[:, 0:n_]), reads=[a32], writes=[a16])
                P.dma("pool", dstb[r_ * 128:(r_ + 1) * 128, c0_:c1_], a16[:, 0:n_], src=a16, dst=dstb)
        passes = [list(range(p0, min(p0 + 8, NR))) for p0 in range(0, NR, 8)]
        ci = [0]
        for u in range(16):
            isd = u < 8
            K_ = 64 if isd else 70
            W = 130 if isd else 66
            hv = (u // 2) if isd else (u - 8)
            qt = (qtileD if isd else qtileF)[u % 2]
            P.dma("sp", qt[0:67 if not isd else 64, :], D_QT[u, 0:67 if not isd else 64, :], src=D_QT, dst=qt)
            if isd:
                ckT, crow, cv, dh = ckdT, u * 64, cvd, 128
                cc_, s__ = u // 2, u % 2
            else:
                ckT, crow, cv, dh = ckfT, (u - 8) * 64, cvf, 64
                cc_, s__ = 4 + (u - 8) // 2, (u - 8) % 2

            def samp_prep_k(u, sq, isd=isd, hv=hv, ckT=ckT, crow=crow, cc_=cc_, s__=s__):
                kst, ks = kS32[sq], kS[sq]
                P.dma("sp", kst[:], ckT[sq, crow:crow + 64, :], dst=kst)
                P.op("dve", lambda e, kst=kst, ks=ks: e.tensor_copy(out=ks[0:64, 0:PAST], in_=kst[:]), reads=[kst], writes=[ks], nowaw=True)
                P.dma("sp", ks[0:64, PAST:PAST + 16], KTn[64 * s__:64 * s__ + 64, cc_, sq * 16:(sq + 1) * 16], src=KTn, dst=ks)
                if not isd:
                    for j3 in range(3):
                        P.dma("sp", ks[67 + j3:68 + j3, :], gcb[j3][hv:hv + 1, sq, :], src=gcb2[j3], dst=ks)

            def samp_prep_v(u, sq, isd=isd, hv=hv, cv=cv, dh=dh, W=W):
                vst, vs = vS32[sq], (vSbD if isd else vSbF)
                P.dma("sp", vst[:, :, 0:dh], cv[sq, :, hv * dh:(hv + 1) * dh].rearrange("(b p) d -> p b d", p=128), dst=vst)
                P.op("dve", lambda e, vst=vst, vs=vs, dh=dh, W=W: e.tensor_copy(
                    out=vs[:, 0:NKB_S * W].rearrange("p (b w) -> p b w", w=W)[:, :, 0:dh], in_=vst[:, :, 0:dh]), reads=[vst], writes=[vs])

            for R in passes:
                r0, nr = R[0], len(R)
                nj = 8 * (r0 + nr)
                nchunks = (nj + NCH - 1) // NCH
                chunk_tiles = {}

                def load_chunk(ch, u=u, isd=isd, hv=hv, W=W):
                    if ch in chunk_tiles or ch >= nchunks:
                        return
                    kt = ktile[ci[0] % 3]
                    vt = vtile[ci[0] % 3]
                    ci[0] += 1
                    chunk_tiles[ch] = (kt, vt)
                    k0 = ch * NCH * 128
                    P.dma("sp", kt[0:64, :], D_KT[u, 0:64, k0:k0 + NCH * 128], src=D_KT, dst=kt)
                    if not isd:
                        P.dma("sp", kt[67:70, :], D_KT[u, 67:70, k0:k0 + NCH * 128], src=D_KT, dst=kt)
                    DV = D_VD if isd else D_VF
                    P.dma("sp", vt[:, 0:NCH * W].rearrange("p (g x) -> p g x", g=2),
                          DV[hv, 2 * ch:2 * ch + 2].rearrange("g p x -> p g x"), src=DV, dst=vt)

                def stage_qk(j):
                    ch, jl = j // NCH, j % NCH
                    load_chunk(ch)
                    kt, vt = chunk_tiles[ch]
                    a0 = max(j // 8, r0) - r0
                    sb2 = SB2[j % 2]
                    for half in range(2):
                        lo_ = max(a0, 4 * half)
                        hi_ = min(nr, 4 * half + 4)
                        if lo_ >= hi_:
                            continue
                        P.op("pe", lambda e, sb2=sb2, kt=kt, jl=jl, lo_=lo_, hi_=hi_, qt=qt, r0=r0: e.matmul(
                            sb2[:, lo_ * 128:hi_ * 128], lhsT=kt[:, jl * 128:(jl + 1) * 128],
                            rhs=qt[:, (r0 + lo_) * 128:(r0 + hi_) * 128], start=True, stop=True, skip_group_check=True),
                            reads=[kt, qt], writes=[sb2])
                    cands = []
                    rr = j // 8
                    if rr in R:
                        cands.append((rr, j - (8 * rr - 1)))
                    if (j + 1) % 8 == 0 and (j + 1) // 8 in R:
                        cands.append(((j + 1) // 8, 0))
                    for (rr, sl) in cands:
                        a = rr - r0
                        if isd:
                            for tl in (slD_hi, slD_lo):
                                P.op("pe", lambda e, sb2=sb2, a=a, tl=tl, sl=sl, hv=hv: e.matmul(
                                    sb2[:, a * 128:(a + 1) * 128], lhsT=ident[:], rhs=tl[:, sl, hv, :], start=False, stop=True,
                                    skip_group_check=True), reads=[ident, tl], writes=[sb2])
                        else:
                            P.op("pe", lambda e, sb2=sb2, a=a, sl=sl: e.matmul(
                                sb2[:, a * 128:(a + 1) * 128], lhsT=ident[:], rhs=slF_b[:, sl, :], start=False, stop=True,
                                skip_group_check=True), reads=[ident, slF_b], writes=[sb2])

                def stage_exp(j):
                    a0 = max(j // 8, r0) - r0
                    sb2, pt = SB2[j % 2], PT[j % 2]
                    P.op("act", lambda e, sb2=sb2, pt=pt, a0=a0, nr=nr: e.activation(
                        out=pt[:, a0 * 128:nr * 128], in_=sb2[:, a0 * 128:nr * 128], func=AF.Exp), reads=[sb2], writes=[pt])

                def stage_pv(j):
                    ch, jl = j // NCH, j % NCH
                    kt, vt = chunk_tiles[ch]
                    a0 = max(j // 8, r0) - r0
                    pt = PT[j % 2]
                    for a in range(a0, nr):
                        c = acc_col(a, W)
                        last = (j == 8 * (r0 + a) + 7)
                        P.op("pe", lambda e, pt=pt, vt=vt, a=a, c=c, jl=jl, j=j, last=last, W=W: e.matmul(
                            ACC[:, c:c + W], lhsT=pt[:, a * 128:(a + 1) * 128], rhs=vt[:, jl * W:(jl + 1) * W],
                            start=(j == 0 and a % 3 == 0), stop=last, skip_group_check=True), reads=[pt, vt], writes=[ACCb[a]])
                        if last:
                            normalize(u, r0 + a, ACC[:, c:c + W], 128, ACCb[a])

                load_chunk(0)
                load_chunk(1)
                stage_qk(0)
                for j in range(nj):
                    if R is passes[-1] and 'samp' not in SKIP:
                        if j == 0:
                            samp_prep_k(u, 0)
                            samp_prep_v(u, 0)
                        if j == nj // 2:
                            samp_prep_k(u, 1)
                    stage_exp(j)
                    if j + 1 < nj:
                        if (j + 1) % NCH == 0:
                            load_chunk((j + 1) // NCH + 1)
                        stage_qk(j + 1)
                    stage_pv(j)
            for sq in range(0 if 'samp' in SKIP else 2):
                i2 = sq
                kst, ks, vst, vs, pts = kS32[i2], kS[i2], vS32[i2], (vSbD if isd else vSbF), PTs[sq]
                if sq == 1:
                    samp_prep_v(u, 1)
                qcol = NR * 128 + sq * 16
                for blk in range(NKB_S):
                    P.op("pe", lambda e, ks=ks, qt=qt, blk=blk, K_=K_, qcol=qcol: e.matmul(
                        B7[:, blk * 16:(blk + 1) * 16], lhsT=ks[0:K_, blk * 128:(blk + 1) * 128], rhs=qt[0:K_, qcol:qcol + 16],
                        start=(blk == 0), stop=True, skip_group_check=True), reads=[ks, qt], writes=[B7])
                P.op("pe", lambda e, ks=ks, qt=qt, K_=K_, qcol=qcol: e.matmul(
                    B7[0:16, 256:272], lhsT=ks[0:K_, PAST:PAST + 16], rhs=qt[0:K_, qcol:qcol + 16],
                    start=False, stop=True, skip_group_check=True), reads=[ks, qt], writes=[B7])
                if isd:
                    for (tp, td) in ((TP_hi, TD_hi), (TP_lo, TD_lo)):
                        P.op("pe", lambda e, tp=tp, hv=hv: e.matmul(
                            B7[:, (NKB_S - 1) * 16:NKB_S * 16], lhsT=ident[:], rhs=tp[:, hv, 0:16], start=False, stop=True,
                            skip_group_check=True), reads=[ident, tp], writes=[B7])
                        P.op("pe", lambda e, td=td, hv=hv: e.matmul(
                            B7[0:16, 256:272], lhsT=ident[0:16, 0:16], rhs=td[0:16, hv, 0:16], start=False, stop=True,
                            skip_group_check=True), reads=[ident, td], writes=[B7])
                else:
                    P.op("pe", lambda e: e.matmul(B7[0:16, 256:272], lhsT=ident[0:16, 0:16], rhs=TC_b[0:16, 0:16], start=False, stop=True,
                                                  skip_group_check=True), reads=[ident, TC_b], writes=[B7])
                P.op("act", lambda e, pts=pts, sq=sq: e.activation(
                    out=pts[:, 0:NKB_S, sq * 16:(sq + 1) * 16], in_=B7[:, 0:NKB_S * 16].rearrange("p (b q) -> p b q", q=16), func=AF.Exp),
                    reads=[B7], writes=[pts], nowaw=True)
                P.op("act", lambda e, pts=pts, sq=sq: e.activation(
                    out=pts[0:16, NKB_S, sq * 16:(sq + 1) * 16], in_=B7[0:16, 256:272], func=AF.Exp), reads=[B7], writes=[pts], nowaw=True)
                vn = (vsnD if isd else vsnF)[sq]
                for blk in range(NKB_S):
                    P.op("pe", lambda e, pts=pts, vs=vs, blk=blk, W=W, sq=sq: e.matmul(
                        ACC[0:32, 0:W], lhsT=pts[:, blk, :], rhs=vs[:, blk * W:(blk + 1) * W], start=(sq == 0 and blk == 0), stop=False,
                        skip_group_check=True), reads=[pts, vs], writes=[ACCb[0]])
                P.op("pe", lambda e, pts=pts, vn=vn, W=W, sq=sq, hv=hv: e.matmul(
                    ACC[0:32, 0:W], lhsT=pts[0:16, NKB_S, :], rhs=vn[0:16, hv, 0:W], start=False, stop=(sq == 1),
                    skip_group_check=True), reads=[pts, vn], writes=[ACCb[0]])
            if 'samp' not in SKIP:
                normalize(u, NR, ACC[0:32, 0:W], 32, ACCb[0])
            unit_finish(u)
        if dbg:
            for (src_, dst_) in ((oa_bf, dbg_oa), (ob_bf, dbg_ob)):
                out_ops.append(P.dma("sp", dst_.rearrange("(b p) c -> p b c", p=128), src_[:], src=src_))
        P.dma("sp", D_OA.t.rearrange("(b p) c -> p b c", p=128), oa_bf[:], src=oa_bf, dst=D_OA)
        P.dma("sp", D_OB.t.rearrange("(b p) c -> p b c", p=128), ob_bf[:], src=ob_bf, dst=D_OB)
        P.pop()
    P.pop()
    P.pop()

    h2T = P.sb("h2T", [128, 8, TOWN], BF16)
    if 3 in phases:
        P.push()
        oabs = [P.sb(f"oab{i}", [128, 512], BF16) for i in range(2)]
        obbs = [P.sb(f"obb{i}", [128, 512], BF16) for i in range(2)]
        PS7b = Buf(P, "PS7b", psall[:, 3584:4096].bitcast(BF16))
        lnbc = P.sb("lnbc", [128, 4, 1024], F32)
        P.dma("sp", lnbc[:].rearrange("p a d -> p (a d)"), ln_in[0:1, :].partition_broadcast(128), dst=lnbc)
        g2_bc = P.sb("g2_bc", [128, 128], F32)
        P.dma("sp", g2_bc[:], subln_g[0:1, :].partition_broadcast(128), dst=g2_bc)
        P.op("dve", lambda e: e.tensor_scalar(out=g2_bc[:], in0=g2_bc[:], scalar1=(1.0 - LAM_INIT), scalar2=None, op0=ALU.mult),
             reads=[g2_bc], writes=[g2_bc])
        RSTD = P.sb("RSTD", [128, NOWN, 4], F32)
        epsb = P.sb("epsb", [128, 1], F32)
        P.op("pool", lambda e: e.memset(epsb[:], LN_EPS), writes=[epsb])
        P.op("act", lambda e: e.activation(out=RSTD[:], in_=SSQ[:], func=AF.Ln, scale=1.0 / 128.0, bias=epsb[:]), reads=[SSQ, epsb], writes=[RSTD])
        P.op("act", lambda e: e.activation(out=RSTD[:], in_=RSTD[:], func=AF.Exp, scale=-0.5), reads=[RSTD], writes=[RSTD])
        modB = [P.sb(f"modB{i}", [128, 4, 1024], F32) for i in range(2)]
        P.push()
        cTr = P.sb("cTr", [128, 8, 256], F32)
        P.dma("sp", cTr[:].rearrange("p k j -> p (k j)"), cT_rep[:, :], dst=cTr)
        sTr = P.sb("sTr", [128, 8, 256], F32)
        P.op("act", lambda e: e.activation(out=sTr[:], in_=cTr[:], func=AF.Silu), reads=[cTr], writes=[sTr])
        wad3 = [P.sb(f"wad3_{i}", [128, 8, 512], F32) for i in range(2)]
        bbc = [P.sb(f"bbc{i}", [128, 512], F32) for i in range(2)]
        for g in range(4, 12):
            wt = wad3[g % 2]
            bb = bbc[g % 2]
            P.dma("sp", wt[:], w_ada_v[:, :, g * 512:(g + 1) * 512], dst=wt)
            P.dma("sp", bb[:], b_ada[0:1, g * 512:(g + 1) * 512].partition_broadcast(128), dst=bb)
            ch, hf = (g - 4) // 2, (g - 4) % 2
            for st_ in range(2):
                bank = banks[(2 * g + st_) % 4]
                for k in range(8):
                    P.op("pe", lambda e, bank=bank, wt=wt, k=k, st_=st_: e.matmul(
                        bank[:, :], lhsT=sTr[:, k, st_ * 128:(st_ + 1) * 128], rhs=wt[:, k, :], start=(k == 0), stop=(k == 7)),
                        reads=[sTr, wt], writes=[bank])
                if ch == 2:
                    P.op("dve", lambda e, bank=bank, bb=bb, st_=st_, ch=ch, hf=hf: e.scalar_tensor_tensor(
                        out=modB[st_][:, ch, hf * 512:(hf + 1) * 512], in0=bank[:, :], scalar=1.0, in1=bb[:], op0=ALU.add, op1=ALU.add),
                        reads=[bank, bb], writes=[modB[st_]], nowaw=True)
                else:
                    P.op("dve", lambda e, bank=bank, bb=bb, st_=st_, ch=ch, hf=hf: e.tensor_tensor(
                        out=modB[st_][:, ch, hf * 512:(hf + 1) * 512], in0=bank[:, :], in1=bb[:], op=ALU.add),
                        reads=[bank, bb], writes=[modB[st_]], nowaw=True)
        P.pop()
        for i_ in range(2):
            P.dma("sp", D_G2[i_], modB[i_][:, 3, :], src=modB[i_], dst=D_G2)
        wa_sb = P.sb("wa_sb", [128, 4, 1024], BF16)
        wb_sb = P.sb("wb_sb", [128, 4, 1024], BF16)
        wo_sb = P.sb("wo_sb", [128, 8, 1024], BF16)
        P.dma("sp", wa_sb[:], D_WA.t.rearrange("(k p) n -> p k n", p=128), src=D_WA, dst=wa_sb)
        P.dma("sp", wb_sb[:], D_WB.t.rearrange("(k p) n -> p k n", p=128), src=D_WB, dst=wb_sb)
        P.dma("sp", wo_sb[:], D_WO.t.rearrange("(k p) n -> p k n", p=128), src=D_WO, dst=wo_sb)

        gat = [P.sb(f"gat{i}", [128, 2048], F32) for i in range(2)]
        xres = [P.sb(f"xres{i}", [128, 1024], F32) for i in range(2)]
        oan = P.sb("oan", [128, 512], BF16)
        oT = P.sb("oT", [128, 8, 128], BF16)
        t1 = P.sb("t1", [128, 1024], F32)
        t2 = P.sb("t2", [128, 1024], F32)
        ybf = P.sb("ybf", [128, 1024], BF16)
        yT = P.sb("yT", [128, 8, 128], BF16)
        x1s = [P.sb(f"x1s{i}", [128, 1024], F32) for i in range(2)]
        st1 = P.sb("st1", [128, 4], F32)

        def layer_norm(src, dst, gi, bi, npart=128):
            P.op("dve", lambda e: e.tensor_reduce(out=st1[:, 0:1], in_=src[:], axis=AX.X, op=ALU.add), reads=[src], writes=[st1])
            P.op("dve", lambda e: e.tensor_scalar(out=st1[:, 0:1], in0=st1[:, 0:1], scalar1=-1.0 / 1024.0, scalar2=None, op0=ALU.mult),
                 reads=[st1], writes=[st1])
            P.op("dve", lambda e: e.tensor_scalar(out=src[:], in0=src[:], scalar1=st1[:, 0:1], scalar2=None, op0=ALU.add),
                 reads=[src, st1], writes=[src])
            P.op("dve", lambda e: e.tensor_tensor(out=dst[:], in0=src[:], in1=src[:], op=ALU.mult), reads=[src], writes=[dst])
            P.op("dve", lambda e: e.tensor_reduce(out=st1[:, 1:2], in_=dst[:], axis=AX.X, op=ALU.add), reads=[dst, st1], writes=[st1])
            P.op("act", lambda e: e.activation(out=st1[:, 2:3], in_=st1[:, 1:2], func=AF.Ln, scale=1.0 / 1024.0, bias=epsb[:]),
                 reads=[st1, epsb], writes=[st1])
            P.op("act", lambda e: e.activation(out=st1[:, 3:4], in_=st1[:, 2:3], func=AF.Exp, scale=-0.5), reads=[st1], writes=[st1])
            P.op("dve", lambda e: e.scalar_tensor_tensor(out=dst[:], in0=src[:], scalar=st1[:, 3:4], in1=lnbc[:, gi, :], op0=ALU.mult, op1=ALU.mult),
                 reads=[src, st1, lnbc], writes=[dst])
            P.op("dve", lambda e: e.tensor_tensor(out=dst[:], in0=dst[:], in1=lnbc[:, bi, :], op=ALU.add), reads=[dst, lnbc], writes=[dst])

        def transposes(src_aps, dst, reads):
            n = len(src_aps)
            for i_, ap_ in enumerate(src_aps):
                P.op("pe", lambda e, i_=i_, ap_=ap_: e.transpose(out=PS7b[:, i_ * 128:(i_ + 1) * 128], in_=ap_, identity=ident[:]),
                     reads=reads + [ident], writes=[PS7b])
            evac(dst[:, 0:n, :].rearrange("p c t -> p (c t)"), PS7b[:, 0:n * 128], [PS7b], [dst])

        oan2 = [oan, P.sb("oan_b", [128, 512], BF16)]
        oT2 = [oT, P.sb("oT_b", [128, 8, 128], BF16)]
        ybf2 = [ybf, P.sb("ybf_b", [128, 1024], BF16)]
        yT2 = [yT, P.sb("yT_b", [128, 8, 128], BF16)]
        tb1 = P.sb("tb1", [128, 1024], F32)
        tb2 = P.sb("tb2", [128, 1024], F32)
        hb = P.sb("hb", [128, 1024], BF16)
        pa_b = [banks[0], banks[1]]
        pb_b = [banks[2], banks[3]]
        ym = [banks[4], banks[5]]

        def front(b):
            ga_ = gat[b % 2]
            xr = xres[b % 2]
            oab = oabs[b % 2]
            obb = obbs[b % 2]
            oan_, oT_, ybf_, yT_ = oan2[b % 2], oT2[b % 2], ybf2[b % 2], yT2[b % 2]
            P.dma("sp", ga_[:], D_GATES[b * 128:(b + 1) * 128, :], src=D_GATES, dst=ga_)
            P.dma("sp", xr[:], x_own[b * 128:(b + 1) * 128, :], dst=xr)
            P.dma("sp", oab[:], D_OA[b * 128:(b + 1) * 128, :], src=D_OA, dst=oab)
            P.dma("sp", obb[:], D_OB[b * 128:(b + 1) * 128, :], src=D_OB, dst=obb)
            P.op("act", lambda e: e.activation(out=ga_[:], in_=ga_[:], func=AF.Sigmoid), reads=[ga_], writes=[ga_])
            for h in range(4):
                P.op("dve", lambda e, h=h: e.scalar_tensor_tensor(
                    out=oan_[:, h * 128:(h + 1) * 128], in0=oab[:, h * 128:(h + 1) * 128], scalar=RSTD[:, b, h:h + 1], in1=g2_bc[:],
                    op0=ALU.mult, op1=ALU.mult), reads=[oab, RSTD, g2_bc], writes=[oan_], nowaw=(h > 0))
            transposes([oan_[:, h * 128:(h + 1) * 128] for h in range(4)] + [obb[:, h * 128:(h + 1) * 128] for h in range(4)], oT_, [oan_, obb])
            for hf in range(2):
                for k in range(4):
                    P.op("pe", lambda e, hf=hf, k=k: e.matmul(pa_b[hf][:, :], lhsT=oT_[:, k, :], rhs=wa_sb[:, k, hf * 512:(hf + 1) * 512],
                                                              start=(k == 0), stop=(k == 3)), reads=[oT_, wa_sb], writes=[pa_b[hf]])
                for k in range(4):
                    P.op("pe", lambda e, hf=hf, k=k: e.matmul(pb_b[hf][:, :], lhsT=oT_[:, 4 + k, :], rhs=wb_sb[:, k, hf * 512:(hf + 1) * 512],
                                                              start=(k == 0), stop=(k == 3)), reads=[oT_, wb_sb], writes=[pb_b[hf]])
            for hf in range(2):
                sl = slice(hf * 512, (hf + 1) * 512)
                P.op("dve", lambda e, hf=hf, sl=sl: e.tensor_tensor(out=t1[:, sl], in0=pa_b[hf][:, :], in1=ga_[:, sl], op=ALU.mult),
                     reads=[pa_b[hf], ga_], writes=[t1], nowaw=True)
                P.op("dve", lambda e, hf=hf, sl=sl: e.tensor_tensor(out=t2[:, sl], in0=pb_b[hf][:, :], in1=ga_[:, 1024 + hf * 512:1024 + (hf + 1) * 512],
                                                                   op=ALU.mult), reads=[pb_b[hf], ga_], writes=[t2], nowaw=True)
            P.op("dve", lambda e: e.tensor_tensor(out=ybf_[:], in0=t1[:], in1=t2[:], op=ALU.add), reads=[t1, t2], writes=[ybf_])
            transposes([ybf_[:, k * 128:(k + 1) * 128] for k in range(8)], yT_, [ybf_])

        def back(b):
            mB = modB[0] if b < NR else modB[1]
            xr = xres[b % 2]
            x1 = x1s[b % 2]
            yT_ = yT2[b % 2]
            for hf in range(2):
                for k in range(8):
                    P.op("pe", lambda e, hf=hf, k=k: e.matmul(ym[hf][:, :], lhsT=yT_[:, k, :], rhs=wo_sb[:, k, hf * 512:(hf + 1) * 512],
                                                              start=(k == 0), stop=(k == 7)), reads=[yT_, wo_sb], writes=[ym[hf]])
            for hf in range(2):
                sl = slice(hf * 512, (hf + 1) * 512)
                P.op("dve", lambda e, hf=hf, sl=sl: e.tensor_tensor(out=tb1[:, sl], in0=ym[hf][:, :], in1=mB[:, 0, sl], op=ALU.mult),
                     reads=[ym[hf], mB], writes=[tb1], nowaw=(hf > 0))
            P.op("dve", lambda e: e.scalar_tensor_tensor(out=tb1[:], in0=xr[:], scalar=ALPHA, in1=tb1[:], op0=ALU.mult, op1=ALU.add),
                 reads=[xr, tb1], writes=[tb1])
            layer_norm(tb1, x1, 0, 1)
            P.dma("sp", D_X1[b * 128:(b + 1) * 128, :], x1[:], src=x1, dst=D_X1)
            P.op("dve", lambda e: e.tensor_tensor(out=tb2[:], in0=x1[:], in1=mB[:, 2, :], op=ALU.mult), reads=[x1, mB], writes=[tb2])
            P.op("dve", lambda e: e.tensor_tensor(out=hb[:], in0=tb2[:], in1=mB[:, 1, :], op=ALU.add), reads=[tb2, mB], writes=[hb])
            for i_ in range(8):
                P.op("pe", lambda e, i_=i_: e.transpose(out=PS7b[:, i_ * 128:(i_ + 1) * 128], in_=hb[:, i_ * 128:(i_ + 1) * 128], identity=ident[:]),
                     reads=[hb, ident], writes=[PS7b])
            evac(h2T[:, :, b * 128:(b + 1) * 128], PS7b[:, :].rearrange("p (c t) -> p c t", t=128), [PS7b], [h2T])

        front(0)
        for b in range(NOWN):
            if b + 1 < NOWN:
                front(b + 1)
            back(b)
        P.pop()

    if 3 in phases:
        P.push()
        lnbc2 = P.sb("lnbc2", [128, 2, 1024], F32)
        P.dma("sp", lnbc2[:].rearrange("p a d -> p (a d)"), ln_in[0:1, 2048:4096].partition_broadcast(128), dst=lnbc2)
        epsb2 = P.sb("epsb2", [128, 1], F32)
        P.op("pool", lambda e: e.memset(epsb2[:], LN_EPS), writes=[epsb2])
        wf1 = P.sb("wf1", [128, 8, 5632], BF16)
        wf2 = P.sb("wf2", [128, 22, 1024], BF16)
        w1v = D_WF1.t.rearrange("(k p) n -> p k n", p=128)
        for k in range(8):
            P.dma("sp" if k % 2 == 0 else "act", wf1[:, k, :], w1v[:, k, :], src=D_WF1, dst=wf1)
        w2v = D_WF2.t.rearrange("(k p) n -> p k n", p=128)
        for k0_ in range(0, 22, 6):
            k1_ = min(22, k0_ + 6)
            P.dma("pool" if (k0_ // 6) % 2 == 0 else "sp", wf2[:, k0_:k1_, :], w2v[:, k0_:k1_, :], src=D_WF2, dst=wf2)
        g2t = [P.sb(f"g2t{i}", [128, 1024], F32) for i in range(2)]
        for i_ in range(2):
            P.dma("sp", g2t[i_][:], D_G2[i_], src=D_G2, dst=g2t[i_])
        eT = [P.sb(f"eT{i}", [128, 256], F32) for i in range(2)]
        aT = [P.sb(f"aT{i}", [128, 256], BF16) for i in range(2)]
        x1r = [P.sb(f"x1r{i}", [128, 1024], F32) for i in range(1)] * 2
        r2 = P.sb("r2", [128, 1024], F32)
        yo = [P.sb(f"yo{i}", [128, 1024], F32) for i in range(2)]
        st2 = P.sb("st2", [128, 4], F32)
        gb_i = [0]
        pairs = [list(range(p0, min(p0 + 2, NOWN))) for p0 in range(0, NOWN, 2)]
        for pr in pairs:
            nt = len(pr) * 128
            t0 = pr[0] * 128
            cbanks = {}

            def st_g(c, nt=nt, t0=t0):
                bank = banks[gb_i[0] % 4]
                gb_i[0] += 1
                cbanks[c] = bank
                for (off, col0) in ((0, c * 128), (256, 2816 + c * 128)):
                    for k in range(8):
                        P.op("pe", lambda e, bank=bank, off=off, col0=col0, k=k, nt=nt, t0=t0: e.matmul(
                            bank[:, off:off + nt], lhsT=wf1[:, k, col0:col0 + 128], rhs=h2T[:, k, t0:t0 + nt], start=(k == 0), stop=(k == 7),
                            skip_group_check=True), reads=[wf1, h2T], writes=[bank])

            def st_act(c, nt=nt):
                bank = cbanks[c]
                et = eT[c % 2]
                at = aT[c % 2]
                P.op("act", lambda e, bank=bank, et=et, nt=nt: e.activation(out=et[:, 0:nt], in_=bank[:, 0:nt], func=AF.Silu),
                     reads=[bank], writes=[et])
                P.op("dve", lambda e, bank=bank, et=et, at=at, nt=nt: e.tensor_tensor(out=at[:, 0:nt], in0=bank[:, 256:256 + nt], in1=et[:, 0:nt], op=ALU.mult),
                     reads=[bank, et], writes=[at])

            def st_out(c, pr=pr):
                at = aT[c % 2]
                for bi_, b in enumerate(pr):
                    for hf in range(2):
                        ab = banks[4 + 2 * bi_ + hf]
                        P.op("pe", lambda e, ab=ab, at=at, bi_=bi_, c=c, hf=hf: e.matmul(
                            ab[:, :], lhsT=at[:, bi_ * 128:(bi_ + 1) * 128], rhs=wf2[:, c, hf * 512:(hf + 1) * 512], start=(c == 0), stop=(c == 21)),
                            reads=[at, wf2], writes=[ab])

            st_g(0)
            for c in range(22):
                st_act(c)
                if c + 1 < 22:
                    st_g(c + 1)
                st_out(c)
            for bi_, b in enumerate(pr):
                xr = x1r[b % 2]
                y_ = yo[b % 2]
                gt = g2t[0] if b < NR else g2t[1]
                P.dma("sp", xr[:], D_X1[b * 128:(b + 1) * 128, :], src=D_X1, dst=xr)
                for hf in range(2):
                    sl = slice(hf * 512, (hf + 1) * 512)
                    ab = banks[4 + 2 * bi_ + hf]
                    P.op("dve", lambda e, ab=ab, sl=sl, gt=gt: e.tensor_tensor(out=r2[:, sl], in0=ab[:, :], in1=gt[:, sl], op=ALU.mult),
                         reads=[ab, gt], writes=[r2], nowaw=(hf > 0))
                P.op("dve", lambda e, xr=xr: e.scalar_tensor_tensor(out=r2[:], in0=xr[:], scalar=ALPHA, in1=r2[:], op0=ALU.mult, op1=ALU.add),
                     reads=[xr, r2], writes=[r2])
                P.op("dve", lambda e: e.tensor_reduce(out=st2[:, 0:1], in_=r2[:], axis=AX.X, op=ALU.add), reads=[r2], writes=[st2])
                P.op("dve", lambda e: e.tensor_scalar(out=st2[:, 0:1], in0=st2[:, 0:1], scalar1=-1.0 / 1024.0, scalar2=None, op0=ALU.mult),
                     reads=[st2], writes=[st2])
                P.op("dve", lambda e: e.tensor_scalar(out=r2[:], in0=r2[:], scalar1=st2[:, 0:1], scalar2=None, op0=ALU.add), reads=[r2, st2], writes=[r2])
                P.op("dve", lambda e, y_=y_: e.tensor_tensor(out=y_[:], in0=r2[:], in1=r2[:], op=ALU.mult), reads=[r2], writes=[y_])
                P.op("dve", lambda e, y_=y_: e.tensor_reduce(out=st2[:, 1:2], in_=y_[:], axis=AX.X, op=ALU.add), reads=[y_, st2], writes=[st2])
                P.op("act", lambda e: e.activation(out=st2[:, 2:3], in_=st2[:, 1:2], func=AF.Ln, scale=1.0 / 1024.0, bias=epsb2[:]),
                     reads=[st2, epsb2], writes=[st2])
                P.op("act", lambda e: e.activation(out=st2[:, 3:4], in_=st2[:, 2:3], func=AF.Exp, scale=-0.5), reads=[st2], writes=[st2])
                P.op("dve", lambda e, y_=y_: e.scalar_tensor_tensor(out=y_[:], in0=r2[:], scalar=st2[:, 3:4], in1=lnbc2[:, 0, :], op0=ALU.mult, op1=ALU.mult),
                     reads=[r2, st2, lnbc2], writes=[y_])
                P.op("dve", lambda e, y_=y_: e.tensor_tensor(out=y_[:], in0=y_[:], in1=lnbc2[:, 1, :], op=ALU.add), reads=[y_, lnbc2], writes=[y_])
                out_ops.append(P.dma("sp", y_own[b * 128:(b + 1) * 128, :], y_[:], src=y_))
        P.pop()
    return nc, P, es, out_ops, locals()


def finish(nc, P, es, out_ops):
    while P.scopes:
        P.pop()
    P.emit(out_ops)
    es.close()
    return nc


def host_inputs(inp, S=16384, PAST=2048):
    NB = S // 128
    NR = NB // 8
    f = lambda a: np.ascontiguousarray(np.asarray(a, dtype=np.float32))
    xp = f(inp["x_prompt"])[0, :S]
    xs = f(inp["x_sample"])
    xT_all = np.ascontiguousarray(xp.T)
    cp = f(inp["c_prompt"])[0]
    cs = f(inp["c_sample"])
    shared = {
        "xT_all": xT_all,
        "b_ada_fm": np.ascontiguousarray(f(inp["b_ada"])[0].reshape(48, 128).T),
        "b_ada": f(inp["b_ada"]),
        "w_ada": f(inp["w_ada"])[0], "w_in": f(inp["w_in"])[0],
        "b_forget": f(inp["b_forget"]),
        "lam": np.concatenate([f(inp[k])[0] for k in ("lambda_q1", "lambda_k1", "lambda_q2", "lambda_k2")])[None, :],
        "subln_g": f(inp["subln_g"]),
        "rel_bias": f(inp["rel_bias"]).reshape(1, 128),
        "w_a": f(inp["w_branch_a"])[0], "w_b": f(inp["w_branch_b"])[0], "w_o": f(inp["w_o"])[0],
        "ln": np.concatenate([f(inp[k])[0] for k in ("ln1_g", "ln1_b", "ln2_g", "ln2_b")])[None, :],
        "w_f1": f(inp["w_ffn_in"])[0], "w_f2": f(inp["w_ffn_out"])[0],
    }
    maps = []
    for c in range(8):
        blocks = [c + 8 * r for r in range(NR)]
        xo = np.zeros((NR * 128 + 128, D), np.float32)
        for r, bl in enumerate(blocks):
            xo[r * 128:(r + 1) * 128] = xp[bl * 128:(bl + 1) * 128]
        xo[NR * 128:NR * 128 + 16] = xs[2 * c]
        xo[NR * 128 + 16:NR * 128 + 32] = xs[2 * c + 1]
        cT = np.stack([cp, cs[2 * c], cs[2 * c + 1]], axis=1)
        cT = cT.reshape(8, 128, 3).transpose(1, 0, 2).reshape(128, 24)
        rep = np.zeros((D, 256), np.float32)
        rep[:, 0:128] = cp[:, None]
        rep[:, 128:144] = cs[2 * c][:, None]
        rep[:, 144:256] = cs[2 * c + 1][:, None]
        rep = rep.reshape(8, 128, 256).transpose(1, 0, 2).reshape(128, 8 * 256)
        sel = np.zeros((9, 3), np.float32)
        for slot in range(9):
            t = slot - 1
            if t == c - 1:
                sel[slot, 0] = 1
            elif t == c:
                sel[slot, 1] = 1
            elif t > c:
                sel[slot, 2] = 1
        selB = np.zeros((NR, NB), np.float32)
        for r, bl in enumerate(blocks):
            selB[r, bl] = 1
        m = dict(shared)
        m.update({
            "xT_own": np.ascontiguousarray(xo.T), "x_own": xo,
            "cT": np.ascontiguousarray(cT), "cT_rep": np.ascontiguousarray(rep),
            "ckdT": np.ascontiguousarray(f(inp["cache_diff_k"])[0, 2 * c:2 * c + 2, :PAST].reshape(2, PAST, 512).transpose(0, 2, 1)),
            "ckfT": np.ascontiguousarray(f(inp["cache_fox_k"])[0, 2 * c:2 * c + 2, :PAST].reshape(2, PAST, 512).transpose(0, 2, 1)),
            "cvd": np.ascontiguousarray(f(inp["cache_diff_v"])[0, 2 * c:2 * c + 2, :PAST].reshape(2, PAST, 512)),
            "cvf": np.ascontiguousarray(f(inp["cache_fox_v"])[0, 2 * c:2 * c + 2, :PAST].reshape(2, PAST, 512)),
            "clfT": np.ascontiguousarray(f(inp["cache_fox_logf"])[0, 2 * c:2 * c + 2, :PAST].transpose(0, 2, 1)),
            "sel": sel.reshape(1, 27), "selB": selB.reshape(1, NR * NB),
        })
        maps.append(m)
    return maps


def assemble(results, S=16384):
    NB = S // 128
    NR = NB // 8
    y_p = np.zeros((1, S, D), np.float32)
    y_s = np.zeros((16, 16, D), np.float32)
    outs_p = {k: np.zeros((S, w), np.float32) for k, w in (("kd", 512), ("vd", 512), ("kf", 512), ("vf", 512), ("lf", 8))}
    outs_s = {k: np.zeros((16, 16, w), np.float32) for k, w in (("kd", 512), ("vd", 512), ("kf", 512), ("vf", 512), ("lf", 8))}
    for c, r in enumerate(results):
        for rr in range(NR):
            bl = c + 8 * rr
            y_p[0, bl * 128:(bl + 1) * 128] = r["y_own"][rr * 128:(rr + 1) * 128]
            for k in outs_p:
                outs_p[k][bl * 128:(bl + 1) * 128] = r[k + "_own"][rr * 128:(rr + 1) * 128]
        o = NR * 128
        y_s[2 * c] = r["y_own"][o:o + 16]
        y_s[2 * c + 1] = r["y_own"][o + 16:o + 32]
        for k in outs_s:
            outs_s[k][2 * c] = r[k + "_own"][o:o + 16]
            outs_s[k][2 * c + 1] = r[k + "_own"][o + 16:o + 32]
    return (y_p, y_s,
            outs_p["kd"].reshape(1, 1, S, 4, 128), outs_p["vd"].reshape(1, 1, S, 4, 128),
            outs_p["kf"].reshape(1, 1, S, 8, 64), outs_p["vf"].reshape(1, 1, S, 8, 64), outs_p["lf"].reshape(1, 1, S, 8),
            outs_s["kd"].reshape(1, 16, 16, 4, 128), outs_s["vd"].reshape(1, 16, 16, 4, 128),
            outs_s["kf"].reshape(1, 16, 16, 8, 64), outs_s["vf"].reshape(1, 16, 16, 8, 64), outs_s["lf"].reshape(1, 16, 16, 8))


def kernel(**inputs):
    from concourse.bass_utils import run_bass_kernel_spmd
    nc, P, es, out_ops, _ = build()
    finish(nc, P, es, out_ops)
    maps = host_inputs(inputs)
    res = run_bass_kernel_spmd(nc, maps, core_ids=list(range(8)))
    return assemble(res.results)
```

```python
from contextlib import ExitStack
import concourse.bass as bass
import concourse.mybir as mybir

F32 = mybir.dt.float32
BF16 = mybir.dt.bfloat16
AF = mybir.ActivationFunctionType
ALU = mybir.AluOpType
AX = mybir.AxisListType

ENGS = ("pe", "act", "dve", "pool", "sp")


class Grp:
    __slots__ = ("sem", "final")

    def __init__(self, sem):
        self.sem = sem
        self.final = 0


class Op:
    __slots__ = ("eng", "fn", "waits", "signal", "value", "grp", "dval")

    def __init__(self, eng, fn):
        self.eng = eng
        self.fn = fn
        self.waits = []
        self.signal = False
        self.value = None
        self.grp = None
        self.dval = None


class Buf:
    def __init__(self, P, name, t=None):
        self.P = P
        self.name = name
        self.t = t
        self.w = []
        self.r_eng = {}
        self.r_dma = []
        self.prev_r = []
        self.ld = None
        self.st = None
        self.ldp = None
        self.stp = None
        P.bufs.append(self)

    def __getitem__(self, k):
        return self.t[k]

    def _dsem(self, which):
        d = getattr(self, which)
        if d is None:
            if self.P.free_sems and not which.endswith("p"):
                sem, cnt = self.P.free_sems.pop()
                d = [sem, cnt, None]
            else:
                sem = self.P.new_sem(f"{which}_{self.name}")
                d = [sem, 0, None]
            setattr(self, which, d)
        return d


class Prog:
    def __init__(self, nc, es):
        self.nc = nc
        self.es = es
        self.ops = {e: [] for e in ENGS}
        self.nsem = 0
        self.esem = {e: self.new_sem("eng_" + e) for e in ENGS}
        self.nbuf = 0
        self.pending_dma = {}
        self.scopes = []
        self.bufs = []
        self.free_sems = []
        self.scope_bufs = []

    def new_sem(self, name):
        self.nsem += 1
        return self.es.enter_context(self.nc.semaphore(f"s{self.nsem}_{name}"))

    def sb(self, name, shape, dt):
        st = self.scopes[-1] if self.scopes else self.es
        t = st.enter_context(self.nc.sbuf_tensor(name, list(shape), dt))
        b = Buf(self, name, t)
        if self.scope_bufs:
            self.scope_bufs[-1].append(b)
        return b

    def push(self):
        self.scopes.append(ExitStack())
        self.scope_bufs.append([])

    def pop(self):
        self.fence()
        self.scopes.pop().close()
        for b in self.scope_bufs.pop():
            for d in (b.ld, b.st):
                if d is not None:
                    self.free_sems.append((d[0], d[1]))
            b.ld = b.st = None

    def ps(self, name, shape, dt):
        t = self.es.enter_context(self.nc.psum_tensor(name, list(shape), dt))
        return Buf(self, name, t)

    def dram(self, name, shape, dt):
        t = self.nc.dram_tensor(name, list(shape), dt).ap()
        return Buf(self, name, t)

    def _deps(self, op, reads, writes, nowaw):
        waits = op.waits
        for b in reads:
            for w in b.w:
                waits.append(w)
        for b in writes:
            if not nowaw:
                for w in b.w:
                    if not (w.eng == "pe" and op.eng == "pe"):
                        waits.append(w)
            for r in b.r_eng.values():
                if not (r.eng == op.eng == "pe"):
                    waits.append(r)
            waits.extend(b.r_dma)
            for r in b.prev_r:
                if not (r.eng == op.eng == "pe"):
                    waits.append(r)
        for w in waits:
            if w.grp is None:
                w.signal = True
        for b in writes:
            if b.r_eng or b.r_dma:
                b.prev_r = list(b.r_eng.values()) + list(b.r_dma)
                b.w = [op]
                b.r_eng = {}
                b.r_dma = []
                if b.st is not None:
                    b.st[2] = None
                if b.stp is not None:
                    b.stp[2] = None
            else:
                if op.grp is None:
                    b.w = [x for x in b.w if x.grp is not None or x.eng != op.eng]
                b.w.append(op)
        for b in reads:
            if op.grp is not None:
                b.r_dma.append(op)
            else:
                b.r_eng[op.eng] = op
            if b.ld is not None:
                b.ld[2] = None
            if b.ldp is not None:
                b.ldp[2] = None

    def op(self, eng, fn, reads=(), writes=(), waits=(), nowaw=False):
        o = Op(eng, fn)
        o.waits.extend(waits)
        self._deps(o, reads, writes, nowaw)
        self.ops[eng].append(o)
        return o

    def dma(self, eng, out, in_, src=None, dst=None, waits=(), nowaw=True, sembuf=None, **kw):
        o = Op(eng, lambda e: e.dma_start(out=out, in_=in_, **kw))
        o.waits.extend(waits)
        sfx = "p" if eng == "pool" else ""
        if sembuf is not None:
            d = sembuf[0]._dsem(sembuf[1])
        elif dst is not None and not dst.name.startswith("D_"):
            d = dst._dsem("ld" + sfx)
        elif src is not None:
            d = src._dsem("st" + sfx)
        else:
            d = dst._dsem("ld" + sfx)
        reads = [src] if src is not None else []
        writes = [dst] if dst is not None else []
        if d[2] is None:
            d[2] = Grp(d[0])
        o.grp = d[2]
        d[1] += 16
        o.dval = d[1]
        o.grp.final = d[1]
        self._deps(o, reads, writes, nowaw)
        if dst is not None and (dst.ld is d or dst.ldp is d):
            d[2] = o.grp
        if src is not None and (src.st is d or src.stp is d):
            d[2] = o.grp
        self.ops[eng].append(o)
        self.pending_dma[id(o.grp)] = o
        return o

    def fence(self):
        lasts = []
        for e in ENGS:
            for o in reversed(self.ops[e]):
                if o.grp is None:
                    lasts.append(o)
                    break
        dmas = list(self.pending_dma.values())
        for o in lasts:
            o.signal = True
        for e in ENGS:
            f = Op(e, lambda en: en.nop())
            f.waits = lasts + dmas
            self.ops[e].append(f)
        self.pending_dma = {}
        for b in self.bufs:
            for d in (b.ld, b.st, b.ldp, b.stp):
                if d is not None:
                    d[2] = None

    def emit(self, final_waits):
        nc = self.nc
        for e in ENGS:
            n = 0
            for o in self.ops[e]:
                if o.signal:
                    n += 1
                    o.value = n
        fin = Op("sp", lambda e: e.nop())
        fin.waits.extend(final_waits)
        for w in final_waits:
            if w.grp is None:
                w.signal = True
        for e in ENGS:
            n = 0
            for o in self.ops[e]:
                if o.signal:
                    n += 1
                    o.value = n
        self.ops["sp"].append(fin)
        esem = self.esem

        def run(engname, eng):
            seen = {}
            for o in self.ops[engname]:
                for w in o.waits:
                    if w.grp is not None:
                        sem, val = w.grp.sem, w.grp.final
                    else:
                        sem, val = esem[w.eng], w.value
                    if seen.get(sem, 0) < val:
                        eng.wait_ge(sem, val)
                        seen[sem] = val
                ins = o.fn(eng)
                if o.grp is not None:
                    ins.then_inc(o.grp.sem, 16)
                elif o.signal:
                    ins.then_inc(esem[engname], 1)

        with nc.Block() as block:
            @block.tensor
            def _(e):
                run("pe", e)

            @block.scalar
            def _(e):
                run("act", e)

            @block.vector
            def _(e):
                run("dve", e)

            @block.gpsimd
            def _(e):
                run("pool", e)

            @block.sync
            def _(e):
                run("sp", e)
import math
import numpy as np

D = 1024
NCOL = 5128
C_QA, C_KA, C_VA, C_QB, C_KB, C_VB, C_F, C_GA, C_GB = 0, 512, 1024, 1536, 2048, 2560, 3072, 3080, 4104
ALPHA = 2.0 ** 0.25
LN_EPS = 1e-5
LAM_INIT = 0.8 - 0.6 * math.exp(0.0)
NEGM = -30000.0


def t5_thresholds():
    import jax, jax.numpy as jnp
    with jax.default_device(jax.devices("cpu")[0]):
        rel = jnp.arange(-255, 128, dtype=jnp.int32)
        nb = 16
        ret = jnp.where(rel > 0, nb, 0)
        n = jnp.abs(rel)
        max_exact = nb // 2
        nf = jnp.maximum(n, 1).astype(jnp.float32)
        large = max_exact + (jnp.log(nf / max_exact) / math.log(128 / max_exact) * (nb - max_exact)).astype(jnp.int32)
        large = jnp.minimum(large, nb - 1)
        bk = np.asarray(ret + jnp.where(n < max_exact, n, large))
    rels = np.arange(-255, 128)
    th = []
    for i in range(1, len(rels)):
        if bk[i] != bk[i - 1]:
            th.append((int(rels[i]), int(bk[i - 1]), int(bk[i])))
    return int(bk[0]), th


def build(S=16384, PAST=2048, phases=(0, 1, 2, 3), dbg=False):
    from contextlib import ExitStack
    import os
    SKIP = os.environ.get('SKIP', '').split(',')
    NB = S // 128
    NR = NB // 8
    NOWN = NR + 1
    TOWN = NOWN * 128
    NG = NB // 4
    NKB_S = PAST // 128

    nc = bass.Bass("TRN2", target_bir_lowering=False)

    def ein(name, shape, dt=F32):
        return nc.dram_tensor(name, list(shape), dt, kind="ExternalInput").ap()

    def eout(name, shape, dt=F32):
        return nc.dram_tensor(name, list(shape), dt, kind="ExternalOutput").ap()

    xT_all = ein("xT_all", [D, S])
    xT_own = ein("xT_own", [D, TOWN])
    x_own = ein("x_own", [TOWN, D])
    cT = ein("cT", [128, 24])
    cT_rep = ein("cT_rep", [128, 8 * 256])
    b_ada_fm = ein("b_ada_fm", [128, 48])
    b_ada = ein("b_ada", [1, 6144])
    w_ada = ein("w_ada", [D, 6144])
    w_in = ein("w_in", [D, NCOL])
    b_forget = ein("b_forget", [1, 8])
    lam_in = ein("lam", [1, 256])
    subln_g = ein("subln_g", [1, 128])
    rel_bias = ein("rel_bias", [1, 128])
    w_a = ein("w_a", [512, D])
    w_b = ein("w_b", [512, D])
    w_o = ein("w_o", [D, D])
    ln_in = ein("ln", [1, 4096])
    w_f1 = ein("w_f1", [D, 5632])
    w_f2 = ein("w_f2", [2816, D])
    ckdT = ein("ckdT", [2, 512, PAST])
    ckfT = ein("ckfT", [2, 512, PAST])
    cvd = ein("cvd", [2, PAST, 512])
    cvf = ein("cvf", [2, PAST, 512])
    clfT = ein("clfT", [2, 8, PAST])
    sel_in = ein("sel", [1, 27])
    selB = ein("selB", [1, NR * NB])

    y_own = eout("y_own", [TOWN, D])
    kd_own = eout("kd_own", [TOWN, 512])
    vd_own = eout("vd_own", [TOWN, 512])
    kf_own = eout("kf_own", [TOWN, 512])
    vf_own = eout("vf_own", [TOWN, 512])
    lf_own = eout("lf_own", [TOWN, 8])
    if dbg:
        dbg_oa = eout("dbg_oa", [TOWN, 512], BF16)
        dbg_ob = eout("dbg_ob", [TOWN, 512], BF16)

    es = ExitStack()
    P = Prog(nc, es)
    out_ops = []

    D_KT = P.dram("D_KT", [16, 72, S], BF16)
    D_QT = P.dram("D_QT", [16, 72, TOWN], BF16)
    D_VD = P.dram("D_VD", [4, NG, 128, 4 * 130], BF16)
    D_VF = P.dram("D_VF", [8, NG, 128, 4 * 66], BF16)
    D_GATES = P.dram("D_GATES", [TOWN, 2048], F32)
    D_X1 = P.dram("D_X1", [TOWN, D], F32)
    D_G2 = P.dram("D_G2", [2, 128, 1024], F32)
    D_OA = P.dram("D_OA", [TOWN, 512], BF16)
    D_OB = P.dram("D_OB", [TOWN, 512], BF16)
    D_WA = P.dram("D_WA", [512, D], BF16)
    D_WB = P.dram("D_WB", [512, D], BF16)
    D_WO = P.dram("D_WO", [D, D], BF16)
    D_WF1 = P.dram("D_WF1", [D, 5632], BF16)
    D_WF2 = P.dram("D_WF2", [2816, D], BF16)

    psall = es.enter_context(nc.psum_tensor("psall", [128, 4096], F32))
    banks = [Buf(P, f"bank{i}", psall[:, i * 512:(i + 1) * 512]) for i in range(8)]

    ident = P.sb("ident", [128, 128], BF16)
    ones_f = P.sb("ones_f", [128, 512], F32)
    ones_b = P.sb("ones_b", [128, 512], BF16)
    P.op("pool", lambda e: e.memset(ident[:], 0.0), writes=[ident])
    P.op("pool", lambda e: e.affine_select(out=ident[:], in_=ident[:], pattern=[[-1, 128]],
                                            compare_op=ALU.not_equal, fill=1.0, base=0, channel_multiplier=1),
         reads=[ident], writes=[ident])
    P.op("pool", lambda e: e.memset(ones_f[:], 1.0), writes=[ones_f])
    P.op("pool", lambda e: e.memset(ones_b[:], 1.0), writes=[ones_b])

    bfm = P.sb("bfm", [128, 48], F32)
    P.dma("sp", bfm[:], b_ada_fm[:, :], dst=bfm)
    nbfo = P.sb("nbfo", [8, 1], F32)
    P.dma("sp", nbfo[:], b_forget.rearrange("o h -> h o"), dst=nbfo)
    P.op("dve", lambda e: e.tensor_scalar(out=nbfo[:], in0=nbfo[:], scalar1=-1.0, scalar2=None, op0=ALU.mult),
         reads=[nbfo], writes=[nbfo])
    bfo_bc = P.sb("bfo_bc", [128, 8], F32)
    P.dma("sp", bfo_bc[:], b_forget[0:1, :].partition_broadcast(128), dst=bfo_bc)

    cT_sb = P.sb("cT_sb", [128, 8, 3], F32)
    P.dma("sp", cT_sb[:].rearrange("p k j -> p (k j)"), cT[:, :], dst=cT_sb)
    sT = P.sb("sT", [128, 8, 3], F32)
    P.op("act", lambda e: e.activation(out=sT[:], in_=cT_sb[:], func=AF.Silu), reads=[cT_sb], writes=[sT])

    SSQ = P.sb("SSQ", [128, NOWN, 4], F32)
    P.op("pool", lambda e: e.memset(SSQ[:], 1.0), writes=[SSQ])
    P.push()
    fTo = P.sb("fTo", [8, NOWN, 128], F32)
    vsn_d = P.sb("vsn_d", [128, 4, 130], BF16)
    vsn_f = P.sb("vsn_f", [128, 8, 66], BF16)
    KTn = P.sb("KTn", [128, 8, 128], BF16)
    Gend = P.sb("Gend", [8, NB + 1], F32)
    sh1 = P.sb("sh1", [128, 8, 3], F32)
    sc1 = P.sb("sc1", [128, 8, 3], F32)
    rb_bc = P.sb("rb_bc", [128, 128], F32)
    zero_f = P.sb("zero_f", [128, 128], F32)
    TP = P.sb("TP", [128, 4, 128], F32)
    TDg = P.sb("TDg", [128, 4, 128], F32)
    Gt = [P.sb(f"Gt{i}", [128, 128], F32) for i in range(2)]
    dl = P.sb("dl", [128, 4], F32)
    P.push()
    win = P.sb("win", [128, 8, NCOL], BF16)
    P.push()
    wada = [P.sb(f"wada{i}", [128, 8, 512], F32) for i in range(2)]
    w_ada_v = w_ada.rearrange("(k p) n -> p k n", p=128)
    mps = banks[7]
    for g in range(4):
        wt = wada[g % 2]
        P.dma("sp", wt[:], w_ada_v[:, :, g * 512:(g + 1) * 512], dst=wt)
        for j in range(4):
            c = (g * 4 + j) * 3
            for k in range(8):
                if 'mod' in SKIP:
                    continue
                P.op("pe", lambda e, wt=wt, j=j, k=k, c=c: e.matmul(
                    mps[:, c:c + 3], lhsT=wt[:, k, j * 128:(j + 1) * 128], rhs=sT[:, k, :],
                    start=(k == 0), stop=(k == 7)), reads=[wt, sT], writes=[mps])
    P.op("dve", lambda e: e.tensor_tensor(out=sh1[:], in0=mps[:, 0:24].rearrange("p (k j) -> p k j", j=3),
                                          in1=bfm[:, 0:8].unsqueeze(2).to_broadcast([128, 8, 3]), op=ALU.add),
         reads=[mps, bfm], writes=[sh1])
    P.op("dve", lambda e: e.scalar_tensor_tensor(out=sc1[:], in0=mps[:, 24:48].rearrange("p (k j) -> p k j", j=3),
                                                 scalar=1.0, in1=bfm[:, 8:16].unsqueeze(2).to_broadcast([128, 8, 3]),
                                                 op0=ALU.add, op1=ALU.add),
         reads=[mps, bfm], writes=[sc1])

    stg32 = [P.sb(f"stg32_{i}", [128, 2048], F32) for i in range(2)]
    w_in_v = w_in.rearrange("(k p) n -> p k n", p=128)
    pieces = [(0, 2048), (2048, 4096), (4096, NCOL)]
    i = 0
    for k in range(8):
        for (c0, c1) in pieces:
            st = stg32[i % 2]
            P.dma("sp", st[:, 0:c1 - c0], w_in_v[:, k, c0:c1], dst=st)
            if i % 2 == 0:
                P.op("pool", lambda e, st=st, k=k, c0=c0, c1=c1: e.tensor_copy(out=win[:, k, c0:c1], in_=st[:, 0:c1 - c0]),
                     reads=[st], writes=[win], nowaw=True)
            else:
                P.op("act", lambda e, st=st, k=k, c0=c0, c1=c1: e.copy(out=win[:, k, c0:c1], in_=st[:, 0:c1 - c0]),
                     reads=[st], writes=[win], nowaw=True)
            i += 1

    def split3(src, hi, mid, lo, r_, npart, n, neg=False):
        if neg:
            P.op("dve", lambda e: e.tensor_scalar(out=r_[0:npart, 0:n], in0=src[0:npart, 0:n], scalar1=-1.0, scalar2=None, op0=ALU.mult),
                 reads=[src], writes=[r_])
            base = r_
        else:
            base = src
        P.op("dve", lambda e: e.tensor_copy(out=hi[0:npart, 0:n], in_=base[0:npart, 0:n]), reads=[base], writes=[hi])
        P.op("dve", lambda e: e.tensor_tensor(out=r_[0:npart, 0:n], in0=base[0:npart, 0:n], in1=hi[0:npart, 0:n], op=ALU.subtract),
             reads=[base, hi], writes=[r_])
        P.op("dve", lambda e: e.tensor_copy(out=mid[0:npart, 0:n], in_=r_[0:npart, 0:n]), reads=[r_], writes=[mid])
        P.op("dve", lambda e: e.tensor_tensor(out=r_[0:npart, 0:n], in0=r_[0:npart, 0:n], in1=mid[0:npart, 0:n], op=ALU.subtract),
             reads=[r_, mid], writes=[r_])
        P.op("dve", lambda e: e.tensor_copy(out=lo[0:npart, 0:n], in_=r_[0:npart, 0:n]), reads=[r_], writes=[lo])

    P.pop()
    P.dma("sp", rb_bc[:], rel_bias[0:1, :].partition_broadcast(128), dst=rb_bc)
    bk0, ths = t5_thresholds()
    P.op("pool", lambda e: e.memset(zero_f[:], 0.0), writes=[zero_f])
    P.op("dve", lambda e: e.memset(TP[:], 0.0), writes=[TP])
    P.op("dve", lambda e: e.memset(TDg[:], 0.0), writes=[TDg])
    gi_ = 0
    for (t, bb, ba) in ths:
        for which, T_, off, lo_, hi_ in (("p", TP, -128, -255, -1), ("d", TDg, 0, -127, 127)):
            if not (lo_ < t <= hi_):
                continue
            G_ = Gt[gi_ % 2]
            gi_ += 1
            P.op("pool", lambda e, G_=G_, off=off, t=t: e.affine_select(
                out=G_[:], in_=ones_f[:, 0:128], pattern=[[-1, 128]], compare_op=ALU.is_ge, fill=0.0,
                base=off - t, channel_multiplier=1), reads=[ones_f], writes=[G_])
            P.op("dve", lambda e, bb=bb, ba=ba: e.tensor_tensor(out=dl[:], in0=rb_bc[:, ba * 4:ba * 4 + 4], in1=rb_bc[:, bb * 4:bb * 4 + 4],
                                                              op=ALU.subtract), reads=[rb_bc], writes=[dl])
            for h in range(4):
                P.op("dve", lambda e, G_=G_, T_=T_, h=h: e.scalar_tensor_tensor(
                    out=T_[:, h, :], in0=G_[:], scalar=dl[:, h:h + 1], in1=T_[:, h, :], op0=ALU.mult, op1=ALU.add),
                    reads=[G_, dl, T_], writes=[T_])
    P.push()
    xto = [P.sb(f"xto{i}", [128, 8, 128], F32) for i in range(2)]
    hTo = [P.sb(f"hTo{i}", [128, 8, 128], BF16) for i in range(2)]
    stgF = [P.sb(f"stgF{i}", [128, 512], F32) for i in range(4)]
    stgL = [P.sb(f"stgL{i}", [128, 8], F32) for i in range(2)]
    stgE = [P.sb(f"stgE{i}", [128, 8], F32) for i in range(2)]
    QTs = [P.sb(f"QTs{i}", [128, 4, 128], BF16) for i in range(2)]
    xT_own_v = xT_own.rearrange("(k p) t -> p k t", p=128)
    P.op("pool", lambda e: e.memset(vsn_d[:], 1.0), writes=[vsn_d])
    P.op("pool", lambda e: e.memset(vsn_f[:], 1.0), writes=[vsn_f])
    bk = [0]

    def nbank(lo=0, hi=8):
        b = banks[lo + bk[0] % (hi - lo)]
        bk[0] += 1
        return b

    evi = [0]

    def evac(out_ap, in_ap, reads, writes, scale=None, eng=None):
        en = eng or ("act" if evi[0] % 2 == 0 else "dve")
        evi[0] += 1
        if en == "act":
            if scale is None:
                return P.op("act", lambda e: e.copy(out=out_ap, in_=in_ap), reads=reads, writes=writes, nowaw=True)
            return P.op("act", lambda e: e.mul(out=out_ap, in_=in_ap, mul=scale), reads=reads, writes=writes, nowaw=True)
        if scale is None:
            return P.op("dve", lambda e: e.tensor_copy(out=out_ap, in_=in_ap), reads=reads, writes=writes, nowaw=True)
        return P.op("dve", lambda e: e.tensor_scalar(out=out_ap, in0=in_ap, scalar1=scale, scalar2=None, op0=ALU.mult),
                    reads=reads, writes=writes, nowaw=True)

    sF = [0]
    if 1 in phases:
        for b in range(0 if 'tm' in SKIP else NOWN):
            xt = xto[b % 2]
            hT = hTo[b % 2]
            if b == 0:
                P.dma("sp", xt[:], xT_own_v[:, :, 0:128], dst=xt)
            if b + 1 < NOWN:
                P.dma("sp", xto[(b + 1) % 2][:], xT_own_v[:, :, (b + 1) * 128:(b + 2) * 128], dst=xto[(b + 1) % 2])
            def modulate_blk(bb):
                xt_, hT_ = xto[bb % 2], hTo[bb % 2]
                for k in range(8):
                    segs = [(0, 128, 0)] if bb < NR else [(0, 16, 1), (16, 128, 2)]
                    for (a0, a1, j) in segs:
                        P.op("dve", lambda e, xt_=xt_, hT_=hT_, k=k, a0=a0, a1=a1, j=j: e.tensor_scalar(
                            out=hT_[:, k, a0:a1], in0=xt_[:, k, a0:a1], scalar1=sc1[:, k, j:j + 1], scalar2=sh1[:, k, j:j + 1],
                            op0=ALU.mult, op1=ALU.add), reads=[xt_, sc1, sh1], writes=[hT_], nowaw=True)

            if b == 0:
                modulate_blk(0)
            tm = [("ka", C_KA, kd_own), ("va", C_VA, vd_own), ("kb", C_KB, kf_own), ("vb", C_VB, vf_own),
                  ("ga0", C_GA, None), ("ga1", C_GA + 512, None), ("gb0", C_GB, None), ("gb1", C_GB + 512, None)]
            for gi, (nm, c0, dest) in enumerate(tm):
                if 'gates' in SKIP and dest is None:
                    continue
                bank = nbank()
                for k in range(8):
                    P.op("pe", lambda e, bank=bank, hT=hT, k=k, c0=c0: e.matmul(
                        bank[:, :], lhsT=hT[:, k, :], rhs=win[:, k, c0:c0 + 512], start=(k == 0), stop=(k == 7)),
                        reads=[hT, win], writes=[bank])
                st = stgF[sF[0] % 4]
                sF[0] += 1
                evac(st[:], bank[:, :], [bank], [st])
                if dest is not None:
                    out_ops.append(P.dma("pool", dest[b * 128:(b + 1) * 128, :], st[:], src=st))
                else:
                    P.dma("pool", D_GATES[b * 128:(b + 1) * 128, (gi - 4) * 512:(gi - 3) * 512], st[:], src=st, dst=D_GATES)
                if b == NR and nm == "va" and 'vsn' not in SKIP:
                    P.op("pool", lambda e, st=st: e.tensor_copy(out=vsn_d[:, :, 0:128], in_=st[:].rearrange("p (h d) -> p h d", d=128)),
                         reads=[st], writes=[vsn_d])
                if b == NR and nm == "vb" and 'vsn' not in SKIP:
                    P.op("pool", lambda e, st=st: e.tensor_copy(out=vsn_f[:, :, 0:64], in_=st[:].rearrange("p (h d) -> p h d", d=64)),
                         reads=[st], writes=[vsn_f])
            if b + 1 < NOWN:
                modulate_blk(b + 1)
            if 'f' in SKIP:
                continue
            bank = nbank()
            for k in range(8):
                P.op("pe", lambda e, bank=bank, hT=hT, k=k: e.matmul(
                    bank[:, 0:8], lhsT=hT[:, k, :], rhs=win[:, k, C_F:C_F + 8], start=(k == 0), stop=(k == 7)),
                    reads=[hT, win], writes=[bank])
            sl = stgL[b % 2]
            se = stgE[b % 2]
            P.op("dve", lambda e, bank=bank, se=se: e.tensor_tensor(out=se[:], in0=bank[:, 0:8], in1=bfo_bc[:], op=ALU.add),
                 reads=[bank, bfo_bc], writes=[se])
            P.op("act", lambda e, se=se: e.activation(out=se[:], in_=se[:], func=AF.Exp, scale=-1.0), reads=[se], writes=[se])
            P.op("act", lambda e, se=se: e.activation(out=se[:], in_=se[:], func=AF.Ln, bias=1.0), reads=[se], writes=[se])
            P.op("dve", lambda e, se=se, sl=sl: e.tensor_scalar(out=sl[:], in0=se[:], scalar1=-1.0, scalar2=None, op0=ALU.mult),
                 reads=[se], writes=[sl])
            out_ops.append(P.dma("sp", lf_own[b * 128:(b + 1) * 128, :], sl[:], src=sl))
            if 'ffm' in SKIP:
                continue
            bank = nbank()
            for k in range(8):
                P.op("pe", lambda e, bank=bank, hT=hT, k=k: e.matmul(
                    bank[0:8, 0:128], lhsT=win[:, k, C_F:C_F + 8], rhs=hT[:, k, :], start=(k == 0), stop=(k == 7)),
                    reads=[hT, win], writes=[bank])
            P.op("act", lambda e, bank=bank, b=b: e.activation(out=fTo[:, b, :], in_=bank[0:8, 0:128], func=AF.Exp, scale=-1.0, bias=nbfo[:]),
                 reads=[bank, nbfo], writes=[fTo], nowaw=True)
            P.op("act", lambda e, b=b: e.activation(out=fTo[:, b, :], in_=fTo[:, b, :], func=AF.Ln, bias=1.0),
                 reads=[fTo], writes=[fTo])
            if 'q' in SKIP:
                continue
            for half, cbase in ((0, C_QA), (1, C_QB)):
                bank = nbank()
                for cc in range(4):
                    for k in range(8):
                        P.op("pe", lambda e, bank=bank, hT=hT, k=k, cc=cc, cbase=cbase: e.matmul(
                            bank[:, cc * 128:(cc + 1) * 128], lhsT=win[:, k, cbase + cc * 128:cbase + (cc + 1) * 128],
                            rhs=hT[:, k, :], start=(k == 0 and cc == 0), stop=(k == 7), skip_group_check=True),
                            reads=[hT, win], writes=[bank])
                qs = QTs[half]
                evac(qs[:].rearrange("p c t -> p (c t)"), bank[:, :], [bank], [qs], scale=0.125, eng="dve")
                for cc in range(4):
                    for s in range(2):
                        u = half * 8 + cc * 2 + s
                        P.dma("pool", D_QT[u, 0:64, b * 128:(b + 1) * 128], qs[64 * s:64 * s + 64, cc, :], src=qs, dst=D_QT)

            if b == NR:
                for half, cbase in ((0, C_KA), (1, C_KB)):
                    bank = nbank()
                    for cc in range(4):
                        for k in range(8):
                            P.op("pe", lambda e, bank=bank, hT=hT, k=k, cc=cc, cbase=cbase: e.matmul(
                                bank[:, cc * 128:(cc + 1) * 128], lhsT=win[:, k, cbase + cc * 128:cbase + (cc + 1) * 128],
                                rhs=hT[:, k, :], start=(k == 0 and cc == 0), stop=(k == 7), skip_group_check=True),
                                reads=[hT, win], writes=[bank])
                    evac(KTn[:, half * 4:half * 4 + 4, :].rearrange("p c t -> p (c t)"), bank[:, :], [bank], [KTn])

    P.pop()
    P.push()
    if 1 in phases:
        xt4s = [P.sb(f"xt4_{i}", [128, 8, 512], F32) for i in range(2)]
        hT4s = [P.sb(f"hT4_{i}", [128, 8, 512], BF16) for i in range(2)]
        ktss = [P.sb(f"kts{i}", [128, 512], BF16) for i in range(4)]
        VDs = [P.sb(f"VDs{i}", [128, 4, 4, 130], BF16) for i in range(2)]
        VFs = [P.sb(f"VFs{i}", [128, 8, 4, 66], BF16) for i in range(2)]
        for t in VDs + VFs:
            P.op("pool", lambda e, t=t: e.memset(t[:], 1.0), writes=[t])
        P.fence()
        spT = [P.sb(f"spT{i}", [8, 512], F32) for i in range(2)]
        Gc = [P.sb(f"Gc{i}", [8, 512], F32) for i in range(2)]
        P.op("dve", lambda e: e.memset(Gend[:], 0.0), writes=[Gend])
        gsp = [[P.sb(f"gsp{i}_{j}", [8, 512], BF16) for j in range(3)] for i in range(2)]
        gr = [P.sb(f"gr{i}", [8, 512], F32) for i in range(2)]
        xT_all_v = xT_all.rearrange("(k p) t -> p k t", p=128)
        kti = [0]
        for g in range(0 if 'b1' in SKIP else NG):
            xt = xt4s[g % 2]
            hT = hT4s[g % 2]

            def modulate_grp(gg):
                xt_, hT_ = xt4s[gg % 2], hT4s[gg % 2]
                for k in range(8):
                    if k % 2 == 0:
                        P.op("dve", lambda e, xt_=xt_, hT_=hT_, k=k: e.tensor_scalar(
                            out=hT_[:, k, :], in0=xt_[:, k, :], scalar1=sc1[:, k, 0:1], scalar2=sh1[:, k, 0:1],
                            op0=ALU.mult, op1=ALU.add), reads=[xt_, sc1, sh1], writes=[hT_], nowaw=True)
                    else:
                        P.op("act", lambda e, xt_=xt_, hT_=hT_, k=k: e.activation(
                            out=hT_[:, k, :], in_=xt_[:, k, :], func=AF.Identity, scale=sc1[:, k, 0:1], bias=sh1[:, k, 0:1]),
                            reads=[xt_, sc1, sh1], writes=[hT_], nowaw=True)

            def load_grp(gg):
                if gg < NG:
                    P.dma("sp", xt4s[gg % 2][:], xT_all_v[:, :, gg * 512:(gg + 1) * 512], dst=xt4s[gg % 2])

            if g == 0:
                load_grp(0)
                load_grp(1)
                modulate_grp(0)
            for cc in range(8):
                cbase = (C_KA + cc * 128) if cc < 4 else (C_KB + (cc - 4) * 128)
                bank = nbank()
                for k in range(8):
                    P.op("pe", lambda e, bank=bank, hT=hT, k=k, cbase=cbase: e.matmul(
                        bank[:, :], lhsT=win[:, k, cbase:cbase + 128], rhs=hT[:, k, :], start=(k == 0), stop=(k == 7)),
                        reads=[hT, win], writes=[bank])
                kt = ktss[kti[0] % 4]
                kti[0] += 1
                evac(kt[:], bank[:, :], [bank], [kt])
                for s_ in range(2):
                    u = cc * 2 + s_
                    P.dma("pool", D_KT[u, 0:64, g * 512:(g + 1) * 512], kt[64 * s_:64 * s_ + 64, :], src=kt, dst=D_KT)
            if g + 1 < NG:
                modulate_grp(g + 1)
            load_grp(g + 2)
            vd = VDs[g % 2]
            vf = VFs[g % 2]
            for blk in range(4):
                for (cbase, vt, nh, dh) in ((C_VA, vd, 4, 128), (C_VB, vf, 8, 64)):
                    bank = nbank()
                    for k in range(8):
                        P.op("pe", lambda e, bank=bank, hT=hT, k=k, cbase=cbase, blk=blk: e.matmul(
                            bank[:, :], lhsT=hT[:, k, blk * 128:(blk + 1) * 128], rhs=win[:, k, cbase:cbase + 512],
                            start=(k == 0), stop=(k == 7)), reads=[hT, win], writes=[bank])
                    evac(vt[:, :, blk, 0:dh], bank[:, :].rearrange("p (h d) -> p h d", d=dh), [bank], [vt])
            for h in range(4):
                P.dma("act", D_VD[h, g], vd[:, h, :, :].rearrange("p b d -> p (b d)"), src=vd, dst=D_VD)
            for h in range(8):
                P.dma("act", D_VF[h, g], vf[:, h, :, :].rearrange("p b d -> p (b d)"), src=vf, dst=D_VF)
            bank = nbank()
            for k in range(8):
                P.op("pe", lambda e, bank=bank, hT=hT, k=k: e.matmul(
                    bank[0:8, :], lhsT=win[:, k, C_F:C_F + 8], rhs=hT[:, k, :], start=(k == 0), stop=(k == 7)),
                    reads=[hT, win], writes=[bank])
            sp_ = spT[g % 2]
            P.op("act", lambda e, bank=bank, sp_=sp_: e.activation(out=sp_[:], in_=bank[0:8, :], func=AF.Exp, scale=-1.0, bias=nbfo[:]),
                 reads=[bank, nbfo], writes=[sp_])
            P.op("act", lambda e, sp_=sp_: e.activation(out=sp_[:], in_=sp_[:], func=AF.Ln, bias=1.0), reads=[sp_], writes=[sp_])
            gc = Gc[g % 2]
            gprev = Gc[(g - 1) % 2]
            if g == 0:
                P.op("dve", lambda e, gc=gc, sp_=sp_: e.tensor_tensor_scan(out=gc[:], data0=ones_f[0:8, :], data1=sp_[:], initial=0.0,
                                                                             op0=ALU.mult, op1=ALU.add), reads=[sp_, ones_f], writes=[gc])
            else:
                P.op("dve", lambda e, gc=gc, sp_=sp_, gprev=gprev: e.tensor_tensor_scan(
                    out=gc[:], data0=ones_f[0:8, :], data1=sp_[:], initial=gprev[:, 511:512], op0=ALU.mult, op1=ALU.add),
                    reads=[sp_, ones_f, gprev], writes=[gc])
            P.op("dve", lambda e, gc=gc, g=g: e.tensor_copy(out=Gend[:, 4 * g + 1:4 * g + 5],
                                                           in_=gc[:].rearrange("p (b t) -> p b t", t=128)[:, :, 127]),
                 reads=[gc], writes=[Gend], nowaw=True)
            hi, mid, lo = gsp[g % 2]
            r_ = gr[g % 2]
            split3(gc, hi, mid, lo, r_, 8, 512)
            for i_, tl in enumerate((hi, mid, lo)):
                P.dma("sp", D_KT[8:16, 67 + i_, g * 512:(g + 1) * 512], tl[:], src=tl, dst=D_KT)
    P.pop()
    P.pop()

    P.push()
    oa_bf = P.sb("oa_bf", [128, NOWN, 512], BF16)
    ob_bf = P.sb("ob_bf", [128, NOWN, 512], BF16)
    P.op("pool", lambda e: e.memset(oa_bf[:], 0.0), writes=[oa_bf])
    P.op("pool", lambda e: e.memset(ob_bf[:], 0.0), writes=[ob_bf])
    P.fence()
    if 2 in phases:
        P.push()
        TP_hi = P.sb("TP_hi", [128, 4, 128], BF16); TP_lo = P.sb("TP_lo", [128, 4, 128], BF16)
        TD_hi = P.sb("TD_hi", [128, 4, 128], BF16); TD_lo = P.sb("TD_lo", [128, 4, 128], BF16)
        TC_b = P.sb("TC_b", [128, 128], BF16)
        slD_hi = P.sb("slD_hi", [128, 9, 4, 128], BF16)
        slD_lo = P.sb("slD_lo", [128, 9, 4, 128], BF16)
        slF_b = P.sb("slF_b", [128, 9, 128], BF16)
        nlam = P.sb("nlam", [128, 1], F32)
        gcb = [P.sb(f"gcb{j}", [8, 2, PAST + 16], BF16) for j in range(3)]
        P.push()
        sel_bc = P.sb("sel_bc", [128, 27], F32)
        P.dma("sp", sel_bc[:], sel_in[0:1, :].partition_broadcast(128), dst=sel_bc)
        lam_bc = P.sb("lam_bc", [128, 256], F32)
        P.dma("sp", lam_bc[:], lam_in[0:1, :].partition_broadcast(128), dst=lam_bc)
        g_bc = P.sb("g_bc", [128, 128], F32)
        P.dma("sp", g_bc[:], subln_g[0:1, :].partition_broadcast(128), dst=g_bc)
        P.op("dve", lambda e: e.tensor_scalar(out=g_bc[:], in0=g_bc[:], scalar1=(1.0 - LAM_INIT), scalar2=None, op0=ALU.mult),
             reads=[g_bc], writes=[g_bc])
        lt = P.sb("lt", [128, 128], F32)
        lsum = P.sb("lsum", [128, 2], F32)
        for i_ in range(2):
            P.op("dve", lambda e, i_=i_: e.tensor_tensor(out=lt[:, i_ * 64:(i_ + 1) * 64], in0=lam_bc[:, i_ * 128:i_ * 128 + 64],
                                                         in1=lam_bc[:, i_ * 128 + 64:i_ * 128 + 128], op=ALU.mult),
                 reads=[lam_bc], writes=[lt])
        P.op("dve", lambda e: e.tensor_reduce(out=lsum[:], in_=lt[:].rearrange("p (a d) -> p a d", d=64), axis=AX.X, op=ALU.add),
             reads=[lt], writes=[lsum])
        P.op("act", lambda e: e.activation(out=lsum[:], in_=lsum[:], func=AF.Exp), reads=[lsum], writes=[lsum])
        P.op("dve", lambda e: e.tensor_tensor(out=nlam[:], in0=lsum[:, 1:2], in1=lsum[:, 0:1], op=ALU.subtract), reads=[lsum], writes=[nlam])
        P.op("dve", lambda e: e.tensor_scalar(out=nlam[:], in0=nlam[:], scalar1=-LAM_INIT, scalar2=None, op0=ALU.add), reads=[nlam], writes=[nlam])

        def hilo(src, hi, lo, tmp, shape_ap=lambda t: t[:]):
            P.op("dve", lambda e: e.tensor_copy(out=shape_ap(hi), in_=shape_ap(src)), reads=[src], writes=[hi])
            P.op("dve", lambda e: e.tensor_tensor(out=shape_ap(tmp), in0=shape_ap(src), in1=shape_ap(hi), op=ALU.subtract),
                 reads=[src, hi], writes=[tmp])
            P.op("dve", lambda e: e.tensor_copy(out=shape_ap(lo), in_=shape_ap(tmp)), reads=[tmp], writes=[lo])
        ttmp = P.sb("ttmp", [128, 4, 128], F32)
        hilo(TP, TP_hi, TP_lo, ttmp)
        hilo(TDg, TD_hi, TD_lo, ttmp)
        P.op("dve", lambda e: e.memset(TDg[64:128, :, 0:64], NEGM), reads=[TDg], writes=[TDg])
        TC = P.sb("TC", [128, 128], F32)
        P.op("pool", lambda e: e.affine_select(out=TC[:], in_=zero_f[:], pattern=[[1, 128]], compare_op=ALU.is_ge, fill=NEGM,
                                                base=0, channel_multiplier=-1), reads=[zero_f], writes=[TC])
        P.op("dve", lambda e: e.tensor_copy(out=TC_b[:], in_=TC[:]), reads=[TC], writes=[TC_b])
        slD = P.sb("slD", [128, 9, 4, 128], F32)
        slF = P.sb("slF", [128, 9, 128], F32)
        sm = P.sb("sm", [128, 9], F32)
        P.op("dve", lambda e: e.tensor_scalar(out=sm[:], in0=sel_bc[:].rearrange("p (s t) -> p s t", t=3)[:, :, 2], scalar1=NEGM, scalar2=None,
                                              op0=ALU.mult), reads=[sel_bc], writes=[sm])
        for s_ in range(9):
            P.op("dve", lambda e, s_=s_: e.tensor_scalar(out=slD[:, s_, :, :], in0=TP[:], scalar1=sel_bc[:, 3 * s_:3 * s_ + 1],
                                                         scalar2=sm[:, s_:s_ + 1], op0=ALU.mult, op1=ALU.add),
                 reads=[TP, sel_bc, sm], writes=[slD], nowaw=True)
            P.op("dve", lambda e, s_=s_: e.scalar_tensor_tensor(out=slD[:, s_, :, :], in0=TDg[:], scalar=sel_bc[:, 3 * s_ + 1:3 * s_ + 2],
                                                                in1=slD[:, s_, :, :], op0=ALU.mult, op1=ALU.add),
                 reads=[TDg, sel_bc, slD], writes=[slD])
            P.op("dve", lambda e, s_=s_: e.tensor_scalar(out=slF[:, s_, :], in0=TC[:], scalar1=sel_bc[:, 3 * s_ + 1:3 * s_ + 2],
                                                         scalar2=sm[:, s_:s_ + 1], op0=ALU.mult, op1=ALU.add),
                 reads=[TC, sel_bc, sm], writes=[slF], nowaw=True)
        sltmp = P.sb("sltmp", [128, 9, 4, 128], F32)
        hilo(slD, slD_hi, slD_lo, sltmp)
        P.op("dve", lambda e: e.tensor_copy(out=slF_b[:], in_=slF[:]), reads=[slF], writes=[slF_b])

        P.pop()
        Gsel = P.sb("Gsel", [8, NOWN], F32)
        onesQ = P.sb("onesQ", [8, TOWN], BF16)
        P.op("pool", lambda e: e.memset(onesQ[:], 1.0), writes=[onesQ])
        P.push()
        selB_bc = P.sb("selB_bc", [8, NR, NB], F32)
        P.dma("sp", selB_bc[:].rearrange("p r j -> p (r j)"), selB[0:1, :].partition_broadcast(8), dst=selB_bc)
        gprod = P.sb("gprod", [8, NR, NB], F32)
        P.op("dve", lambda e: e.tensor_tensor(out=gprod[:], in0=selB_bc[:], in1=Gend[:, 0:NB].unsqueeze(1).to_broadcast([8, NR, NB]),
                                              op=ALU.mult), reads=[selB_bc, Gend], writes=[gprod])
        P.op("dve", lambda e: e.memset(Gsel[:], 0.0), writes=[Gsel])
        P.op("dve", lambda e: e.tensor_reduce(out=Gsel[:, 0:NR], in_=gprod[:], axis=AX.X, op=ALU.add), reads=[gprod, Gsel], writes=[Gsel])
        P.pop()
        P.push()
        Gq = P.sb("Gq", [8, NOWN, 128], F32)
        W_ = NOWN * 128
        qh = [P.sb(f"qh{j}", [8, W_], BF16) for j in range(3)]
        qr = P.sb("qr", [8, W_], F32)
        for b_ in range(NR):
            P.op("dve", lambda e, b_=b_: e.tensor_tensor_scan(out=Gq[:, b_, :], data0=ones_f[0:8, 0:128], data1=fTo[:, b_, :],
                                                             initial=Gsel[:, b_:b_ + 1], op0=ALU.mult, op1=ALU.add),
                 reads=[fTo, ones_f, Gsel], writes=[Gq], nowaw=True)
        P.op("dve", lambda e: e.memset(Gq[:, NR, :], 0.0), writes=[Gq], nowaw=True)
        P.push()
        clf = P.sb("clf", [8, PAST], F32)
        Gcs = P.sb("Gcs", [8, PAST + 16], F32)
        gcr = P.sb("gcr", [8, PAST + 16], F32)
        gtmp = [P.sb(f"gtmp{j}", [8, PAST + 16], BF16) for j in range(3)]
        for sq in range(2):
            P.dma("sp", clf[:], clfT[sq], dst=clf)
            for cch in range(PAST // 512):
                init = 0.0 if cch == 0 else Gcs[:, cch * 512 - 1:cch * 512]
                P.op("dve", lambda e, cch=cch, init=init: e.tensor_tensor_scan(
                    out=Gcs[:, cch * 512:(cch + 1) * 512], data0=ones_f[0:8, 0:512], data1=clf[:, cch * 512:(cch + 1) * 512],
                    initial=init, op0=ALU.mult, op1=ALU.subtract), reads=[clf, ones_f, Gcs], writes=[Gcs])
            P.op("dve", lambda e, sq=sq: e.tensor_tensor_scan(out=Gq[:, NR, sq * 16:(sq + 1) * 16], data0=ones_f[0:8, 0:16],
                                                             data1=fTo[:, NR, sq * 16:(sq + 1) * 16], initial=Gcs[:, PAST - 1:PAST],
                                                             op0=ALU.mult, op1=ALU.add), reads=[fTo, ones_f, Gcs, Gq], writes=[Gq])
            P.op("dve", lambda e, sq=sq: e.tensor_copy(out=Gcs[:, PAST:PAST + 16], in_=Gq[:, NR, sq * 16:(sq + 1) * 16]),
                 reads=[Gq], writes=[Gcs])
            split3(Gcs, gtmp[0], gtmp[1], gtmp[2], gcr, 8, PAST + 16)
            for j in range(3):
                P.op("pool", lambda e, j=j, sq=sq: e.tensor_copy(out=gcb[j][:, sq, :], in_=gtmp[j][:]), reads=[gtmp[j]], writes=[gcb[j]])
        P.pop()
        Gq2 = Buf(P, "Gq2", Gq[:].rearrange("p b t -> p (b t)"))
        Gq2.w = Gq.w
        split3(Gq2, qh[0], qh[1], qh[2], qr, 8, W_, neg=True)
        for j in range(3):
            P.dma("sp", D_QT[8:16, 64 + j, :], qh[j][:], src=qh[j], dst=D_QT)
        gcb2 = gcb
        P.pop()

        NCH = 8
        ktile = [P.sb(f"ktile{i}", [128, NCH * 128], BF16) for i in range(3)]
        vtile = [P.sb(f"vtile{i}", [128, NCH * 130], BF16) for i in range(3)]
        qtileD = [P.sb(f"qtileD{i}", [128, TOWN], BF16) for i in range(2)]
        qtileF = [P.sb(f"qtileF{i}", [128, TOWN], BF16) for i in range(2)]
        qtile = qtileF
        PT = [P.sb(f"PT{i}", [128, 1024], BF16) for i in range(2)]
        O1 = P.sb("O1", [128, NOWN, 128], F32)
        for kt in ktile:
            P.op("pool", lambda e, kt=kt: e.memset(kt[:], 0.0), writes=[kt])
            P.op("pool", lambda e, kt=kt: e.memset(kt[64:67, :], 1.0), writes=[kt])
        for qt in qtileD + qtileF:
            P.op("pool", lambda e, qt=qt: e.memset(qt[:], 0.0), writes=[qt])
        P.fence()
        for qt in qtileF:
            P.dma("sp", qt[67:70, :], onesQ[0:3, 0:TOWN], src=onesQ, dst=qt)
        P.fence()
        SB2 = [Buf(P, f"SB2_{i}", psall[:, i * 1024:(i + 1) * 1024]) for i in range(2)]
        ACC = Buf(P, "ACC", psall[:, 2048:3584])
        _ACCbank = [Buf(P, f"ACCbank{i}", None) for i in range(3)]
        ACCb = [_ACCbank[a // 3] for a in range(8)]
        B7 = Buf(P, "B7", psall[:, 3584:4096])

        def acc_col(a, W):
            return (a // 3) * 512 + (a % 3) * W

        Ubuf = P.sb("Ubuf", [128, NOWN, 130], F32)
        rsall = P.sb("rsall", [128, NOWN], F32)
        P.op("pool", lambda e: e.memset(Ubuf[:], 1.0), writes=[Ubuf])
        P.fence()

        def normalize(u, blk, acc_ap, npart, AB):
            Wc = 130 if u < 8 else 66
            P.op("dve", lambda e: e.tensor_copy(out=Ubuf[0:npart, blk, 0:Wc], in_=acc_ap[:, 0:Wc]), reads=[AB], writes=[Ubuf], nowaw=True)

        def unit_finish(u):
            if u < 8:
                h, s_ = u // 2, u % 2
                P.op("dve", lambda e: e.reciprocal(out=rsall[:], in_=Ubuf[:, :, 128]), reads=[Ubuf], writes=[rsall])
                rb_ = lambda: rsall[:].unsqueeze(2).to_broadcast([128, NOWN, 128])
                if s_ == 0:
                    P.op("dve", lambda e: e.tensor_tensor(out=O1[:], in0=Ubuf[:, :, 0:128], in1=rb_(), op=ALU.mult),
                         reads=[Ubuf, rsall], writes=[O1])
                else:
                    U_ = lambda: Ubuf[:, :, 0:128]
                    P.op("dve", lambda e: e.tensor_tensor(out=U_(), in0=U_(), in1=rb_(), op=ALU.mult),
                         reads=[Ubuf, rsall], writes=[Ubuf])
                    P.op("dve", lambda e: e.scalar_tensor_tensor(out=U_(), in0=U_(), scalar=nlam[:, 0:1], in1=O1[:], op0=ALU.mult, op1=ALU.add),
                         reads=[Ubuf, nlam, O1], writes=[Ubuf])
                    P.op("dve", lambda e: e.tensor_copy(out=oa_bf[:, :, h * 128:(h + 1) * 128], in_=U_()), reads=[Ubuf], writes=[oa_bf], nowaw=True)
                    P.op("dve", lambda e: e.tensor_tensor(out=U_(), in0=U_(), in1=U_(), op=ALU.mult), reads=[Ubuf], writes=[Ubuf])
                    P.op("dve", lambda e: e.tensor_reduce(out=SSQ[:, :, h], in_=U_(), axis=AX.X, op=ALU.add), reads=[Ubuf], writes=[SSQ], nowaw=True)
            else:
                h = u - 8
                P.op("dve", lambda e: e.reciprocal(out=rsall[:], in_=Ubuf[:, :, 64]), reads=[Ubuf], writes=[rsall])
                P.op("dve", lambda e: e.tensor_tensor(out=ob_bf[:, :, h * 64:(h + 1) * 64], in0=Ubuf[:, :, 0:64],
                                                      in1=rsall[:].unsqueeze(2).to_broadcast([128, NOWN, 64]), op=ALU.mult),
                     reads=[Ubuf, rsall], writes=[ob_bf], nowaw=True)

        kS32 = [P.sb(f"kS32_{i}", [64, PAST], F32) for i in range(1)] * 2
        kS = [P.sb(f"kS{i}", [70, PAST + 16], BF16) for i in range(2)]
        vS32 = [P.sb(f"vS32_{i}", [128, NKB_S, 128], F32) for i in range(1)] * 2
        vSbD = P.sb("vSbD", [128, NKB_S * 130], BF16)
        vSbF = P.sb("vSbF", [128, NKB_S * 66], BF16)
        PTs = [P.sb(f"PTs{i}", [128, NKB_S + 1, 32], BF16) for i in range(2)]
        vsnD = [P.sb(f"vsnD{i}", [16, 4, 130], BF16) for i in range(2)]
        vsnF = [P.sb(f"vsnF{i}", [16, 8, 66], BF16) for i in range(2)]
        for i2 in range(2):
            P.op("pool", lambda e, i2=i2: e.memset(kS[i2][64:67, :], 1.0), writes=[kS[i2]])
            P.op("pool", lambda e, i2=i2: e.memset((vSbD, vSbF)[i2][:], 1.0), writes=[(vSbD, vSbF)[i2]])
            P.op("pool", lambda e, i2=i2: e.memset(PTs[i2][:], 0.0), writes=[PTs[i2]])
            P.dma("sp", vsnD[i2][:], vsn_d[i2 * 16:(i2 + 1) * 16, :, :], src=vsn_d, dst=vsnD[i2])
            P.dma("sp", vsnF[i2][:], vsn_f[i2 * 16:(i2 + 1) * 16, :, :], src=vsn_f, dst=vsnF[i2])
        P.fence()
        if 3 in phases:
            wc32 = [P.sb(f"wc32_{i}", [128, 512], F32) for i in range(1)] * 2
            wc16 = [P.sb(f"wc16_{i}", [128, 512], BF16) for i in range(1)] * 2
            wjobs = []
            for (src_, dstb, rows, cols) in ((w_a, D_WA, 512, 1024), (w_b, D_WB, 512, 1024), (w_o, D_WO, 1024, 1024),
                                             (w_f1, D_WF1, 1024, 5632), (w_f2, D_WF2, 2816, 1024)):
                for r_ in range(rows // 128):
                    for c0_ in range(0, cols, 512):
                        c1_ = min(cols, c0_ + 512)
                        wjobs.append((src_, dstb, r_, c0_, c1_))
            for i_, (src_, dstb, r_, c0_, c1_) in enumerate(wjobs):
                a32, a16 = wc32[i_ % 2], wc16[i_ % 2]
                n_ = c1_ - c0_
                P.dma("pool", a32[:, 0:n_], src_[r_ * 128:(r_ + 1) * 128, c0_:c1_], dst=a32)
                P.op("pool", lambda e, a32=a32, a16=a16, n_=n_: e.tensor_copy(out=a16[:, 0:n_], in_=a32[:, 0:n_]), reads=[a32], writes=[a16])
                P.dma("pool", dstb[r_ * 128:(r_ + 1) * 128, c0_:c1_], a16[:, 0:n_], src=a16, dst=dstb)
        passes = [list(range(p0, min(p0 + 8, NR))) for p0 in range(0, NR, 8)]
        ci = [0]
        for u in range(16):
            isd = u < 8
            K_ = 64 if isd else 70
            W = 130 if isd else 66
            hv = (u // 2) if isd else (u - 8)
            qt = (qtileD if isd else qtileF)[u % 2]
            P.dma("sp", qt[0:67 if not isd else 64, :], D_QT[u, 0:67 if not isd else 64, :], src=D_QT, dst=qt)
            if isd:
                ckT, crow, cv, dh = ckdT, u * 64, cvd, 128
                cc_, s__ = u // 2, u % 2
            else:
                ckT, crow, cv, dh = ckfT, (u - 8) * 64, cvf, 64
                cc_, s__ = 4 + (u - 8) // 2, (u - 8) % 2

            def samp_prep_k(u, sq, isd=isd, hv=hv, ckT=ckT, crow=crow, cc_=cc_, s__=s__):
                kst, ks = kS32[sq], kS[sq]
                P.dma("sp", kst[:], ckT[sq, crow:crow + 64, :], dst=kst)
                P.op("dve", lambda e, kst=kst, ks=ks: e.tensor_copy(out=ks[0:64, 0:PAST], in_=kst[:]), reads=[kst], writes=[ks], nowaw=True)
                P.dma("sp", ks[0:64, PAST:PAST + 16], KTn[64 * s__:64 * s__ + 64, cc_, sq * 16:(sq + 1) * 16], src=KTn, dst=ks)
                if not isd:
                    for j3 in range(3):
                        P.dma("sp", ks[67 + j3:68 + j3, :], gcb[j3][hv:hv + 1, sq, :], src=gcb2[j3], dst=ks)

            def samp_prep_v(u, sq, isd=isd, hv=hv, cv=cv, dh=dh, W=W):
                vst, vs = vS32[sq], (vSbD if isd else vSbF)
                P.dma("sp", vst[:, :, 0:dh], cv[sq, :, hv * dh:(hv + 1) * dh].rearrange("(b p) d -> p b d", p=128), dst=vst)
                P.op("dve", lambda e, vst=vst, vs=vs, dh=dh, W=W: e.tensor_copy(
                    out=vs[:, 0:NKB_S * W].rearrange("p (b w) -> p b w", w=W)[:, :, 0:dh], in_=vst[:, :, 0:dh]), reads=[vst], writes=[vs])

            for R in passes:
                r0, nr = R[0], len(R)
                nj = 8 * (r0 + nr)
                nchunks = (nj + NCH - 1) // NCH
                chunk_tiles = {}

                def load_chunk(ch, u=u, isd=isd, hv=hv, W=W):
                    if ch in chunk_tiles or ch >= nchunks:
                        return
                    kt = ktile[ci[0] % 3]
                    vt = vtile[ci[0] % 3]
                    ci[0] += 1
                    chunk_tiles[ch] = (kt, vt)
                    k0 = ch * NCH * 128
                    P.dma("sp", kt[0:64, :], D_KT[u, 0:64, k0:k0 + NCH * 128], src=D_KT, dst=kt)
                    if not isd:
                        P.dma("sp", kt[67:70, :], D_KT[u, 67:70, k0:k0 + NCH * 128], src=D_KT, dst=kt)
                    DV = D_VD if isd else D_VF
                    P.dma("sp", vt[:, 0:NCH * W].rearrange("p (g x) -> p g x", g=2),
                          DV[hv, 2 * ch:2 * ch + 2].rearrange("g p x -> p g x"), src=DV, dst=vt)

                def stage_qk(j):
                    ch, jl = j // NCH, j % NCH
                    load_chunk(ch)
                    kt, vt = chunk_tiles[ch]
                    a0 = max(j // 8, r0) - r0
                    sb2 = SB2[j % 2]
                    for half in range(2):
                        lo_ = max(a0, 4 * half)
                        hi_ = min(nr, 4 * half + 4)
                        if lo_ >= hi_:
                            continue
                        P.op("pe", lambda e, sb2=sb2, kt=kt, jl=jl, lo_=lo_, hi_=hi_, qt=qt, r0=r0: e.matmul(
                            sb2[:, lo_ * 128:hi_ * 128], lhsT=kt[:, jl * 128:(jl + 1) * 128],
                            rhs=qt[:, (r0 + lo_) * 128:(r0 + hi_) * 128], start=True, stop=True, skip_group_check=True),
                            reads=[kt, qt], writes=[sb2])
                    cands = []
                    rr = j // 8
                    if rr in R:
                        cands.append((rr, j - (8 * rr - 1)))
                    if (j + 1) % 8 == 0 and (j + 1) // 8 in R:
                        cands.append(((j + 1) // 8, 0))
                    for (rr, sl) in cands:
                        a = rr - r0
                        if isd:
                            for tl in (slD_hi, slD_lo):
                                P.op("pe", lambda e, sb2=sb2, a=a, tl=tl, sl=sl, hv=hv: e.matmul(
                                    sb2[:, a * 128:(a + 1) * 128], lhsT=ident[:], rhs=tl[:, sl, hv, :], start=False, stop=True,
                                    skip_group_check=True), reads=[ident, tl], writes=[sb2])
                        else:
                            P.op("pe", lambda e, sb2=sb2, a=a, sl=sl: e.matmul(
                                sb2[:, a * 128:(a + 1) * 128], lhsT=ident[:], rhs=slF_b[:, sl, :], start=False, stop=True,
                                skip_group_check=True), reads=[ident, slF_b], writes=[sb2])

                def stage_exp(j):
                    a0 = max(j // 8, r0) - r0
                    sb2, pt = SB2[j % 2], PT[j % 2]
                    P.op("act", lambda e, sb2=sb2, pt=pt, a0=a0, nr=nr: e.activation(
                        out=pt[:, a0 * 128:nr * 128], in_=sb2[:, a0 * 128:nr * 128], func=AF.Exp), reads=[sb2], writes=[pt])

                def stage_pv(j):
                    ch, jl = j // NCH, j % NCH
                    kt, vt = chunk_tiles[ch]
                    a0 = max(j // 8, r0) - r0
                    pt = PT[j % 2]
                    for a in range(a0, nr):
                        c = acc_col(a, W)
                        last = (j == 8 * (r0 + a) + 7)
                        P.op("pe", lambda e, pt=pt, vt=vt, a=a, c=c, jl=jl, j=j, last=last, W=W: e.matmul(
                            ACC[:, c:c + W], lhsT=pt[:, a * 128:(a + 1) * 128], rhs=vt[:, jl * W:(jl + 1) * W],
                            start=(j == 0 and a % 3 == 0), stop=last, skip_group_check=True), reads=[pt, vt], writes=[ACCb[a]])
                        if last:
                            normalize(u, r0 + a, ACC[:, c:c + W], 128, ACCb[a])

                load_chunk(0)
                load_chunk(1)
                stage_qk(0)
                for j in range(nj):
                    if R is passes[-1] and 'samp' not in SKIP:
                        if j == 0:
                            samp_prep_k(u, 0)
                            samp_prep_v(u, 0)
                        if j == nj // 2:
                            samp_prep_k(u, 1)
                    stage_exp(j)
                    if j + 1 < nj:
                        if (j + 1) % NCH == 0:
                            load_chunk((j + 1) // NCH + 1)
                        stage_qk(j + 1)
                    stage_pv(j)
            for sq in range(0 if 'samp' in SKIP else 2):
                i2 = sq
                kst, ks, vst, vs, pts = kS32[i2], kS[i2], vS32[i2], (vSbD if isd else vSbF), PTs[sq]
                if sq == 1:
                    samp_prep_v(u, 1)
                qcol = NR * 128 + sq * 16
                for blk in range(NKB_S):
                    P.op("pe", lambda e, ks=ks, qt=qt, blk=blk, K_=K_, qcol=qcol: e.matmul(
                        B7[:, blk * 16:(blk + 1) * 16], lhsT=ks[0:K_, blk * 128:(blk + 1) * 128], rhs=qt[0:K_, qcol:qcol + 16],
                        start=(blk == 0), stop=True, skip_group_check=True), reads=[ks, qt], writes=[B7])
                P.op("pe", lambda e, ks=ks, qt=qt, K_=K_, qcol=qcol: e.matmul(
                    B7[0:16, 256:272], lhsT=ks[0:K_, PAST:PAST + 16], rhs=qt[0:K_, qcol:qcol + 16],
                    start=False, stop=True, skip_group_check=True), reads=[ks, qt], writes=[B7])
                if isd:
                    for (tp, td) in ((TP_hi, TD_hi), (TP_lo, TD_lo)):
                        P.op("pe", lambda e, tp=tp, hv=hv: e.matmul(
                            B7[:, (NKB_S - 1) * 16:NKB_S * 16], lhsT=ident[:], rhs=tp[:, hv, 0:16], start=False, stop=True,
                            skip_group_check=True), reads=[ident, tp], writes=[B7])
                        P.op("pe", lambda e, td=td, hv=hv: e.matmul(
                            B7[0:16, 256:272], lhsT=ident[0:16, 0:16], rhs=td[0:16, hv, 0:16], start=False, stop=True,
                            skip_group_check=True), reads=[ident, td], writes=[B7])
                else:
                    P.op("pe", lambda e: e.matmul(B7[0:16, 256:272], lhsT=ident[0:16, 0:16], rhs=TC_b[0:16, 0:16], start=False, stop=True,
                                                  skip_group_check=True), reads=[ident, TC_b], writes=[B7])
                P.op("act", lambda e, pts=pts, sq=sq: e.activation(
                    out=pts[:, 0:NKB_S, sq * 16:(sq + 1) * 16], in_=B7[:, 0:NKB_S * 16].rearrange("p (b q) -> p b q", q=16), func=AF.Exp),
                    reads=[B7], writes=[pts], nowaw=True)
                P.op("act", lambda e, pts=pts, sq=sq: e.activation(
                    out=pts[0:16, NKB_S, sq * 16:(sq + 1) * 16], in_=B7[0:16, 256:272], func=AF.Exp), reads=[B7], writes=[pts], nowaw=True)
                vn = (vsnD if isd else vsnF)[sq]
                for blk in range(NKB_S):
                    P.op("pe", lambda e, pts=pts, vs=vs, blk=blk, W=W, sq=sq: e.matmul(
                        ACC[0:32, 0:W], lhsT=pts[:, blk, :], rhs=vs[:, blk * W:(blk + 1) * W], start=(sq == 0 and blk == 0), stop=False,
                        skip_group_check=True), reads=[pts, vs], writes=[ACCb[0]])
                P.op("pe", lambda e, pts=pts, vn=vn, W=W, sq=sq, hv=hv: e.matmul(
                    ACC[0:32, 0:W], lhsT=pts[0:16, NKB_S, :], rhs=vn[0:16, hv, 0:W], start=False, stop=(sq == 1),
                    skip_group_check=True), reads=[pts, vn], writes=[ACCb[0]])
            if 'samp' not in SKIP:
                normalize(u, NR, ACC[0:32, 0:W], 32, ACCb[0])
            unit_finish(u)
        if dbg:
            for (src_, dst_) in ((oa_bf, dbg_oa), (ob_bf, dbg_ob)):
                out_ops.append(P.dma("sp", dst_.rearrange("(b p) c -> p b c", p=128), src_[:], src=src_))
        P.dma("sp", D_OA.t.rearrange("(b p) c -> p b c", p=128), oa_bf[:], src=oa_bf, dst=D_OA)
        P.dma("sp", D_OB.t.rearrange("(b p) c -> p b c", p=128), ob_bf[:], src=ob_bf, dst=D_OB)
        P.pop()
    P.pop()
    P.pop()

    h2T = P.sb("h2T", [128, 8, TOWN], BF16)
    if 3 in phases:
        P.push()
        oabs = [P.sb(f"oab{i}", [128, 512], BF16) for i in range(2)]
        obbs = [P.sb(f"obb{i}", [128, 512], BF16) for i in range(2)]
        PS7b = Buf(P, "PS7b", psall[:, 3584:4096].bitcast(BF16))
        lnbc = P.sb("lnbc", [128, 4, 1024], F32)
        P.dma("sp", lnbc[:].rearrange("p a d -> p (a d)"), ln_in[0:1, :].partition_broadcast(128), dst=lnbc)
        g2_bc = P.sb("g2_bc", [128, 128], F32)
        P.dma("sp", g2_bc[:], subln_g[0:1, :].partition_broadcast(128), dst=g2_bc)
        P.op("dve", lambda e: e.tensor_scalar(out=g2_bc[:], in0=g2_bc[:], scalar1=(1.0 - LAM_INIT), scalar2=None, op0=ALU.mult),
             reads=[g2_bc], writes=[g2_bc])
        RSTD = P.sb("RSTD", [128, NOWN, 4], F32)
        epsb = P.sb("epsb", [128, 1], F32)
        P.op("pool", lambda e: e.memset(epsb[:], LN_EPS), writes=[epsb])
        P.op("act", lambda e: e.activation(out=RSTD[:], in_=SSQ[:], func=AF.Ln, scale=1.0 / 128.0, bias=epsb[:]), reads=[SSQ, epsb], writes=[RSTD])
        P.op("act", lambda e: e.activation(out=RSTD[:], in_=RSTD[:], func=AF.Exp, scale=-0.5), reads=[RSTD], writes=[RSTD])
        modB = [P.sb(f"modB{i}", [128, 4, 1024], F32) for i in range(2)]
        P.push()
        cTr = P.sb("cTr", [128, 8, 256], F32)
        P.dma("sp", cTr[:].rearrange("p k j -> p (k j)"), cT_rep[:, :], dst=cTr)
        sTr = P.sb("sTr", [128, 8, 256], F32)
        P.op("act", lambda e: e.activation(out=sTr[:], in_=cTr[:], func=AF.Silu), reads=[cTr], writes=[sTr])
        wad3 = [P.sb(f"wad3_{i}", [128, 8, 512], F32) for i in range(2)]
        bbc = [P.sb(f"bbc{i}", [128, 512], F32) for i in range(2)]
        for g in range(4, 12):
            wt = wad3[g % 2]
            bb = bbc[g % 2]
            P.dma("sp", wt[:], w_ada_v[:, :, g * 512:(g + 1) * 512], dst=wt)
            P.dma("sp", bb[:], b_ada[0:1, g * 512:(g + 1) * 512].partition_broadcast(128), dst=bb)
            ch, hf = (g - 4) // 2, (g - 4) % 2
            for st_ in range(2):
                bank = banks[(2 * g + st_) % 4]
                for k in range(8):
                    P.op("pe", lambda e, bank=bank, wt=wt, k=k, st_=st_: e.matmul(
                        bank[:, :], lhsT=sTr[:, k, st_ * 128:(st_ + 1) * 128], rhs=wt[:, k, :], start=(k == 0), stop=(k == 7)),
                        reads=[sTr, wt], writes=[bank])
                if ch == 2:
                    P.op("dve", lambda e, bank=bank, bb=bb, st_=st_, ch=ch, hf=hf: e.scalar_tensor_tensor(
                        out=modB[st_][:, ch, hf * 512:(hf + 1) * 512], in0=bank[:, :], scalar=1.0, in1=bb[:], op0=ALU.add, op1=ALU.add),
                        reads=[bank, bb], writes=[modB[st_]], nowaw=True)
                else:
                    P.op("dve", lambda e, bank=bank, bb=bb, st_=st_, ch=ch, hf=hf: e.tensor_tensor(
                        out=modB[st_][:, ch, hf * 512:(hf + 1) * 512], in0=bank[:, :], in1=bb[:], op=ALU.add),
                        reads=[bank, bb], writes=[modB[st_]], nowaw=True)
        P.pop()
        for i_ in range(2):
            P.dma("sp", D_G2[i_], modB[i_][:, 3, :], src=modB[i_], dst=D_G2)
        wa_sb = P.sb("wa_sb", [128, 4, 1024], BF16)
        wb_sb = P.sb("wb_sb", [128, 4, 1024], BF16)
        wo_sb = P.sb("wo_sb", [128, 8, 1024], BF16)
        P.dma("sp", wa_sb[:], D_WA.t.rearrange("(k p) n -> p k n", p=128), src=D_WA, dst=wa_sb)
        P.dma("sp", wb_sb[:], D_WB.t.rearrange("(k p) n -> p k n", p=128), src=D_WB, dst=wb_sb)
        P.dma("sp", wo_sb[:], D_WO.t.rearrange("(k p) n -> p k n", p=128), src=D_WO, dst=wo_sb)

        gat = [P.sb(f"gat{i}", [128, 2048], F32) for i in range(2)]
        xres = [P.sb(f"xres{i}", [128, 1024], F32) for i in range(2)]
        oan = P.sb("oan", [128, 512], BF16)
        oT = P.sb("oT", [128, 8, 128], BF16)
        t1 = P.sb("t1", [128, 1024], F32)
        t2 = P.sb("t2", [128, 1024], F32)
        ybf = P.sb("ybf", [128, 1024], BF16)
        yT = P.sb("yT", [128, 8, 128], BF16)
        x1s = [P.sb(f"x1s{i}", [128, 1024], F32) for i in range(2)]
        st1 = P.sb("st1", [128, 4], F32)

        def layer_norm(src, dst, gi, bi, npart=128):
            P.op("dve", lambda e: e.tensor_reduce(out=st1[:, 0:1], in_=src[:], axis=AX.X, op=ALU.add), reads=[src], writes=[st1])
            P.op("dve", lambda e: e.tensor_scalar(out=st1[:, 0:1], in0=st1[:, 0:1], scalar1=-1.0 / 1024.0, scalar2=None, op0=ALU.mult),
                 reads=[st1], writes=[st1])
            P.op("dve", lambda e: e.tensor_scalar(out=src[:], in0=src[:], scalar1=st1[:, 0:1], scalar2=None, op0=ALU.add),
                 reads=[src, st1], writes=[src])
            P.op("dve", lambda e: e.tensor_tensor(out=dst[:], in0=src[:], in1=src[:], op=ALU.mult), reads=[src], writes=[dst])
            P.op("dve", lambda e: e.tensor_reduce(out=st1[:, 1:2], in_=dst[:], axis=AX.X, op=ALU.add), reads=[dst, st1], writes=[st1])
            P.op("act", lambda e: e.activation(out=st1[:, 2:3], in_=st1[:, 1:2], func=AF.Ln, scale=1.0 / 1024.0, bias=epsb[:]),
                 reads=[st1, epsb], writes=[st1])
            P.op("act", lambda e: e.activation(out=st1[:, 3:4], in_=st1[:, 2:3], func=AF.Exp, scale=-0.5), reads=[st1], writes=[st1])
            P.op("dve", lambda e: e.scalar_tensor_tensor(out=dst[:], in0=src[:], scalar=st1[:, 3:4], in1=lnbc[:, gi, :], op0=ALU.mult, op1=ALU.mult),
                 reads=[src, st1, lnbc], writes=[dst])
            P.op("dve", lambda e: e.tensor_tensor(out=dst[:], in0=dst[:], in1=lnbc[:, bi, :], op=ALU.add), reads=[dst, lnbc], writes=[dst])

        def transposes(src_aps, dst, reads):
            n = len(src_aps)
            for i_, ap_ in enumerate(src_aps):
                P.op("pe", lambda e, i_=i_, ap_=ap_: e.transpose(out=PS7b[:, i_ * 128:(i_ + 1) * 128], in_=ap_, identity=ident[:]),
                     reads=reads + [ident], writes=[PS7b])
            evac(dst[:, 0:n, :].rearrange("p c t -> p (c t)"), PS7b[:, 0:n * 128], [PS7b], [dst])

        oan2 = [oan, P.sb("oan_b", [128, 512], BF16)]
        oT2 = [oT, P.sb("oT_b", [128, 8, 128], BF16)]
        ybf2 = [ybf, P.sb("ybf_b", [128, 1024], BF16)]
        yT2 = [yT, P.sb("yT_b", [128, 8, 128], BF16)]
        tb1 = P.sb("tb1", [128, 1024], F32)
        tb2 = P.sb("tb2", [128, 1024], F32)
        hb = P.sb("hb", [128, 1024], BF16)
        pa_b = [banks[0], banks[1]]
        pb_b = [banks[2], banks[3]]
        ym = [banks[4], banks[5]]

        def front(b):
            ga_ = gat[b % 2]
            xr = xres[b % 2]
            oab = oabs[b % 2]
            obb = obbs[b % 2]
            oan_, oT_, ybf_, yT_ = oan2[b % 2], oT2[b % 2], ybf2[b % 2], yT2[b % 2]
            P.dma("sp", ga_[:], D_GATES[b * 128:(b + 1) * 128, :], src=D_GATES, dst=ga_)
            P.dma("sp", xr[:], x_own[b * 128:(b + 1) * 128, :], dst=xr)
            P.dma("sp", oab[:], D_OA[b * 128:(b + 1) * 128, :], src=D_OA, dst=oab)
            P.dma("sp", obb[:], D_OB[b * 128:(b + 1) * 128, :], src=D_OB, dst=obb)
            P.op("act", lambda e: e.activation(out=ga_[:], in_=ga_[:], func=AF.Sigmoid), reads=[ga_], writes=[ga_])
            for h in range(4):
                P.op("dve", lambda e, h=h: e.scalar_tensor_tensor(
                    out=oan_[:, h * 128:(h + 1) * 128], in0=oab[:, h * 128:(h + 1) * 128], scalar=RSTD[:, b, h:h + 1], in1=g2_bc[:],
                    op0=ALU.mult, op1=ALU.mult), reads=[oab, RSTD, g2_bc], writes=[oan_], nowaw=(h > 0))
            transposes([oan_[:, h * 128:(h + 1) * 128] for h in range(4)] + [obb[:, h * 128:(h + 1) * 128] for h in range(4)], oT_, [oan_, obb])
            for hf in range(2):
                for k in range(4):
                    P.op("pe", lambda e, hf=hf, k=k: e.matmul(pa_b[hf][:, :], lhsT=oT_[:, k, :], rhs=wa_sb[:, k, hf * 512:(hf + 1) * 512],
                                                              start=(k == 0), stop=(k == 3)), reads=[oT_, wa_sb], writes=[pa_b[hf]])
                for k in range(4):
                    P.op("pe", lambda e, hf=hf, k=k: e.matmul(pb_b[hf][:, :], lhsT=oT_[:, 4 + k, :], rhs=wb_sb[:, k, hf * 512:(hf + 1) * 512],
                                                              start=(k == 0), stop=(k == 3)), reads=[oT_, wb_sb], writes=[pb_b[hf]])
            for hf in range(2):
                sl = slice(hf * 512, (hf + 1) * 512)
                P.op("dve", lambda e, hf=hf, sl=sl: e.tensor_tensor(out=t1[:, sl], in0=pa_b[hf][:, :], in1=ga_[:, sl], op=ALU.mult),
                     reads=[pa_b[hf], ga_], writes=[t1], nowaw=True)
                P.op("dve", lambda e, hf=hf, sl=sl: e.tensor_tensor(out=t2[:, sl], in0=pb_b[hf][:, :], in1=ga_[:, 1024 + hf * 512:1024 + (hf + 1) * 512],
                                                                   op=ALU.mult), reads=[pb_b[hf], ga_], writes=[t2], nowaw=True)
            P.op("dve", lambda e: e.tensor_tensor(out=ybf_[:], in0=t1[:], in1=t2[:], op=ALU.add), reads=[t1, t2], writes=[ybf_])
            transposes([ybf_[:, k * 128:(k + 1) * 128] for k in range(8)], yT_, [ybf_])

        def back(b):
            mB = modB[0] if b < NR else modB[1]
            xr = xres[b % 2]
            x1 = x1s[b % 2]
            yT_ = yT2[b % 2]
            for hf in range(2):
                for k in range(8):
                    P.op("pe", lambda e, hf=hf, k=k: e.matmul(ym[hf][:, :], lhsT=yT_[:, k, :], rhs=wo_sb[:, k, hf * 512:(hf + 1) * 512],
                                                              start=(k == 0), stop=(k == 7)), reads=[yT_, wo_sb], writes=[ym[hf]])
            for hf in range(2):
                sl = slice(hf * 512, (hf + 1) * 512)
                P.op("dve", lambda e, hf=hf, sl=sl: e.tensor_tensor(out=tb1[:, sl], in0=ym[hf][:, :], in1=mB[:, 0, sl], op=ALU.mult),
                     reads=[ym[hf], mB], writes=[tb1], nowaw=(hf > 0))
            P.op("dve", lambda e: e.scalar_tensor_tensor(out=tb1[:], in0=xr[:], scalar=ALPHA, in1=tb1[:], op0=ALU.mult, op1=ALU.add),
                 reads=[xr, tb1], writes=[tb1])
            layer_norm(tb1, x1, 0, 1)
            P.dma("sp", D_X1[b * 128:(b + 1) * 128, :], x1[:], src=x1, dst=D_X1)
            P.op("dve", lambda e: e.tensor_tensor(out=tb2[:], in0=x1[:], in1=mB[:, 2, :], op=ALU.mult), reads=[x1, mB], writes=[tb2])
            P.op("dve", lambda e: e.tensor_tensor(out=hb[:], in0=tb2[:], in1=mB[:, 1, :], op=ALU.add), reads=[tb2, mB], writes=[hb])
            for i_ in range(8):
                P.op("pe", lambda e, i_=i_: e.transpose(out=PS7b[:, i_ * 128:(i_ + 1) * 128], in_=hb[:, i_ * 128:(i_ + 1) * 128], identity=ident[:]),
                     reads=[hb, ident], writes=[PS7b])
            evac(h2T[:, :, b * 128:(b + 1) * 128], PS7b[:, :].rearrange("p (c t) -> p c t", t=128), [PS7b], [h2T])

        front(0)
        for b in range(NOWN):
            if b + 1 < NOWN:
                front(b + 1)
            back(b)
        P.pop()

    if 3 in phases:
        P.push()
        lnbc2 = P.sb("lnbc2", [128, 2, 1024], F32)
        P.dma("sp", lnbc2[:].rearrange("p a d -> p (a d)"), ln_in[0:1, 2048:4096].partition_broadcast(128), dst=lnbc2)
        epsb2 = P.sb("epsb2", [128, 1], F32)
        P.op("pool", lambda e: e.memset(epsb2[:], LN_EPS), writes=[epsb2])
        wf1 = P.sb("wf1", [128, 8, 5632], BF16)
        wf2 = P.sb("wf2", [128, 22, 1024], BF16)
        w1v = D_WF1.t.rearrange("(k p) n -> p k n", p=128)
        for k in range(8):
            P.dma("sp" if k % 2 == 0 else "act", wf1[:, k, :], w1v[:, k, :], src=D_WF1, dst=wf1)
        w2v = D_WF2.t.rearrange("(k p) n -> p k n", p=128)
        for k0_ in range(0, 22, 6):
            k1_ = min(22, k0_ + 6)
            P.dma("pool" if (k0_ // 6) % 2 == 0 else "sp", wf2[:, k0_:k1_, :], w2v[:, k0_:k1_, :], src=D_WF2, dst=wf2)
        g2t = [P.sb(f"g2t{i}", [128, 1024], F32) for i in range(2)]
        for i_ in range(2):
            P.dma("sp", g2t[i_][:], D_G2[i_], src=D_G2, dst=g2t[i_])
        eT = [P.sb(f"eT{i}", [128, 256], F32) for i in range(2)]
        aT = [P.sb(f"aT{i}", [128, 256], BF16) for i in range(2)]
        x1r = [P.sb(f"x1r{i}", [128, 1024], F32) for i in range(1)] * 2
        r2 = P.sb("r2", [128, 1024], F32)
        yo = [P.sb(f"yo{i}", [128, 1024], F32) for i in range(2)]
        st2 = P.sb("st2", [128, 4], F32)
        gb_i = [0]
        pairs = [list(range(p0, min(p0 + 2, NOWN))) for p0 in range(0, NOWN, 2)]
        for pr in pairs:
            nt = len(pr) * 128
            t0 = pr[0] * 128
            cbanks = {}

            def st_g(c, nt=nt, t0=t0):
                bank = banks[gb_i[0] % 4]
                gb_i[0] += 1
                cbanks[c] = bank
                for (off, col0) in ((0, c * 128), (256, 2816 + c * 128)):
                    for k in range(8):
                        P.op("pe", lambda e, bank=bank, off=off, col0=col0, k=k, nt=nt, t0=t0: e.matmul(
                            bank[:, off:off + nt], lhsT=wf1[:, k, col0:col0 + 128], rhs=h2T[:, k, t0:t0 + nt], start=(k == 0), stop=(k == 7),
                            skip_group_check=True), reads=[wf1, h2T], writes=[bank])

            def st_act(c, nt=nt):
                bank = cbanks[c]
                et = eT[c % 2]
                at = aT[c % 2]
                P.op("act", lambda e, bank=bank, et=et, nt=nt: e.activation(out=et[:, 0:nt], in_=bank[:, 0:nt], func=AF.Silu),
                     reads=[bank], writes=[et])
                P.op("dve", lambda e, bank=bank, et=et, at=at, nt=nt: e.tensor_tensor(out=at[:, 0:nt], in0=bank[:, 256:256 + nt], in1=et[:, 0:nt], op=ALU.mult),
                     reads=[bank, et], writes=[at])

            def st_out(c, pr=pr):
                at = aT[c % 2]
                for bi_, b in enumerate(pr):
                    for hf in range(2):
                        ab = banks[4 + 2 * bi_ + hf]
                        P.op("pe", lambda e, ab=ab, at=at, bi_=bi_, c=c, hf=hf: e.matmul(
                            ab[:, :], lhsT=at[:, bi_ * 128:(bi_ + 1) * 128], rhs=wf2[:, c, hf * 512:(hf + 1) * 512], start=(c == 0), stop=(c == 21)),
                            reads=[at, wf2], writes=[ab])

            st_g(0)
            st_g(1)
            for c in range(22):
                st_act(c)
                if c + 2 < 22:
                    st_g(c + 2)
                st_out(c)
            for bi_, b in enumerate(pr):
                xr = x1r[b % 2]
                y_ = yo[b % 2]
                gt = g2t[0] if b < NR else g2t[1]
                P.dma("sp", xr[:], D_X1[b * 128:(b + 1) * 128, :], src=D_X1, dst=xr)
                for hf in range(2):
                    sl = slice(hf * 512, (hf + 1) * 512)
                    ab = banks[4 + 2 * bi_ + hf]
                    P.op("dve", lambda e, ab=ab, sl=sl, gt=gt: e.tensor_tensor(out=r2[:, sl], in0=ab[:, :], in1=gt[:, sl], op=ALU.mult),
                         reads=[ab, gt], writes=[r2], nowaw=(hf > 0))
                P.op("dve", lambda e, xr=xr: e.scalar_tensor_tensor(out=r2[:], in0=xr[:], scalar=ALPHA, in1=r2[:], op0=ALU.mult, op1=ALU.add),
                     reads=[xr, r2], writes=[r2])
                P.op("dve", lambda e: e.tensor_reduce(out=st2[:, 0:1], in_=r2[:], axis=AX.X, op=ALU.add), reads=[r2], writes=[st2])
                P.op("dve", lambda e: e.tensor_scalar(out=st2[:, 0:1], in0=st2[:, 0:1], scalar1=-1.0 / 1024.0, scalar2=None, op0=ALU.mult),
                     reads=[st2], writes=[st2])
                P.op("dve", lambda e: e.tensor_scalar(out=r2[:], in0=r2[:], scalar1=st2[:, 0:1], scalar2=None, op0=ALU.add), reads=[r2, st2], writes=[r2])
                P.op("dve", lambda e, y_=y_: e.tensor_tensor(out=y_[:], in0=r2[:], in1=r2[:], op=ALU.mult), reads=[r2], writes=[y_])
                P.op("dve", lambda e, y_=y_: e.tensor_reduce(out=st2[:, 1:2], in_=y_[:], axis=AX.X, op=ALU.add), reads=[y_, st2], writes=[st2])
                P.op("act", lambda e: e.activation(out=st2[:, 2:3], in_=st2[:, 1:2], func=AF.Ln, scale=1.0 / 1024.0, bias=epsb2[:]),
                     reads=[st2, epsb2], writes=[st2])
                P.op("act", lambda e: e.activation(out=st2[:, 3:4], in_=st2[:, 2:3], func=AF.Exp, scale=-0.5), reads=[st2], writes=[st2])
                P.op("dve", lambda e, y_=y_: e.scalar_tensor_tensor(out=y_[:], in0=r2[:], scalar=st2[:, 3:4], in1=lnbc2[:, 0, :], op0=ALU.mult, op1=ALU.mult),
                     reads=[r2, st2, lnbc2], writes=[y_])
                P.op("dve", lambda e, y_=y_: e.tensor_tensor(out=y_[:], in0=y_[:], in1=lnbc2[:, 1, :], op=ALU.add), reads=[y_, lnbc2], writes=[y_])
                out_ops.append(P.dma("sp", y_own[b * 128:(b + 1) * 128, :], y_[:], src=y_))
        P.pop()
    return nc, P, es, out_ops, locals()


def finish(nc, P, es, out_ops):
    while P.scopes:
        P.pop()
    P.emit(out_ops)
    es.close()
    return nc


def host_inputs(inp, S=16384, PAST=2048):
    NB = S // 128
    NR = NB // 8
    f = lambda a: np.ascontiguousarray(np.asarray(a, dtype=np.float32))
    xp = f(inp["x_prompt"])[0, :S]
    xs = f(inp["x_sample"])
    xT_all = np.ascontiguousarray(xp.T)
    cp = f(inp["c_prompt"])[0]
    cs = f(inp["c_sample"])
    shared = {
        "xT_all": xT_all,
        "b_ada_fm": np.ascontiguousarray(f(inp["b_ada"])[0].reshape(48, 128).T),
        "b_ada": f(inp["b_ada"]),
        "w_ada": f(inp["w_ada"])[0], "w_in": f(inp["w_in"])[0],
        "b_forget": f(inp["b_forget"]),
        "lam": np.concatenate([f(inp[k])[0] for k in ("lambda_q1", "lambda_k1", "lambda_q2", "lambda_k2")])[None, :],
        "subln_g": f(inp["subln_g"]),
        "rel_bias": f(inp["rel_bias"]).reshape(1, 128),
        "w_a": f(inp["w_branch_a"])[0], "w_b": f(inp["w_branch_b"])[0], "w_o": f(inp["w_o"])[0],
        "ln": np.concatenate([f(inp[k])[0] for k in ("ln1_g", "ln1_b", "ln2_g", "ln2_b")])[None, :],
        "w_f1": f(inp["w_ffn_in"])[0], "w_f2": f(inp["w_ffn_out"])[0],
    }
    maps = []
    for c in range(8):
        blocks = [c + 8 * r for r in range(NR)]
        xo = np.zeros((NR * 128 + 128, D), np.float32)
        for r, bl in enumerate(blocks):
            xo[r * 128:(r + 1) * 128] = xp[bl * 128:(bl + 1) * 128]
        xo[NR * 128:NR * 128 + 16] = xs[2 * c]
        xo[NR * 128 + 16:NR * 128 + 32] = xs[2 * c + 1]
        cT = np.stack([cp, cs[2 * c], cs[2 * c + 1]], axis=1)
        cT = cT.reshape(8, 128, 3).transpose(1, 0, 2).reshape(128, 24)
        rep = np.zeros((D, 256), np.float32)
        rep[:, 0:128] = cp[:, None]
        rep[:, 128:144] = cs[2 * c][:, None]
        rep[:, 144:256] = cs[2 * c + 1][:, None]
        rep = rep.reshape(8, 128, 256).transpose(1, 0, 2).reshape(128, 8 * 256)
        sel = np.zeros((9, 3), np.float32)
        for slot in range(9):
            t = slot - 1
            if t == c - 1:
                sel[slot, 0] = 1
            elif t == c:
                sel[slot, 1] = 1
            elif t > c:
                sel[slot, 2] = 1
        selB = np.zeros((NR, NB), np.float32)
        for r, bl in enumerate(blocks):
            selB[r, bl] = 1
        m = dict(shared)
        m.update({
            "xT_own": np.ascontiguousarray(xo.T), "x_own": xo,
            "cT": np.ascontiguousarray(cT), "cT_rep": np.ascontiguousarray(rep),
            "ckdT": np.ascontiguousarray(f(inp["cache_diff_k"])[0, 2 * c:2 * c + 2, :PAST].reshape(2, PAST, 512).transpose(0, 2, 1)),
            "ckfT": np.ascontiguousarray(f(inp["cache_fox_k"])[0, 2 * c:2 * c + 2, :PAST].reshape(2, PAST, 512).transpose(0, 2, 1)),
            "cvd": np.ascontiguousarray(f(inp["cache_diff_v"])[0, 2 * c:2 * c + 2, :PAST].reshape(2, PAST, 512)),
            "cvf": np.ascontiguousarray(f(inp["cache_fox_v"])[0, 2 * c:2 * c + 2, :PAST].reshape(2, PAST, 512)),
            "clfT": np.ascontiguousarray(f(inp["cache_fox_logf"])[0, 2 * c:2 * c + 2, :PAST].transpose(0, 2, 1)),
            "sel": sel.reshape(1, 27), "selB": selB.reshape(1, NR * NB),
        })
        maps.append(m)
    return maps


def assemble(results, S=16384):
    NB = S // 128
    NR = NB // 8
    y_p = np.zeros((1, S, D), np.float32)
    y_s = np.zeros((16, 16, D), np.float32)
    outs_p = {k: np.zeros((S, w), np.float32) for k, w in (("kd", 512), ("vd", 512), ("kf", 512), ("vf", 512), ("lf", 8))}
    outs_s = {k: np.zeros((16, 16, w), np.float32) for k, w in (("kd", 512), ("vd", 512), ("kf", 512), ("vf", 512), ("lf", 8))}
    for c, r in enumerate(results):
        for rr in range(NR):
            bl = c + 8 * rr
            y_p[0, bl * 128:(bl + 1) * 128] = r["y_own"][rr * 128:(rr + 1) * 128]
            for k in outs_p:
                outs_p[k][bl * 128:(bl + 1) * 128] = r[k + "_own"][rr * 128:(rr + 1) * 128]
        o = NR * 128
        y_s[2 * c] = r["y_own"][o:o + 16]
        y_s[2 * c + 1] = r["y_own"][o + 16:o + 32]
        for k in outs_s:
            outs_s[k][2 * c] = r[k + "_own"][o:o + 16]
            outs_s[k][2 * c + 1] = r[k + "_own"][o + 16:o + 32]
    return (y_p, y_s,
            outs_p["kd"].reshape(1, 1, S, 4, 128), outs_p["vd"].reshape(1, 1, S, 4, 128),
            outs_p["kf"].reshape(1, 1, S, 8, 64), outs_p["vf"].reshape(1, 1, S, 8, 64), outs_p["lf"].reshape(1, 1, S, 8),
            outs_s["kd"].reshape(1, 16, 16, 4, 128), outs_s["vd"].reshape(1, 16, 16, 4, 128),
            outs_s["kf"].reshape(1, 16, 16, 8, 64), outs_s["vf"].reshape(1, 16, 16, 8, 64), outs_s["lf"].reshape(1, 16, 16, 8))


def kernel(**inputs):
    from concourse.bass_utils import run_bass_kernel_spmd
    nc, P, es, out_ops, _ = build()
    finish(nc, P, es, out_ops)
    maps = host_inputs(inputs)
    res = run_bass_kernel_spmd(nc, maps, core_ids=list(range(8)))
    return assemble(res.results)
```
